# Optimizing a Trainium2 kernel written in Bass

```python
import math
import jax, jax.numpy as jnp
from jax import lax
import numpy as np

D_MODEL = 1024
BATCH = 4
SEQ = 8192
DEPTH = 2
DEC_BATCH = 32
DEC_SEQ = 1
PAST_LEN = 16384
PAGE_SIZE = 128

S_GROUP = 16
S_GROUPS = 16
S_STATE = 64
S_WIDTH = S_GROUPS * S_GROUP
DT_MIN = 1e-3
DT_MAX = 1e-1
R_HEAD = 64
R_HEADS = 4
R_WIDTH = R_HEADS * R_HEAD
W_LORA = 32
A_LORA = 32
G_LORA = 64
R_IN = 3 * R_WIDTH + W_LORA + A_LORA + G_LORA
GN_EPS = 64e-5
A_HEAD = 64
A_HEADS = 8
A_KV_HEADS = 4
A_QW = A_HEADS * A_HEAD
A_KVW = A_KV_HEADS * A_HEAD
ROT_DIM = A_HEAD // 4
ROPE_THETA = 500000.0
MOBA_BLOCK = 256
MOBA_TOPK = 3
Q_CHUNK = 32
N_BRANCH = 3
O_SSM = N_BRANCH * D_MODEL
O_RWKV = O_SSM + S_WIDTH
O_Q = O_RWKV + R_IN
O_K = O_Q + A_QW
O_V = O_K + A_KVW
N_IN = O_V + A_KVW
D_FF = 2816
CONV_W = 3
ALPHA = (2 * DEPTH) ** 0.25
BETA = (8 * DEPTH) ** -0.25
LN_EPS = 1e-5

kernel_name = 'hybrid_s5_rwkv7_moba_convffn_step'


def layer_norm(x, g, b):
    xf = x.astype(jnp.float32)
    mu = jnp.mean(xf, -1, keepdims=True)
    var = jnp.mean(jnp.square(xf - mu), -1, keepdims=True)
    return ((xf - mu) * lax.rsqrt(var + LN_EPS) * g + b).astype(x.dtype)


def rotary(x, pos):
    half = ROT_DIM // 2
    inv = ROPE_THETA ** (-jnp.arange(half, dtype=jnp.float32) / half)
    ang = pos.astype(jnp.float32)[:, None] * inv[None, :]
    cos = jnp.cos(ang)[None, :, None, :]
    sin = jnp.sin(ang)[None, :, None, :]
    xr = x[..., :ROT_DIM].astype(jnp.float32)
    x1, x2 = xr[..., :half], xr[..., half:]
    rot = jnp.concatenate([x1 * cos - x2 * sin, x2 * cos + x1 * sin], -1).astype(x.dtype)
    return jnp.concatenate([rot, x[..., ROT_DIM:]], -1)


def s5_branch(u, a_re, a_im, log_dt, b_re, b_im, c_re, c_im, d, w_glu, b_glu, s0_re, s0_im):
    f32 = jnp.float32
    nb, L, _ = u.shape
    ug = u.reshape(nb, L, S_GROUPS, S_GROUP).astype(f32)
    dt = jnp.exp(log_dt.astype(f32))[:, None]
    ar, ai = a_re.astype(f32), a_im.astype(f32)
    mag = jnp.exp(ar * dt)
    abar_re, abar_im = mag * jnp.cos(ai * dt), mag * jnp.sin(ai * dt)
    den = ar * ar + ai * ai
    em_re = abar_re - 1.0
    coef_re = (em_re * ar + abar_im * ai) / den
    coef_im = (abar_im * ar - em_re * ai) / den
    bu_re = jnp.einsum('blgc,gpc->blgp', ug, b_re.astype(f32))
    bu_im = jnp.einsum('blgc,gpc->blgp', ug, b_im.astype(f32))
    x_re = coef_re * bu_re - coef_im * bu_im
    x_im = coef_re * bu_im + coef_im * bu_re
    s0_re, s0_im = s0_re.astype(f32), s0_im.astype(f32)
    x_re = x_re.at[:, 0].add(abar_re * s0_re - abar_im * s0_im)
    x_im = x_im.at[:, 0].add(abar_re * s0_im + abar_im * s0_re)
    shp = x_re.shape
    elems = (jnp.broadcast_to(abar_re, shp), jnp.broadcast_to(abar_im, shp), x_re, x_im)

    def combine(e1, e2):
        a1r, a1i, b1r, b1i = e1
        a2r, a2i, b2r, b2i = e2
        return (a2r * a1r - a2i * a1i, a2r * a1i + a2i * a1r,
                a2r * b1r - a2i * b1i + b2r, a2r * b1i + a2i * b1r + b2i)

    _, _, s_re, s_im = lax.associative_scan(combine, elems, axis=1)
    y = (jnp.einsum('blgp,gcp->blgc', s_re, c_re.astype(f32))
         - jnp.einsum('blgp,gcp->blgc', s_im, c_im.astype(f32)) + d * ug)
    z = jax.nn.gelu(y.reshape(nb, L, S_WIDTH))
    out = z * jax.nn.sigmoid(z @ w_glu.astype(f32) + b_glu)
    return out.astype(u.dtype), s_re[:, -1], s_im[:, -1]


def rwkv7_branch(c, shift0, s0, mu, w0, w2, a0, a2, g2, k_k, k_a, r_k, lnx_g, lnx_b):
    f32 = jnp.float32
    nb, L, _ = c.shape
    prev = jnp.concatenate([shift0[:, None].astype(c.dtype), c[:, :-1]], axis=1)
    cf = (c + (prev - c) * mu).astype(f32)
    o = 3 * R_WIDTH
    r = cf[..., :R_WIDTH]
    k = cf[..., R_WIDTH:2 * R_WIDTH]
    v = cf[..., 2 * R_WIDTH:o]
    wl = cf[..., o:o + W_LORA]
    al = cf[..., o + W_LORA:o + W_LORA + A_LORA]
    gl = cf[..., o + W_LORA + A_LORA:]
    w_log = -jax.nn.softplus(-(w0 + jnp.tanh(wl) @ w2)) - 0.5
    decay = jnp.exp(-jnp.exp(w_log))
    a = jax.nn.sigmoid(a0 + al @ a2)
    g = jax.nn.sigmoid(gl) @ g2

    def heads(t):
        return t.reshape(nb, L, R_HEADS, R_HEAD)

    kk = heads(k * k_k)
    kk = kk * lax.rsqrt(jnp.maximum(jnp.sum(kk * kk, -1, keepdims=True), 1e-24))
    k = k * (1.0 + (a - 1.0) * k_a)
    rh, wh, kh, vh, ah = heads(r), heads(decay), heads(k), heads(v), heads(a)
    a_vec = -kk
    b_vec = kk * ah

    def step(S, inp):
        r_t, w_t, k_t, v_t, a_t, b_t = inp
        sa = jnp.einsum('bhij,bhj->bhi', S, a_t)
        S = (S * w_t[:, :, None, :] + sa[..., None] * b_t[:, :, None, :]
             + v_t[..., None] * k_t[:, :, None, :])
        return S, jnp.einsum('bhij,bhj->bhi', S, r_t)

    xs = tuple(jnp.moveaxis(t, 1, 0) for t in (rh, wh, kh, vh, a_vec, b_vec))
    s_fin, y = lax.scan(step, s0.astype(f32), xs)
    y = jnp.moveaxis(y, 0, 1)
    m = jnp.mean(y, -1, keepdims=True)
    var = jnp.mean(jnp.square(y - m), -1, keepdims=True)
    y = ((y - m) * lax.rsqrt(var + GN_EPS)).reshape(nb, L, R_WIDTH) * lnx_g + lnx_b
    bonus = jnp.sum(rh * kh * r_k, -1, keepdims=True) * vh
    y = y + bonus.reshape(nb, L, R_WIDTH)
    return (y * g).astype(c.dtype), c[:, -1], s_fin


def moba_attend(q, q_pos, kb, vb, k_means, n_topk):
    f32 = jnp.float32
    nb, H, Q, dh = q.shape
    NB = kb.shape[1]
    kv_of_h = jnp.arange(H) // (H // A_KV_HEADS)
    qf = q.astype(f32)
    own = q_pos // MOBA_BLOCK
    gate = jnp.einsum('bhqd,bnhd->bhqn', qf, k_means[:, :, kv_of_h])
    past = jnp.arange(NB)[None, :] < own[:, None]
    gate = jnp.where(past, gate, -jnp.inf)
    _, top_idx = lax.top_k(gate, n_topk)
    own_idx = jnp.broadcast_to(own[None, None, :, None], (nb, H, Q, 1)).astype(top_idx.dtype)
    idx = jnp.concatenate([top_idx, own_idx], -1)
    rank = jnp.arange(n_topk + 1)
    blk_ok = jnp.where(rank[None, :] < n_topk, rank[None, :] < own[:, None], True)
    bi = jnp.arange(nb)[:, None, None, None]
    hi = kv_of_h[None, :, None, None]
    kg = kb[bi, idx, :, hi].astype(f32)
    vg = vb[bi, idx, :, hi].astype(f32)
    key_pos = idx[..., None] * MOBA_BLOCK + jnp.arange(MOBA_BLOCK)
    mask = blk_ok[None, None, :, :, None] & (key_pos <= q_pos[None, None, :, None, None])
    s = jnp.einsum('bhqd,bhqnsd->bhqns', qf, kg) * (dh ** -0.5)
    s = jnp.where(mask, s, -jnp.inf)
    p = jax.nn.softmax(s.reshape(nb, H, Q, -1), axis=-1).reshape(s.shape)
    o = jnp.einsum('bhqns,bhqnsd->bhqd', p, vg)
    return o.astype(q.dtype)


def moba_branch(q, pos, k_parts, v_parts):
    nb, L, H, dh = q.shape
    T = sum(t.shape[1] for t in k_parts)
    NB = -(-T // MOBA_BLOCK)
    pad = NB * MOBA_BLOCK - T

    def to_blocks(parts):
        dt = parts[0].dtype
        z = jnp.zeros((nb, pad, A_KV_HEADS, dh), dt)
        full = jnp.concatenate([t.astype(dt) for t in parts] + [z], axis=1)
        return full.reshape(nb, NB, MOBA_BLOCK, A_KV_HEADS, dh)

    kb = to_blocks(k_parts)
    vb = to_blocks(v_parts)
    k_means = jnp.mean(kb, axis=2, dtype=jnp.float32)
    n_topk = min(MOBA_TOPK, NB)
    qc = Q_CHUNK if L % Q_CHUNK == 0 else L
    nc = L // qc
    q_chunks = q.reshape(nb, nc, qc, H, dh).transpose(1, 0, 3, 2, 4)
    pos_chunks = pos.reshape(nc, qc)
    o = lax.map(lambda a: moba_attend(a[0], a[1], kb, vb, k_means, n_topk), (q_chunks, pos_chunks))
    return o.transpose(1, 0, 3, 2, 4).reshape(nb, L, H * dh)


def mixer_sublayer(x, pos, l, P, s_re0, s_im0, rwkv0, shift0, k_parts, v_parts):
    nb, L, _ = x.shape
    h = jnp.einsum('bld,dn->bln', x, P['w_in'][l])
    gates = jax.nn.sigmoid(h[..., :O_SSM].astype(jnp.float32)).reshape(nb, L, N_BRANCH, D_MODEL)
    u = h[..., O_SSM:O_RWKV]
    c = h[..., O_RWKV:O_Q]
    q = rotary(h[..., O_Q:O_K].reshape(nb, L, A_HEADS, A_HEAD), pos)
    k = rotary(h[..., O_K:O_V].reshape(nb, L, A_KV_HEADS, A_HEAD), pos)
    v = h[..., O_V:].reshape(nb, L, A_KV_HEADS, A_HEAD)
    y_s, s_re, s_im = s5_branch(u, P['ssm_a_re'][l], P['ssm_a_im'][l], P['ssm_log_dt'][l],
                                P['ssm_b_re'][l], P['ssm_b_im'][l], P['ssm_c_re'][l], P['ssm_c_im'][l],
                                P['ssm_d'][l], P['ssm_w_glu'][l], P['ssm_b_glu'][l], s_re0, s_im0)
    y_r, shift, rwkv = rwkv7_branch(c, shift0, rwkv0, P['rwkv_mu'][l], P['rwkv_w0'][l], P['rwkv_w2'][l],
                                    P['rwkv_a0'][l], P['rwkv_a2'][l], P['rwkv_g2'][l], P['rwkv_k_k'][l],
                                    P['rwkv_k_a'][l], P['rwkv_r_k'][l], P['rwkv_lnx_g'][l], P['rwkv_lnx_b'][l])
    y_a = moba_branch(q, pos, k_parts + [k], v_parts + [v])
    merged = (gates[:, :, 0] * (y_s @ P['proj_ssm'][l])
              + gates[:, :, 1] * (y_r @ P['proj_rwkv'][l])
              + gates[:, :, 2] * (y_a @ P['proj_attn'][l]))
    out = merged.astype(x.dtype) @ P['w_o'][l]
    x = layer_norm(ALPHA * x + out, P['ln1_g'][l], P['ln1_b'][l])
    return x, (k, v, s_re, s_im, rwkv, shift)


def ffn_sublayer(x, l, P, conv0):
    L = x.shape[1]
    up = x @ P['ffn_w_up'][l]
    ext = jnp.concatenate([conv0.astype(up.dtype), up], axis=1)
    w = P['ffn_conv_w'][l]
    cv = P['ffn_conv_b'][l] + w[0] * ext[:, 0:L]
    for j in range(1, CONV_W):
        cv = cv + w[j] * ext[:, j:j + L]
    hmid = jax.nn.gelu(cv[..., :D_FF]) * cv[..., D_FF:]
    out = hmid @ P['ffn_w_down'][l]
    x = layer_norm(ALPHA * x + out, P['ln2_g'][l], P['ln2_b'][l])
    return x, ext[:, -(CONV_W - 1):]


def trunk(x, pos, P, ssm_re0, ssm_im0, rwkv0, shift0, conv0, cache_k, cache_v, page_table):
    f32 = jnp.float32
    nb = x.shape[0]
    x = layer_norm(x, P['ln_in_g'], P['ln_in_b'])
    ks, vs, sres, sims, rws, shs, cvs = [], [], [], [], [], [], []
    for l in range(DEPTH):
        if cache_k is None:
            s_re0 = jnp.zeros((nb, S_GROUPS, S_STATE), f32)
            s_im0 = jnp.zeros((nb, S_GROUPS, S_STATE), f32)
            r0 = jnp.zeros((nb, R_HEADS, R_HEAD, R_HEAD), f32)
            sh0 = jnp.zeros((nb, R_IN), x.dtype)
            c0 = jnp.zeros((nb, CONV_W - 1, 2 * D_FF), x.dtype)
            k_parts, v_parts = [], []
        else:
            s_re0, s_im0, r0, sh0, c0 = ssm_re0[l], ssm_im0[l], rwkv0[l], shift0[l], conv0[l]
            k_parts = [cache_k[l, page_table].reshape(nb, -1, A_KV_HEADS, A_HEAD)]
            v_parts = [cache_v[l, page_table].reshape(nb, -1, A_KV_HEADS, A_HEAD)]
        x, (k, v, s_re, s_im, rw, sh) = mixer_sublayer(x, pos, l, P, s_re0, s_im0, r0, sh0, k_parts, v_parts)
        x, cv = ffn_sublayer(x, l, P, c0)
        ks.append(k); vs.append(v); sres.append(s_re); sims.append(s_im)
        rws.append(rw); shs.append(sh); cvs.append(cv)
    return x, (jnp.stack(ks), jnp.stack(vs), jnp.stack(sres), jnp.stack(sims),
               jnp.stack(rws), jnp.stack(shs), jnp.stack(cvs))


def setup_inputs(seed: int = 0) -> dict:
    key = jax.random.key(seed)
    keys = iter(jax.random.split(key, 64))
    f32 = jnp.float32

    def nrm(shape, scale=1.0):
        return jax.random.normal(next(keys), shape, f32) * scale

    n_pages = PAST_LEN // PAGE_SIZE
    n_used = DEC_BATCH * n_pages
    n_pool = n_used + max(1, n_used // 4)
    perm = jax.random.permutation(next(keys), n_pool)
    page_table = perm[:n_used].reshape(DEC_BATCH, n_pages).astype(jnp.int32)
    inp = {}
    inp['x_prompt'] = nrm((BATCH, SEQ, D_MODEL))
    inp['x_sample'] = nrm((DEC_BATCH, DEC_SEQ, D_MODEL))
    inp['cache_k'] = nrm((DEPTH, n_pool, PAGE_SIZE, A_KV_HEADS, A_HEAD))
    inp['cache_v'] = nrm((DEPTH, n_pool, PAGE_SIZE, A_KV_HEADS, A_HEAD))
    inp['page_table'] = page_table
    inp['state_ssm_re'] = nrm((DEPTH, DEC_BATCH, S_GROUPS, S_STATE), 0.1)
    inp['state_ssm_im'] = nrm((DEPTH, DEC_BATCH, S_GROUPS, S_STATE), 0.1)
    inp['state_rwkv'] = nrm((DEPTH, DEC_BATCH, R_HEADS, R_HEAD, R_HEAD), 0.3)
    inp['state_rwkv_shift'] = nrm((DEPTH, DEC_BATCH, R_IN))
    inp['state_conv'] = nrm((DEPTH, DEC_BATCH, CONV_W - 1, 2 * D_FF))
    inp['ln_in_g'] = 1.0 + nrm((D_MODEL,), 0.02)
    inp['ln_in_b'] = nrm((D_MODEL,), 0.02)
    inp['w_in'] = nrm((DEPTH, D_MODEL, N_IN), D_MODEL ** -0.5)
    inp['ssm_a_re'] = -0.5 + nrm((DEPTH, S_GROUPS, S_STATE), 0.01)
    inp['ssm_a_im'] = math.pi * jnp.arange(S_STATE, dtype=f32) + nrm((DEPTH, S_GROUPS, S_STATE), 0.01)
    inp['ssm_log_dt'] = jax.random.uniform(next(keys), (DEPTH, S_GROUPS), f32, math.log(DT_MIN), math.log(DT_MAX))
    inp['ssm_b_re'] = nrm((DEPTH, S_GROUPS, S_STATE, S_GROUP), S_GROUP ** -0.5)
    inp['ssm_b_im'] = nrm((DEPTH, S_GROUPS, S_STATE, S_GROUP), S_GROUP ** -0.5)
    inp['ssm_c_re'] = nrm((DEPTH, S_GROUPS, S_GROUP, S_STATE), S_STATE ** -0.5)
    inp['ssm_c_im'] = nrm((DEPTH, S_GROUPS, S_GROUP, S_STATE), S_STATE ** -0.5)
    inp['ssm_d'] = nrm((DEPTH, S_GROUPS, S_GROUP))
    inp['ssm_w_glu'] = nrm((DEPTH, S_WIDTH, S_WIDTH), S_WIDTH ** -0.5)
    inp['ssm_b_glu'] = nrm((DEPTH, S_WIDTH), 0.02)
    inp['rwkv_mu'] = jax.random.uniform(next(keys), (DEPTH, R_IN), f32)
    inp['rwkv_w0'] = jax.random.uniform(next(keys), (DEPTH, R_WIDTH), f32, -6.0, -1.0)
    inp['rwkv_w2'] = nrm((DEPTH, W_LORA, R_WIDTH), 0.1 * W_LORA ** -0.5)
    inp['rwkv_a0'] = nrm((DEPTH, R_WIDTH), 0.1)
    inp['rwkv_a2'] = nrm((DEPTH, A_LORA, R_WIDTH), 0.1 * A_LORA ** -0.5)
    inp['rwkv_g2'] = nrm((DEPTH, G_LORA, R_WIDTH), G_LORA ** -0.5)
    inp['rwkv_k_k'] = 0.85 + nrm((DEPTH, R_WIDTH), 0.02)
    inp['rwkv_k_a'] = 1.0 + nrm((DEPTH, R_WIDTH), 0.02)
    inp['rwkv_r_k'] = -0.04 + nrm((DEPTH, R_HEADS, R_HEAD), 0.02)
    inp['rwkv_lnx_g'] = 1.0 + nrm((DEPTH, R_WIDTH), 0.02)
    inp['rwkv_lnx_b'] = nrm((DEPTH, R_WIDTH), 0.02)
    inp['proj_ssm'] = nrm((DEPTH, S_WIDTH, D_MODEL), S_WIDTH ** -0.5)
    inp['proj_rwkv'] = nrm((DEPTH, R_WIDTH, D_MODEL), R_WIDTH ** -0.5)
    inp['proj_attn'] = nrm((DEPTH, A_QW, D_MODEL), A_QW ** -0.5)
    inp['w_o'] = nrm((DEPTH, D_MODEL, D_MODEL), BETA * D_MODEL ** -0.5)
    inp['ln1_g'] = 1.0 + nrm((DEPTH, D_MODEL), 0.02)
    inp['ln1_b'] = nrm((DEPTH, D_MODEL), 0.02)
    inp['ffn_w_up'] = nrm((DEPTH, D_MODEL, 2 * D_FF), D_MODEL ** -0.5)
    inp['ffn_conv_w'] = nrm((DEPTH, CONV_W, 2 * D_FF), CONV_W ** -0.5)
    inp['ffn_conv_b'] = nrm((DEPTH, 2 * D_FF), 0.02)
    inp['ffn_w_down'] = nrm((DEPTH, D_FF, D_MODEL), BETA * D_FF ** -0.5)
    inp['ln2_g'] = 1.0 + nrm((DEPTH, D_MODEL), 0.02)
    inp['ln2_b'] = nrm((DEPTH, D_MODEL), 0.02)
    return inp


def reference(x_prompt, x_sample, cache_k, cache_v, page_table, state_ssm_re, state_ssm_im,
              state_rwkv, state_rwkv_shift, state_conv, ln_in_g, ln_in_b, w_in,
              ssm_a_re, ssm_a_im, ssm_log_dt, ssm_b_re, ssm_b_im, ssm_c_re, ssm_c_im, ssm_d,
              ssm_w_glu, ssm_b_glu, rwkv_mu, rwkv_w0, rwkv_w2, rwkv_a0, rwkv_a2, rwkv_g2,
              rwkv_k_k, rwkv_k_a, rwkv_r_k, rwkv_lnx_g, rwkv_lnx_b, proj_ssm, proj_rwkv, proj_attn,
              w_o, ln1_g, ln1_b, ffn_w_up, ffn_conv_w, ffn_conv_b, ffn_w_down, ln2_g, ln2_b):
    P = dict(ln_in_g=ln_in_g, ln_in_b=ln_in_b, w_in=w_in,
             ssm_a_re=ssm_a_re, ssm_a_im=ssm_a_im, ssm_log_dt=ssm_log_dt,
             ssm_b_re=ssm_b_re, ssm_b_im=ssm_b_im, ssm_c_re=ssm_c_re, ssm_c_im=ssm_c_im,
             ssm_d=ssm_d, ssm_w_glu=ssm_w_glu, ssm_b_glu=ssm_b_glu,
             rwkv_mu=rwkv_mu, rwkv_w0=rwkv_w0, rwkv_w2=rwkv_w2, rwkv_a0=rwkv_a0, rwkv_a2=rwkv_a2,
             rwkv_g2=rwkv_g2, rwkv_k_k=rwkv_k_k, rwkv_k_a=rwkv_k_a, rwkv_r_k=rwkv_r_k,
             rwkv_lnx_g=rwkv_lnx_g, rwkv_lnx_b=rwkv_lnx_b,
             proj_ssm=proj_ssm, proj_rwkv=proj_rwkv, proj_attn=proj_attn, w_o=w_o,
             ln1_g=ln1_g, ln1_b=ln1_b, ffn_w_up=ffn_w_up, ffn_conv_w=ffn_conv_w,
             ffn_conv_b=ffn_conv_b, ffn_w_down=ffn_w_down, ln2_g=ln2_g, ln2_b=ln2_b)
    pos_p = jnp.arange(x_prompt.shape[1], dtype=jnp.int32)
    y_prompt, (k_p, v_p, sre_p, sim_p, rw_p, sh_p, cv_p) = trunk(
        x_prompt, pos_p, P, None, None, None, None, None, None, None, None)
    past_len = page_table.shape[1] * PAGE_SIZE
    pos_s = past_len + jnp.arange(x_sample.shape[1], dtype=jnp.int32)
    y_sample, (k_s, v_s, sre_s, sim_s, rw_s, sh_s, cv_s) = trunk(
        x_sample, pos_s, P, state_ssm_re, state_ssm_im, state_rwkv, state_rwkv_shift, state_conv,
        cache_k, cache_v, page_table)
    return (y_prompt, y_sample, k_p, v_p, k_s, v_s, sre_p, sim_p, sre_s, sim_s,
            rw_p, rw_s, sh_p, sh_s, cv_p, cv_s)
```

```python
import math
from contextlib import ExitStack
import numpy as np
import concourse.bass as bass
import concourse.mybir as mybir
from concourse.bass_utils import run_bass_kernel_spmd

F32 = mybir.dt.float32
BF16 = mybir.dt.bfloat16
I32 = mybir.dt.int32
AF = mybir.ActivationFunctionType
ALU = mybir.AluOpType
AX = mybir.AxisListType

N_DSEM = 12


class Trk:
    __slots__ = ("w", "r", "excl")

    def __init__(self, excl=False):
        self.w = None
        self.r = {}
        self.excl = excl


class V:
    __slots__ = ("trk", "ap")

    def __init__(self, trk, ap):
        self.trk = trk
        self.ap = ap

    def __getitem__(self, idx):
        return V(self.trk, self.ap[idx])


class Buf:
    def __init__(self, t, space):
        self.t = t
        self.space = space
        self.trk = Trk(space == "ps")
        self.parts = {}

    def __getitem__(self, idx):
        return V(self.trk, self.t[idx])

    def part(self, key, idx):
        trk = self.parts.get(key)
        if trk is None:
            trk = self.parts[key] = Trk(self.space == "ps")
        return V(trk, self.t[idx])

    def v(self, ap, key=None):
        if key is None:
            return V(self.trk, ap)
        trk = self.parts.get(key)
        if trk is None:
            trk = self.parts[key] = Trk(self.space == "ps")
        return V(trk, ap)


class Prog:
    ENG = ("pe", "act", "dve", "pool", "sp")

    def __init__(self, nc):
        self.nc = nc
        self.ops = {k: [] for k in self.ENG}
        self.cnt = {k: 0 for k in self.ENG}
        self.sem = {k: nc.alloc_semaphore(name="s_" + k) for k in self.ENG}
        self.semobj = {("e", k): self.sem[k] for k in self.ENG}
        self.dq = {}
        for q in ("sp", "pool", "act"):
            sems = [nc.alloc_semaphore(name="d_%s_%d" % (q, i)) for i in range(N_DSEM)]
            for i, s in enumerate(sems):
                self.semobj[("d", q, i)] = s
            self.dq[q] = {"tgt": [0] * N_DSEM, "next": 0}
        self.seen = {k: {} for k in self.ENG}
        self.stack = ExitStack()
        self.n_inst = 0
        self.psum_pool = []

    def sb(self, name, shape, dtype, stack=None):
        self.uid = getattr(self, "uid", 0) + 1
        name = "%s_u%d" % (name, self.uid)
        t = (stack or self.stack).enter_context(self.nc.sbuf_tensor(name, list(shape), dtype))
        return Buf(t, "sb")

    def ps(self, name, shape=(128, 512), dtype=F32, stack=None):
        t = (stack or self.stack).enter_context(self.nc.psum_tensor(name, list(shape), dtype))
        return Buf(t, "ps")

    def dram(self, name, shape, dtype, kind="Internal"):
        t = self.nc.dram_tensor(name, list(shape), dtype, kind=kind)
        return Buf(t, "dram")

    def _deps(self, reads, writes):
        deps = {}

        def add(tok):
            if tok is None:
                return
            k, v = tok
            if deps.get(k, 0) < v:
                deps[k] = v

        for x in reads:
            add(x.trk.w)
            if x.trk.excl:
                for k, v in x.trk.r.items():
                    add((k, v))
        for x in writes:
            add(x.trk.w)
            for k, v in x.trk.r.items():
                add((k, v))
        return deps

    def _mark(self, tok, reads, writes):
        for x in reads:
            k, v = tok
            if x.trk.r.get(k, 0) < v:
                x.trk.r[k] = v
        for x in writes:
            x.trk.w = tok
            x.trk.r = {}

    def _waits(self, eng, deps):
        ws = []
        seen = self.seen[eng]
        for k, v in deps.items():
            if eng == "pe" and k == ("e", "pe"):
                continue
            if seen.get(k, 0) < v:
                seen[k] = v
                ws.append((k, v))
        return ws

    def emit(self, eng, fn, reads, writes):
        reads = [r for r in reads if isinstance(r, V)]
        deps = self._deps(reads, writes)
        ws = self._waits(eng, deps)
        self.cnt[eng] += 1
        tok = (("e", eng), self.cnt[eng])
        self.ops[eng].append((ws, fn, (self.sem[eng], 1)))
        self._mark(tok, reads, writes)
        self.n_inst += 1 + len(ws)
        return tok

    def dma(self, q, out, in_, **kw):
        oap, iap = out.ap, in_.ap
        return self.dma_custom(q, lambda e: e.dma_start(out=oap, in_=iap, **kw), [in_], [out])

    def dma_custom(self, q, fn, reads, writes):
        deps = self._deps(reads, writes)
        st = self.dq[q]
        slot = st["next"] % N_DSEM
        st["next"] += 1
        key = ("d", q, slot)
        if st["tgt"][slot] > 0:
            if deps.get(key, 0) < st["tgt"][slot]:
                deps[key] = st["tgt"][slot]
        ws = self._waits(q, deps)
        st["tgt"][slot] += 16
        tok = (key, st["tgt"][slot])
        self.ops[q].append((ws, fn, (self.semobj[key], 16)))
        self._mark(tok, reads, writes)
        self.n_inst += 1 + len(ws)
        return tok

    def barrier(self):
        deps = {}
        for k in self.ENG:
            if self.cnt[k] > 0:
                deps[("e", k)] = self.cnt[k]
        for q, st in self.dq.items():
            for i in range(N_DSEM):
                if st["tgt"][i] > 0:
                    deps[("d", q, i)] = st["tgt"][i]
        for eng in self.ENG:
            d = {k: v for k, v in deps.items() if k != ("e", eng)}
            ws = self._waits(eng, d)
            if ws:
                self.ops[eng].append((ws, None, None))
                self.n_inst += len(ws)

    def finish(self):
        self.barrier()
        nc = self.nc
        eobj = {"pe": "tensor", "act": "scalar", "dve": "vector", "pool": "gpsimd", "sp": "sync"}
        with nc.Block() as block:
            for eng in self.ENG:
                ops = self.ops[eng]
                semobj = self.semobj

                def run(e, ops=ops):
                    for ws, fn, inc in ops:
                        for k, v in ws:
                            e.wait_ge(semobj[k], v)
                        if fn is not None:
                            ins = fn(e)
                            ins.then_inc(inc[0], inc[1])

                getattr(block, eobj[eng])(run)
        self.stack.close()

    def mm(self, out, lhsT, rhs, start=True, stop=True, **kw):
        o, l, r = out.ap, lhsT.ap, rhs.ap
        return self.emit("pe", lambda e: e.matmul(o, l, r, start=start, stop=stop, **kw), [lhsT, rhs], [out])

    def tr(self, out, in_, ident):
        o, i, d = out.ap, in_.ap, ident.ap
        return self.emit("pe", lambda e: e.transpose(o, i, d), [in_, ident], [out])

    def act(self, out, in_, func, bias=None, scale=1.0, accum=None, eng="act"):
        o, i = out.ap, in_.ap
        b = bias.ap if isinstance(bias, V) else bias
        s = scale.ap if isinstance(scale, V) else scale
        kw = {}
        if b is not None:
            kw["bias"] = b
        if accum is not None:
            kw["accum_out"] = accum.ap
        wr = [out] + ([accum] if accum is not None else [])
        return self.emit("act", lambda e: e.activation(out=o, in_=i, func=func, scale=s, **kw),
                         [in_, bias, scale], wr)

    def copy(self, eng, out, in_):
        o, i = out.ap, in_.ap
        if eng == "act":
            return self.emit("act", lambda e: e.copy(o, i), [in_], [out])
        return self.emit(eng, lambda e: e.tensor_copy(o, i), [in_], [out])

    def tt(self, eng, out, in0, in1, op):
        o, a, b = out.ap, in0.ap, in1.ap
        return self.emit(eng, lambda e: e.tensor_tensor(o, a, b, op), [in0, in1], [out])

    def ts(self, eng, out, in0, s1, s2=None, op0=ALU.mult, op1=None, accum=None):
        o, a = out.ap, in0.ap
        x1 = s1.ap if isinstance(s1, V) else s1
        x2 = s2.ap if isinstance(s2, V) else s2
        kw = {}
        if op1 is not None:
            kw["op1"] = op1
        if accum is not None:
            kw["accum_out"] = accum.ap
        wr = [out] + ([accum] if accum is not None else [])
        return self.emit(eng, lambda e: e.tensor_scalar(o, a, x1, x2, op0, **kw), [in0, s1, s2], wr)

    def stt(self, out, in0, scalar, in1, op0, op1, eng="dve"):
        o, a, b = out.ap, in0.ap, in1.ap
        s = scalar.ap if isinstance(scalar, V) else scalar
        return self.emit(eng, lambda e: e.scalar_tensor_tensor(o, a, s, b, op0, op1), [in0, scalar, in1], [out])

    def scan(self, out, d0, d1, init, op0, op1):
        o, a, b = out.ap, d0.ap, d1.ap
        s = init.ap if isinstance(init, V) else init
        return self.emit("dve", lambda e: e.tensor_tensor_scan(o, a, b, s, op0, op1), [d0, d1, init], [out])

    def memset(self, eng, out, val):
        o = out.ap
        return self.emit(eng, lambda e: e.memset(o, val), [], [out])

    def reduce(self, out, in_, op, axis=AX.X, eng="dve"):
        o, i = out.ap, in_.ap
        return self.emit(eng, lambda e: e.tensor_reduce(o, i, axis, op), [in_], [out])

    def recip(self, out, in_):
        o, i = out.ap, in_.ap
        return self.emit("dve", lambda e: e.reciprocal(o, i), [in_], [out])


D = 1024
DEPTH = 2
NS = 4
PAGE = 128
R_IN = 896
N_IN = 5248
O_SSM, O_RWKV, O_Q, O_K, O_V = 3072, 3328, 4224, 4736, 4992
D_FF = 2816
NFC = 44
ALPHA = (2 * DEPTH) ** 0.25
LN_EPS = 1e-5
GN_EPS = 64e-5
ROPE_THETA = 500000.0
NEG = -30000.0


def bc_mid(ap, n):
    return bass.AP(ap.tensor, ap.offset, [list(ap.ap[0]), [0, n]] + [list(x) for x in ap.ap[1:]])


def bc_last(ap, n):
    return bass.AP(ap.tensor, ap.offset, [list(x) for x in ap.ap] + [[0, n]])


class Ctx:
    pass


class Model:
    def __init__(self, cfg):
        self.cfg = cfg
        self.L = cfg["L"]
        self.NPG = cfg["NPG"]
        self.npool = cfg["npool"]
        self.dbg = cfg.get("dbg", {})
        self.NT = self.L + NS
        self.nc = bass.Bass("TRN2", target_bir_lowering=False)
        self.P = Prog(self.nc)
        self.inputs = {}
        self.outputs = {}
        L = self.L
        self.groups = [(t0, 512) for t0 in range(0, L, 512)] + [(L, NS)]

    def din(self, name, shape, dtype=F32):
        b = self.P.dram(name, shape, dtype, kind="ExternalInput")
        self.inputs[name] = (tuple(shape), dtype)
        return b

    def dout(self, name, shape, dtype=F32):
        b = self.P.dram(name, shape, dtype, kind="ExternalOutput")
        self.outputs[name] = (tuple(shape), dtype)
        return b

    def scratch(self, name, shape, dtype=F32):
        if name in self.dbg.get("out", ()):
            return self.dout(name, shape, dtype)
        if name in self.dbg.get("in", ()):
            return self.din(name, shape, dtype)
        return self.P.dram(name, shape, dtype)

    def subs(self, n):
        return [(i, 128) for i in range(n // 128)] if n >= 128 else [(0, n)]

    def load_x(self, st, xsrc, t0, n, xt):
        P = self.P
        for i, m in self.subs(n):
            P.dma("sp", xt.part(i, (slice(0, m), i, slice(None))), xsrc.part(("r", t0 + i * 128), (slice(t0 + i * 128, t0 + i * 128 + m), slice(None))))

    def make_xT(self, xt, n, xT):
        P = self.P
        for k in range(8):
            ps = self.PS[k % 2]
            for i, m in self.subs(n):
                P.tr(ps[:, i * 128:i * 128 + m], xt.part(i, (slice(0, m), i, slice(k * 128, (k + 1) * 128))), self.ident[0:m, 0:m])
            if k % 2 == 0:
                P.copy("act", xT[:, k, 0:n], ps[:, 0:n])
            else:
                P.copy("dve", xT[:, k, 0:n], ps[:, 0:n])

    def ln_rows(self, src, m, gB, bB, dst, eps):
        P = self.P
        st, mv = self.ln_st, self.ln_mv
        for h in range(2):
            sap = src.ap[0:m, h * 512:(h + 1) * 512]
            P.emit("dve", lambda e, h=h, sap=sap: e.bn_stats(st.t[0:m, h * 6:(h + 1) * 6], sap), [src], [st[0:m, :]])
        P.emit("dve", lambda e: e.bn_aggr(mv.t[0:m, 0:2], st.t[0:m, 0:12]), [st[0:m, :]], [mv[0:m, :]])
        P.act(mv[0:m, 2:3], mv[0:m, 1:2], AF.Sqrt, bias=self.epsc[eps][0:m, 0:1])
        P.recip(mv[0:m, 3:4], mv[0:m, 2:3])
        P.ts("dve", V(dst.trk, dst.ap[0:m, :]), V(src.trk, src.ap[0:m, :]), mv[0:m, 0:1], mv[0:m, 3:4], op0=ALU.subtract, op1=ALU.mult)
        P.tt("pool", V(dst.trk, dst.ap[0:m, :]), V(dst.trk, dst.ap[0:m, :]), gB[0:m, :], ALU.mult)
        P.tt("pool", V(dst.trk, dst.ap[0:m, :]), V(dst.trk, dst.ap[0:m, :]), bB[0:m, :], ALU.add)

    def bcast_row_load(self, dst, src_buf, off, ncols, np_=128):
        ap = bass.AP(src_buf.t, off, [[0, np_], [1, ncols]])
        self.P.dma("sp", dst[0:np_, 0:ncols], src_buf.v(ap))

    def setup(self):
        P = self.P
        L, NT = self.L, self.NT
        self.PS = [P.ps("ps%d" % i) for i in range(8)]
        self.x_p = self.din("x_p", [L, D])
        self.x_s = self.din("x_s", [NS, D])
        self.c_ident = self.din("c_ident", [128, 128])
        self.w_in = self.din("w_in", [DEPTH, D, N_IN])
        self.proj = self.din("proj", [DEPTH, D, D])
        self.w_o = self.din("w_o", [DEPTH, D, D])
        self.w_up = self.din("ffn_w_up", [DEPTH, D, 2 * D_FF])
        self.w_dn = self.din("ffn_w_down", [DEPTH, D_FF, D])
        self.lnrows = self.din("lnrows", [2 + 4 * DEPTH, D])
        self.convp = self.din("convp", [DEPTH, 128, NFC, 4])
        self.conv0 = self.din("conv0", [DEPTH, 128, NFC, NS, 2])
        self.ropeC = self.din("ropeC", [128, L]); self.ropeS = self.din("ropeS", [128, L])
        self.ropeCs = self.din("ropeCs", [128, NS]); self.ropeSs = self.din("ropeSs", [128, NS])
        self.c_rot = self.din("c_rot", [128, 128])
        self.c_causal = self.din("c_causal", [128, 2, 512])
        self.c_onehot = self.din("c_onehot", [32, 32 * 128])
        self.c_pair = self.din("c_pair", [128, 64]); self.c_pairT = self.din("c_pairT", [64, 128])
        self.ptT = self.din("ptT", [self.NPG, NS], I32)
        self.ck_flat = self.din("ck_flat", [DEPTH * self.npool * 16, 2048])
        self.cv_flat = self.din("cv_flat", [DEPTH * self.npool * 16, 2048])
        self.qscr = [self.P.dram("qscr%d" % l, [4, NS, 128], F32) for l in range(DEPTH)]
        self.kscr = [self.P.dram("kscr%d" % l, [4, NS, 64], F32) for l in range(DEPTH)]
        self.k_p = self.dout("k_p", [DEPTH, 4, 64, L]); self.v_p = self.dout("v_p", [DEPTH, L, 256])
        self.k_s = self.dout("k_s", [DEPTH, 4, 64, NS]); self.v_s = self.dout("v_s", [DEPTH, NS, 256])
        self.s5B = self.din("s5B", [DEPTH, 2, 8, 128, 128])
        self.s5C = self.din("s5C", [DEPTH, 2, 8, 128, 128])
        self.w_glu = self.din("ssm_w_glu", [DEPTH, 256, 256])
        self.s5p = self.din("s5p", [DEPTH, 128, 3, 8])
        self.s5cols = self.din("s5cols", [DEPTH, 128, 2, 2])
        self.s5s0 = self.din("s5s0", [DEPTH, 128, 2, 8, NS])
        self.rwcols = self.din("rwcols", [DEPTH, 128, 21])
        self.rwrows = self.din("rwrows", [DEPTH * 2, 256])
        self.lora = self.din("lora", [DEPTH, 128, 256])
        self.c_masks = self.din("c_masks", [128, 3, 128])
        self.c_blk = self.din("c_blk", [128, 128])
        self.c_hsel = self.din("c_hsel", [128, 2])
        self.rw_sh0 = self.din("rw_sh0", [DEPTH, 128, 7, NS])
        self.st_rwkv = self.din("st_rwkv", [DEPTH, NS, 2, 128, 64])
        self.rwscr = [self.P.dram("rwscr%d" % l, [5, 2, NS, 128], F32) for l in range(DEPTH)]
        self.rw_p = self.dout("rw_p", [DEPTH, 128, 2, 64])
        self.rw_s = self.dout("rw_s", [DEPTH, NS, 2, 128, 64])
        self.sh_p = self.dout("sh_p", [DEPTH, 128, 7])
        self.sh_s = self.dout("sh_s", [DEPTH, 128, 7, NS])
        self.sre_p = self.dout("sre_p", [DEPTH, 128, 8])
        self.sim_p = self.dout("sim_p", [DEPTH, 128, 8])
        self.sre_s = self.dout("sre_s", [DEPTH, 128, 8, NS])
        self.sim_s = self.dout("sim_s", [DEPTH, 128, 8, NS])
        self.y_p = self.dout("y_p", [L, D])
        self.y_s = self.dout("y_s", [NS, D])
        self.cv_p = self.dout("cv_p", [DEPTH, 128, NFC, 2])
        self.cv_s = self.dout("cv_s", [DEPTH, 128, NFC, NS, 2])
        self.xA = self.scratch("xA", [NT, D])
        self.xB = self.scratch("xB", [NT, D])
        self.xC = self.scratch("xC", [NT, D])
        self.yT = [self.scratch("yT%d" % l, [8, 128, NT], BF16) for l in range(DEPTH)]
        self.hT = self.scratch("hT", [22, 128, NT], BF16)
        self.ident = P.sb("ident", [128, 128], F32)
        P.dma("sp", self.ident[:, :], self.c_ident[:, :])
        self.identb = P.sb("identb", [128, 128], BF16)
        P.copy("dve", self.identb[:, :], self.ident[:, :])
        self.ln_st = P.sb("ln_st", [128, 12], F32)
        self.ln_mv = P.sb("ln_mv", [128, 4], F32)
        self.epsc = {}
        for eps in (LN_EPS, GN_EPS):
            t = P.sb("eps%d" % len(self.epsc), [128, 1], F32)
            P.memset("pool", t[:, :], eps)
            self.epsc[eps] = t
        self.gB = P.sb("gB", [128, D], F32)
        self.bB = P.sb("bB", [128, D], F32)

    def load_ln(self, row):
        self.bcast_row_load(self.gB, self.lnrows, row * D, D)
        self.bcast_row_load(self.bB, self.lnrows, (row + 1) * D, D)

    def xrow(self, buf, t0, m):
        return buf.part(("r", t0), (slice(t0, t0 + m), slice(None)))

    def pass2(self, l, xsrc, xdst):
        P = self.P
        with ExitStack() as st:
            wg = P.sb("p2_wg", [128, 8, 3072], BF16, st)
            pj = P.sb("p2_pj", [128, 8, D], BF16, st)
            wo = P.sb("p2_wo", [128, 8, D], BF16, st)
            xt = P.sb("p2_xt", [128, 4, D], F32, st)
            xT = P.sb("p2_xT", [128, 8, 512], BF16, st)
            yt = P.sb("p2_yt", [128, 8, 512], BF16, st)
            mT = P.sb("p2_mT", [128, 8, 512], BF16, st)
            gt = [P.sb("p2_g%d" % b, [128, 512], F32, st) for b in range(3)]
            acc = [P.sb("p2_a%d" % b, [128, 512], F32, st) for b in range(3)]
            tres = [P.sb("p2_t%d" % b, [128, D], F32, st) for b in range(2)]
            for k in range(8):
                for c0 in range(0, 3072, 1024):
                    P.dma("pool", wg[:, k, c0:c0 + 1024], self.w_in[l, k * 128:(k + 1) * 128, c0:c0 + 1024])
                P.dma("pool", pj[:, k, :], self.proj[l, k * 128:(k + 1) * 128, :])
                P.dma("pool", wo[:, k, :], self.w_o[l, k * 128:(k + 1) * 128, :])
            self.load_ln(2 + 4 * l)
            PS = self.PS
            for (t0, n) in self.groups:
                self.load_x(st, xsrc, t0, n, xt)
                self.make_xT(xt, n, xT)
                for kc in range(8):
                    P.dma("sp", yt[:, kc, 0:n], self.yT[l].part(("g", t0), (kc, slice(None), slice(t0, t0 + n))))
                for m in range(8):
                    cs = slice(m * 128, (m + 1) * 128)
                    for b in range(3):
                        for k in range(8):
                            P.mm(PS[2 + b][:, 0:n], wg[:, k, b * 1024 + m * 128:b * 1024 + (m + 1) * 128], xT[:, k, 0:n], k == 0, k == 7)
                    kcs = [(0, 2), (2, 4), (4, 8)]
                    for b in range(3):
                        a, e_ = kcs[b]
                        for kc in range(a, e_):
                            P.mm(PS[5 + b][:, 0:n], pj[:, kc, cs], yt[:, kc, 0:n], kc == a, kc == e_ - 1)
                    for b in range(3):
                        P.act(gt[b][:, 0:n], PS[2 + b][:, 0:n], AF.Sigmoid)
                    for b in range(3):
                        P.tt("dve", acc[b][:, 0:n], gt[b][:, 0:n], PS[5 + b][:, 0:n], ALU.mult)
                    P.tt("pool", acc[0][:, 0:n], acc[0][:, 0:n], acc[1][:, 0:n], ALU.add)
                    P.tt("pool", mT[:, m, 0:n], acc[0][:, 0:n], acc[2][:, 0:n], ALU.add)
                for i, m_ in self.subs(n):
                    tr_ = tres[i % 2]
                    for half in range(2):
                        ps = PS[half]
                        for k in range(8):
                            P.mm(ps[0:m_, :], mT[:, k, i * 128:i * 128 + m_], wo[:, k, half * 512:(half + 1) * 512], k == 0, k == 7)
                        P.stt(tr_[0:m_, half * 512:(half + 1) * 512], xt.part(i, (slice(0, m_), i, slice(half * 512, (half + 1) * 512))),
                              ALPHA, ps[0:m_, :], ALU.mult, ALU.add)
                    self.ln_rows(tr_[:, :], m_, self.gB, self.bB, tr_[:, :], LN_EPS)
                    P.dma("sp", self.xrow(xdst, t0 + i * 128, m_), tr_[0:m_, :])
        P.barrier()

    def pass3a(self, l, xsrc):
        P = self.P
        L = self.L
        with ExitStack() as st:
            wup = P.sb("p3_wup", [128, 8, 2 * D_FF], BF16, st)
            xt = P.sb("p3_xt", [128, 4, D], F32, st)
            xT = P.sb("p3_xT", [128, 8, 512], BF16, st)
            cp = P.sb("p3_cp", [128, NFC, 4], F32, st)
            c0 = P.sb("p3_c0", [128, NFC, NS, 2], F32, st)
            halo = P.sb("p3_halo", [128, NFC, 2], F32, st)
            cvs = P.sb("p3_cvs", [128, NFC, NS, 2], F32, st)
            ext = [P.sb("p3_ext%d" % i, [128, 514], F32, st) for i in range(4)]
            cv = [P.sb("p3_cv%d" % i, [128, 512], F32, st) for i in range(4)]
            gl = [P.sb("p3_gl%d" % i, [128, 512], F32, st) for i in range(2)]
            hm = [P.sb("p3_hm%d" % i, [128, 512], BF16, st) for i in range(2)]
            for k in range(8):
                for c in range(0, 2 * D_FF, 1024):
                    w = min(1024, 2 * D_FF - c)
                    P.dma("pool", wup[:, k, c:c + w], self.w_up[l, k * 128:(k + 1) * 128, c:c + w])
            P.dma("sp", cp[:, :, :], self.convp[l, :, :, :])
            P.dma("sp", c0[:, :, :, :], self.conv0[l, :, :, :, :])
            P.memset("pool", halo[:, :, :], 0.0)
            PS = self.PS
            it = 0
            import os as _os
            _sk = _os.environ.get("SKIP", "")
            for (t0, n) in self.groups:
                samp = n < 128
                if samp and "s" in _sk:
                    continue
                self.load_x(st, xsrc, t0, n, xt)
                self.make_xT(xt, n, xT)
                for c in range(22):
                    cvp = []
                    for cc in (c + 22, c):
                        ps = PS[2 + it % 6]
                        e_, cv_ = ext[it % 4], cv[it % 4]
                        it += 1
                        for k in range(8):
                            P.mm(ps[:, 0:n], wup[:, k, cc * 128:(cc + 1) * 128], xT[:, k, 0:n], k == 0, k == 7)
                        if not samp:
                            ce = "act"
                            P.copy(ce, e_[:, 0:2], halo[:, cc, :])
                            P.copy("act", e_[:, 2:2 + n], ps[:, 0:n])
                            P.copy(ce, halo[:, cc, :], e_[:, n:n + 2])
                            P.ts("dve", cv_[:, 0:n], ps[:, 0:n], cp[:, cc, 2:3], cp[:, cc, 3:4], op0=ALU.mult, op1=ALU.add)
                            P.stt(cv_[:, 0:n], e_[:, 1:1 + n], cp[:, cc, 1:2], cv_[:, 0:n], ALU.mult, ALU.add)
                            P.stt(cv_[:, 0:n], e_[:, 0:n], cp[:, cc, 0:1], cv_[:, 0:n], ALU.mult, ALU.add)
                        else:
                            P.copy("act", cvs[:, cc, :, 1], ps[:, 0:n])
                            P.copy("dve", cvs[:, cc, :, 0], c0[:, cc, :, 1])
                            P.ts("dve", cv_[:, 0:n], ps[:, 0:n], cp[:, cc, 2:3], cp[:, cc, 3:4], op0=ALU.mult, op1=ALU.add)
                            P.stt(cv_[:, 0:n], c0[:, cc, :, 1], cp[:, cc, 1:2], cv_[:, 0:n], ALU.mult, ALU.add)
                            P.stt(cv_[:, 0:n], c0[:, cc, :, 0], cp[:, cc, 0:1], cv_[:, 0:n], ALU.mult, ALU.add)
                        cvp.append(cv_)
                    g_, h_ = gl[c % 2], hm[c % 2]
                    P.act(g_[:, 0:n], cvp[1][:, 0:n], AF.Identity if "g" in _sk else AF.Gelu_apprx_tanh)
                    P.tt("dve" if "p" in _sk else "pool", h_[:, 0:n], g_[:, 0:n], cvp[0][:, 0:n], ALU.mult)
                    if "h" not in _sk:
                        P.dma("sp", self.hT.part(("g", t0), (c, slice(None), slice(t0, t0 + n))), h_[:, 0:n])
                if t0 + n == L and "o" not in _sk and "1" not in _sk:
                    P.dma("sp", self.cv_p[l, :, :, :], halo[:, :, :])
            if "o" not in _sk and "2" not in _sk:
                P.dma("sp", self.cv_s[l, :, :, :, :], cvs[:, :, :, :])
        P.barrier()

    def pass3b(self, l, xsrc, xdst_p, xdst_s):
        P = self.P
        L = self.L
        with ExitStack() as st:
            wdn = P.sb("p4_wdn", [128, 22, D], BF16, st)
            xt = P.sb("p4_xt", [128, 4, D], F32, st)
            ht = P.sb("p4_ht", [128, 22, 512], BF16, st)
            tres = [P.sb("p4_t%d" % b, [128, D], F32, st) for b in range(2)]
            for c in range(22):
                P.dma("pool", wdn[:, c, :], self.w_dn[l, c * 128:(c + 1) * 128, :])
            self.load_ln(4 + 4 * l)
            PS = self.PS
            for (t0, n) in self.groups:
                self.load_x(st, xsrc, t0, n, xt)
                for c in range(22):
                    P.dma("sp", ht[:, c, 0:n], self.hT.part(("g", t0), (c, slice(None), slice(t0, t0 + n))))
                for i, m_ in self.subs(n):
                    tr_ = tres[i % 2]
                    for half in range(2):
                        ps = PS[(2 * i + half) % 8]
                        for c in range(22):
                            P.mm(ps[0:m_, :], ht[:, c, i * 128:i * 128 + m_], wdn[:, c, half * 512:(half + 1) * 512], c == 0, c == 21)
                        P.stt(tr_[0:m_, half * 512:(half + 1) * 512], xt.part(i, (slice(0, m_), i, slice(half * 512, (half + 1) * 512))),
                              ALPHA, ps[0:m_, :], ALU.mult, ALU.add)
                    self.ln_rows(tr_[:, :], m_, self.gB, self.bB, tr_[:, :], LN_EPS)
                    if t0 < L:
                        P.dma("sp", self.xrow(xdst_p, t0 + i * 128, m_), tr_[0:m_, :])
                    else:
                        P.dma("sp", self.xrow(xdst_s, (t0 - L) if xdst_s is not xdst_p else t0, m_), tr_[0:m_, :])
        P.barrier()

    def pass0(self):
        P = self.P
        L = self.L
        with ExitStack() as st:
            xt = P.sb("p0_xt", [128, 4, D], F32, st)
            self.load_ln(0)
            for (t0, n) in self.groups:
                src = self.x_p if t0 < L else self.x_s
                s0 = t0 if t0 < L else 0
                self.load_x(st, src, s0, n, xt)
                for i, m_ in self.subs(n):
                    v = xt.part(i, (slice(None), i, slice(None)))
                    self.ln_rows(v, m_, self.gB, self.bB, v, LN_EPS)
                    P.dma("sp", self.xrow(self.xA, t0 + i * 128, m_), xt.part(i, (slice(0, m_), i, slice(None))))
        P.barrier()

    def build(self):
        self.setup()
        stages = self.cfg.get("stages", "0ab23")
        if "0" in stages:
            self.pass0()
        for l in self.cfg.get("layers", range(DEPTH)):
            if "a" in stages and "s" not in self.cfg.get("skip", ""):
                self.pass1a(l, self.xA, "s")
            if "a" in stages and "r" not in self.cfg.get("skip", ""):
                self.pass1a(l, self.xA, "r")
            if "b" in stages:
                self.pass1b(l, self.xA)
            if "2" in stages:
                self.pass2(l, self.xA, self.xB)
            if "3" in stages or "x" in stages:
                self.pass3a(l, self.xB)
            if "3" in stages or "y" in stages:
                last = l == DEPTH - 1
                self.pass3b(l, self.xB, self.y_p if last else self.xA, self.y_s if last else self.xA)
        self.P.finish()
        return self.nc


def _fm(v, nch):
    v = np.asarray(v)
    lead = v.shape[:-1]
    t = v.reshape(lead + (nch, 128))
    t = np.moveaxis(t, -1, 0)
    t = np.moveaxis(t, -1, 1)
    return np.ascontiguousarray(t)


def host_inputs(inp, cfg, m, ncores, prompt_b, nsb=None):
    L = cfg["L"]
    f32 = np.float32
    shared = {}
    shared["c_ident"] = np.eye(128, dtype=f32)
    shared["w_in"] = np.ascontiguousarray(inp["w_in"], dtype=f32)
    shared["proj"] = np.ascontiguousarray(np.concatenate([inp["proj_ssm"], inp["proj_rwkv"], inp["proj_attn"]], axis=1), dtype=f32)
    shared["w_o"] = np.ascontiguousarray(inp["w_o"], dtype=f32)
    shared["ffn_w_up"] = np.ascontiguousarray(inp["ffn_w_up"], dtype=f32)
    shared["ffn_w_down"] = np.ascontiguousarray(inp["ffn_w_down"], dtype=f32)
    rows = [inp["ln_in_g"], inp["ln_in_b"]]
    for l in range(DEPTH):
        rows += [inp["ln1_g"][l], inp["ln1_b"][l], inp["ln2_g"][l], inp["ln2_b"][l]]
    shared["lnrows"] = np.ascontiguousarray(np.stack(rows), dtype=f32)
    cw = np.concatenate([inp["ffn_conv_w"], inp["ffn_conv_b"][:, None, :]], axis=1)
    shared["convp"] = np.ascontiguousarray(np.stack([np.moveaxis(_fm(cw[l], NFC), 2, 3)[:, :, :] for l in range(DEPTH)]), dtype=f32) \
        if False else np.ascontiguousarray(np.stack([_fm(cw[l], NFC) for l in range(DEPTH)]), dtype=f32)
    s5B = np.zeros((DEPTH, 2, 8, 128, 128), f32)
    s5C = np.zeros((DEPTH, 2, 8, 128, 128), f32)
    for l in range(DEPTH):
        for ri, (bm, cm) in enumerate(((inp["ssm_b_re"], inp["ssm_c_re"]), (inp["ssm_b_im"], inp["ssm_c_im"]))):
            for j in range(8):
                for gl in range(2):
                    g = 2 * j + gl
                    r0 = 32 * (j % 4) + 16 * gl
                    s5B[l, ri, j, r0:r0 + 16, gl * 64:(gl + 1) * 64] = bm[l, g].T
                    s5C[l, ri, j, gl * 64:(gl + 1) * 64, r0:r0 + 16] = cm[l, g].T
    shared["s5B"] = s5B
    shared["s5C"] = s5C
    shared["ssm_w_glu"] = np.ascontiguousarray(inp["ssm_w_glu"], dtype=f32)
    def qj(a):
        return np.ascontiguousarray(np.asarray(a).reshape(8, 2, 64).transpose(1, 2, 0).reshape(128, 8))
    shared["s5p"] = np.ascontiguousarray(np.stack([np.stack([qj(inp["ssm_a_re"][l]), qj(inp["ssm_a_im"][l]),
                                                             qj(np.repeat(inp["ssm_log_dt"][l][:, None], 64, 1))], 1) for l in range(DEPTH)]), dtype=f32)
    shared["s5cols"] = np.ascontiguousarray(np.stack([np.stack([_fm(inp["ssm_d"][l].reshape(-1), 2), _fm(inp["ssm_b_glu"][l], 2)], 1)
                                                      for l in range(DEPTH)]), dtype=f32)
    rc = np.zeros((DEPTH, 128, 21), f32)
    for l in range(DEPTH):
        rc[l, :, 0:7] = _fm(inp["rwkv_mu"][l], 7)
        for p, nm in enumerate(("rwkv_w0", "rwkv_a0", "rwkv_k_k", "rwkv_k_a", "rwkv_r_k", "rwkv_lnx_g", "rwkv_lnx_b")):
            rc[l, :, 7 + 2 * p:9 + 2 * p] = _fm(np.asarray(inp[nm][l]).reshape(-1), 2)
    shared["rwcols"] = rc
    shared["rwrows"] = np.ascontiguousarray(np.stack([r for l in range(DEPTH) for r in (inp["rwkv_lnx_g"][l], inp["rwkv_lnx_b"][l])]), dtype=f32)
    shared["lora"] = np.ascontiguousarray(np.stack([np.concatenate([inp["rwkv_w2"][l], inp["rwkv_a2"][l], inp["rwkv_g2"][l]], 0) for l in range(DEPTH)]), dtype=f32)
    ii = np.arange(128)
    su = (ii[:, None] < ii[None, :]).astype(f32)
    iu = (ii[:, None] <= ii[None, :]).astype(f32)
    slo = (ii[None, :] < ii[:, None]).astype(f32)
    shared["c_masks"] = np.ascontiguousarray(np.stack([su, iu, slo], 1))
    shared["c_blk"] = ((ii[:, None] // 64) == (ii[None, :] // 64)).astype(f32)
    shared["c_hsel"] = ((ii[:, None] // 64) == np.arange(2)[None, :]).astype(f32)
    NPG = cfg["NPG"]
    half = 8
    inv = (np.float32(ROPE_THETA) ** (-np.arange(half, dtype=f32) / np.float32(half))).astype(f32)
    def rope_tab(pos):
        ang = pos.astype(f32)[:, None] * inv[None, :]
        cs_, sn_ = np.cos(ang).astype(f32), np.sin(ang).astype(f32)
        C = np.ones((64, len(pos)), f32); S = np.zeros((64, len(pos)), f32)
        C[0:8] = cs_.T; C[8:16] = cs_.T; S[0:8] = sn_.T; S[8:16] = sn_.T
        return np.ascontiguousarray(np.concatenate([C, C], 0)), np.ascontiguousarray(np.concatenate([S, S], 0))
    shared["ropeC"], shared["ropeS"] = rope_tab(np.arange(L))
    shared["ropeCs"], shared["ropeSs"] = rope_tab(np.full((NS,), NPG * PAGE))
    rot = np.zeros((128, 128), f32)
    for b0 in (0, 64):
        for d_ in range(8):
            rot[b0 + d_ + 8, b0 + d_] = -1.0
            rot[b0 + d_, b0 + d_ + 8] = 1.0
    shared["c_rot"] = rot
    pk = np.arange(128)[:, None, None]; kt_ = np.arange(2)[None, :, None]; qq = (np.arange(512) % 256)[None, None, :]
    shared["c_causal"] = np.where(kt_ * 128 + pk <= qq, 1.0, 0.0).astype(f32)
    oh = np.zeros((32, 32, 128), f32)
    for n_ in range(32):
        oh[n_, n_, :] = 1.0
    shared["c_onehot"] = oh.reshape(32, 32 * 128)
    shared["c_pair"] = ((np.arange(128)[:, None] // 2) == np.arange(64)[None, :]).astype(f32) * np.float32(1.0 / 256)
    shared["c_pairT"] = ((np.arange(128)[None, :] // 2) == np.arange(64)[:, None]).astype(f32)
    npool = inp["cache_k"].shape[1]
    shared["ck_flat"] = np.ascontiguousarray(inp["cache_k"], dtype=f32).reshape(DEPTH * npool * 16, 2048)
    shared["cv_flat"] = np.ascontiguousarray(inp["cache_v"], dtype=f32).reshape(DEPTH * npool * 16, 2048)
    maps = []
    for c in range(ncores):
        d = dict(shared)
        sl = slice(c * NS, (c + 1) * NS)
        d["x_p"] = np.ascontiguousarray(inp["x_prompt"][prompt_b[c]], dtype=f32)
        d["x_s"] = np.ascontiguousarray(inp["x_sample"][sl, 0], dtype=f32)
        sre, sim = inp["state_ssm_re"][:, sl], inp["state_ssm_im"][:, sl]
        d["s5s0"] = np.ascontiguousarray(np.stack([np.stack([np.stack([qj(t[l, s_]) for s_ in range(NS)], -1) for t in (sre, sim)], 1)
                                                   for l in range(DEPTH)]), dtype=f32)
        d["rw_sh0"] = np.ascontiguousarray(np.stack([_fm(inp["state_rwkv_shift"][l, sl], 7) for l in range(DEPTH)]), dtype=f32)
        d["st_rwkv"] = np.ascontiguousarray(inp["state_rwkv"][:, sl].reshape(DEPTH, NS, 2, 128, 64), dtype=f32)
        d["ptT"] = np.ascontiguousarray(inp["page_table"][sl].T, dtype=np.int32)
        sc = inp["state_conv"][:, sl]
        d["conv0"] = np.ascontiguousarray(np.stack([_fm(sc[l], NFC) for l in range(DEPTH)]), dtype=f32)
        maps.append({k: v for k, v in d.items() if k in m.inputs})
        for k in m.inputs:
            if k in maps[-1]:
                assert tuple(maps[-1][k].shape) == m.inputs[k][0], (k, maps[-1][k].shape, m.inputs[k][0])
    return maps


def host_outputs(results, cfg, ncores, prompt_b):
    L = cfg["L"]
    nbp = len(set(prompt_b))
    first = {b: prompt_b.index(b) for b in set(prompt_b)}
    def P_(name, fn=lambda a: a):
        if name not in results[0]:
            return None
        return np.stack([fn(results[first[b]][name]) for b in range(nbp)])
    def S_(name, fn=lambda a: a):
        if name not in results[0]:
            return None
        return np.concatenate([fn(results[c][name]) for c in range(ncores)], axis=0)
    def unfm(a):
        a = np.moveaxis(a, 0, -1)
        a = np.moveaxis(a, 0, -2)
        return a.reshape(a.shape[:-2] + (-1,))
    out = [None] * 16
    def unqj(a):
        return a.reshape(2, 64, 8).transpose(2, 0, 1).reshape(16, 64)
    for idx, nm in ((6, "sre_p"), (7, "sim_p")):
        t = P_(nm, lambda a: np.stack([unqj(a[l]) for l in range(DEPTH)]))
        out[idx] = None if t is None else np.ascontiguousarray(np.moveaxis(t, 0, 1))
    for idx, nm in ((8, "sre_s"), (9, "sim_s")):
        if nm in results[0]:
            t = np.concatenate([np.stack([np.stack([unqj(results[c][nm][l][:, :, s_]) for s_ in range(NS)]) for l in range(DEPTH)])
                                for c in range(ncores)], axis=1)
            out[idx] = np.ascontiguousarray(t)
    t = P_("rw_p")
    if t is not None:
        B_ = t.shape[0]
        t = t.reshape(B_, DEPTH, 2, 64, 2, 64)
        out[10] = np.ascontiguousarray(t.transpose(1, 0, 4, 2, 5, 3).reshape(DEPTH, B_, 4, 64, 64))
    if "rw_s" in results[0]:
        out[11] = np.ascontiguousarray(np.concatenate([results[c]["rw_s"].reshape(DEPTH, NS, 4, 64, 64) for c in range(ncores)], axis=1))
    t = P_("sh_p", lambda a: np.stack([unfm(a[l]) for l in range(DEPTH)]))
    out[12] = None if t is None else np.ascontiguousarray(np.moveaxis(t, 0, 1))
    if "sh_s" in results[0]:
        out[13] = np.ascontiguousarray(np.concatenate([np.stack([unfm(results[c]["sh_s"][l]) for l in range(DEPTH)]) for c in range(ncores)], axis=1))
    t = P_("k_p")
    out[2] = None if t is None else np.ascontiguousarray(t.transpose(1, 0, 4, 2, 3))
    t = P_("v_p")
    out[3] = None if t is None else np.ascontiguousarray(np.moveaxis(t, 0, 1).reshape(DEPTH, t.shape[0], L, 4, 64))
    if "k_s" in results[0]:
        out[4] = np.ascontiguousarray(np.concatenate([results[c]["k_s"].transpose(0, 3, 1, 2) for c in range(ncores)], axis=1))[:, :, None]
        out[5] = np.ascontiguousarray(np.concatenate([results[c]["v_s"].reshape(DEPTH, NS, 4, 64) for c in range(ncores)], axis=1))[:, :, None]
    out[0] = P_("y_p")
    t = S_("y_s")
    out[1] = None if t is None else t[:, None, :]
    t = P_("cv_p", lambda a: np.stack([unfm(a[l]) for l in range(DEPTH)]))
    out[14] = None if t is None else np.ascontiguousarray(np.moveaxis(t, 0, 1))
    if "cv_s" in results[0]:
        t = np.concatenate([np.stack([unfm(results[c]["cv_s"][l]) for l in range(DEPTH)], 0) for c in range(ncores)], axis=1)
        out[15] = np.ascontiguousarray(t)
    return out


def _pass1a(self, l, xsrc, which):
    P = self.P
    L = self.L
    PS = self.PS
    TS = 256
    with ExitStack() as st:
        sb = lambda name, shape, dt=F32: P.sb("a_" + name, shape, dt, st)
        wuc = sb("wuc", [128, 8, 1152], BF16)
        for k in range(8):
            for c0 in (0, 576):
                P.dma("pool", wuc[:, k, c0:c0 + 576], self.w_in[l, k * 128:(k + 1) * 128, O_SSM + c0:O_SSM + c0 + 576])
        xt = sb("xt", [128, 4, D]); xT = sb("xT", [128, 8, 512], BF16)
        if which == "r":
            rw = self.rwkv_setup(l, st)
            for (t0, n) in self.groups:
                self.load_x(st, xsrc, t0, n, xt)
                self.make_xT(xt, n, xT)
                self.rwkv_group(l, rw, t0, n, xT, wuc)
            P.barrier()
            return
        WB = sb("WB", [128, 2, 8, 128], BF16)
        WC = sb("WC", [128, 2, 8, 128], BF16)
        for ri in range(2):
            P.dma("pool", WB[:, ri, :, :], self.s5B.v(self.s5B.t[l, ri, :, :, :].rearrange("j k m -> k j m")))
            P.dma("pool", WC[:, ri, :, :], self.s5C.v(self.s5C.t[l, ri, :, :, :].rearrange("j k m -> k j m")))
        wglu = sb("wglu", [128, 2, 256], BF16)
        for k in range(2):
            P.dma("pool", wglu[:, k, :], self.w_glu[l, k * 128:(k + 1) * 128, :])
        sp_ = sb("s5p", [128, 3, 8])
        P.dma("sp", sp_[:, :, :], self.s5p[l, :, :, :])
        cols = sb("s5cols", [128, 2, 2])
        P.dma("sp", cols[:, :, :], self.s5cols[l, :, :, :])
        s0 = sb("s5s0", [128, 2, 8, NS])
        P.dma("sp", s0[:, :, :, :], self.s5s0[l, :, :, :, :])
        dt_ = sb("dt", [128, 8]); ard = sb("ard", [128, 8]); th = sb("th", [128, 8])
        rho = sb("rho", [128, 8]); nq = sb("nq", [128, 8]); tq = sb("tq", [128, 8])
        cs = sb("cs", [128, 8]); sn = sb("sn", [128, 8]); abr = sb("abr", [128, 8]); abi = sb("abi", [128, 8])
        cfr = sb("cfr", [128, 8]); cfi = sb("cfi", [128, 8]); cfin = sb("cfin", [128, 8]); den = sb("den", [128, 8])
        t8 = [sb("t8_%d" % i, [128, 8]) for i in range(3)]
        are, aim = sp_[:, 0, :], sp_[:, 1, :]
        P.act(dt_[:, :], sp_[:, 2, :], AF.Exp)
        P.tt("dve", ard[:, :], are, dt_[:, :], ALU.mult)
        P.tt("dve", th[:, :], aim, dt_[:, :], ALU.mult)
        P.act(rho[:, :], ard[:, :], AF.Exp)
        P.memset("dve", nq[:, :], 0.0)
        for mlt in (1, 3, 5, 7, 9, 11):
            P.ts("dve", tq[:, :], th[:, :], float(mlt * math.pi), None, op0=ALU.is_gt)
            P.tt("dve", nq[:, :], nq[:, :], tq[:, :], ALU.add)
        C1 = 6.28125
        C2 = 2.0 * math.pi - C1
        P.stt(tq[:, :], nq[:, :], -C1, th[:, :], ALU.mult, ALU.add)
        P.stt(tq[:, :], nq[:, :], -C2, tq[:, :], ALU.mult, ALU.add)
        P.act(sn[:, :], tq[:, :], AF.Sin)
        P.ts("dve", t8[1][:, :], tq[:, :], -1.0, None, op0=ALU.mult)
        P.tt("dve", t8[0][:, :], tq[:, :], t8[1][:, :], ALU.max)
        P.ts("dve", t8[0][:, :], t8[0][:, :], -1.0, float(math.pi / 2), op0=ALU.mult, op1=ALU.add)
        P.act(cs[:, :], t8[0][:, :], AF.Sin)
        P.tt("dve", abr[:, :], rho[:, :], cs[:, :], ALU.mult)
        P.tt("dve", abi[:, :], rho[:, :], sn[:, :], ALU.mult)
        P.tt("dve", den[:, :], are, are, ALU.mult)
        P.tt("dve", t8[0][:, :], aim, aim, ALU.mult)
        P.tt("dve", den[:, :], den[:, :], t8[0][:, :], ALU.add)
        P.recip(den[:, :], den[:, :])
        P.ts("dve", t8[0][:, :], abr[:, :], -1.0, None, op0=ALU.add)
        P.tt("dve", t8[1][:, :], t8[0][:, :], are, ALU.mult)
        P.tt("dve", t8[2][:, :], abi[:, :], aim, ALU.mult)
        P.tt("dve", t8[1][:, :], t8[1][:, :], t8[2][:, :], ALU.add)
        P.tt("dve", cfr[:, :], t8[1][:, :], den[:, :], ALU.mult)
        P.tt("dve", t8[1][:, :], abi[:, :], are, ALU.mult)
        P.tt("dve", t8[2][:, :], t8[0][:, :], aim, ALU.mult)
        P.tt("dve", t8[1][:, :], t8[1][:, :], t8[2][:, :], ALU.subtract)
        P.tt("dve", cfi[:, :], t8[1][:, :], den[:, :], ALU.mult)
        P.ts("dve", cfin[:, :], cfi[:, :], -1.0, None, op0=ALU.mult)
        Ec = sb("Ec", [128, 8, TS]); Es = sb("Es", [128, 8, TS])
        tA = sb("tA", [128, 8, TS // 2]); tB = sb("tB", [128, 8, TS // 2])
        P.copy("dve", Ec[:, :, 0], cs[:, :])
        P.copy("dve", Es[:, :, 0], sn[:, :])
        s_ = 1
        while s_ < TS:
            cb = Ec.v(bc_last(Ec.t[:, :, s_ - 1], s_))
            sbb = Es.v(bc_last(Es.t[:, :, s_ - 1], s_))
            P.tt("dve", tA[:, :, 0:s_], Ec[:, :, 0:s_], cb, ALU.mult)
            P.tt("dve", tB[:, :, 0:s_], Es[:, :, 0:s_], sbb, ALU.mult)
            P.tt("dve", tA[:, :, 0:s_], tA[:, :, 0:s_], tB[:, :, 0:s_], ALU.subtract)
            P.tt("dve", tB[:, :, 0:s_], Ec[:, :, 0:s_], sbb, ALU.mult)
            P.tt("dve", Es[:, :, s_:2 * s_], Es[:, :, 0:s_], cb, ALU.mult)
            P.tt("dve", Es[:, :, s_:2 * s_], Es[:, :, s_:2 * s_], tB[:, :, 0:s_], ALU.add)
            P.copy("dve", Ec[:, :, s_:2 * s_], tA[:, :, 0:s_])
            s_ *= 2
        rhoT = sb("rhoT", [128, 8, TS])
        P.memset("dve", rhoT[:, :, :], 1.0)
        P.tt("dve", rhoT[:, :, :], rhoT[:, :, :], rho.v(bc_last(rho.t[:, :], TS)), ALU.mult)
        carry = sb("s5carry", [128, 2, 8])
        P.memset("dve", carry[:, :, :], 0.0)

        uTb = sb("uTb", [128, 2, 512], BF16); uTf = sb("uTf", [128, 2, 512])
        dbl = {}
        for nm in ("xr", "xi", "tm1", "tm2", "hr", "hi", "zr", "zi", "sr", "si"):
            dbl[nm] = [sb(nm + str(i), [128, 512]) for i in range(2)]
        for nm in ("srb", "sib"):
            dbl[nm] = [sb(nm + str(i), [128, 512], BF16) for i in range(2)]
        yv = sb("yv", [128, 2, 512]); zf = sb("zf", [128, 2, 512]); zb = sb("zb", [128, 2, 512], BF16)
        sg = sb("sg", [128, 512]); ysb = sb("ysb", [128, 2, 512], BF16)

        def s5_group(t0, n):
            samp = n < 128
            for yc in range(2):
                ps = PS[2 + yc]
                for k in range(8):
                    P.mm(ps[:, 0:n], wuc[:, k, yc * 128:(yc + 1) * 128], xT[:, k, 0:n], k == 0, k == 7)
                P.copy("act", uTb[:, yc, 0:n], ps[:, 0:n])
                P.copy("dve", uTf[:, yc, 0:n], ps[:, 0:n])
            for j in range(8):
                psA, psB = PS[4], PS[5]
                xr, xi, tm1, tm2, hr, hi, zr, zi, sr, si, srb, sib = [dbl[nm][j % 2] for nm in
                                                                       ("xr", "xi", "tm1", "tm2", "hr", "hi", "zr", "zi", "sr", "si", "srb", "sib")]
                P.mm(psA[:, 0:n], WB[:, 0, j, :], uTb[:, j // 4, 0:n])
                P.mm(psB[:, 0:n], WB[:, 1, j, :], uTb[:, j // 4, 0:n])
                P.act(tm1[:, 0:n], psB[:, 0:n], AF.Identity, scale=cfin[:, j:j + 1])
                P.act(tm2[:, 0:n], psA[:, 0:n], AF.Identity, scale=cfi[:, j:j + 1])
                P.stt(xr[:, 0:n], psA[:, 0:n], cfr[:, j:j + 1], tm1[:, 0:n], ALU.mult, ALU.add)
                P.stt(xi[:, 0:n], psB[:, 0:n], cfr[:, j:j + 1], tm2[:, 0:n], ALU.mult, ALU.add)
                if samp:
                    P.stt(xr[:, 0:n], s0[:, 0, j, :], abr[:, j:j + 1], xr[:, 0:n], ALU.mult, ALU.add)
                    P.ts("dve", tm1[:, 0:n], s0[:, 1, j, :], abi[:, j:j + 1], -1.0, op0=ALU.mult, op1=ALU.mult)
                    P.tt("dve", sr[:, 0:n], xr[:, 0:n], tm1[:, 0:n], ALU.add)
                    P.stt(xi[:, 0:n], s0[:, 1, j, :], abr[:, j:j + 1], xi[:, 0:n], ALU.mult, ALU.add)
                    P.stt(si[:, 0:n], s0[:, 0, j, :], abi[:, j:j + 1], xi[:, 0:n], ALU.mult, ALU.add)
                    P.dma("sp", self.sre_s[l, :, j, :], sr[:, 0:n])
                    P.dma("sp", self.sim_s[l, :, j, :], si[:, 0:n])
                else:
                    for sg_ in range(n // TS):
                        c_ = slice(sg_ * TS, (sg_ + 1) * TS)
                        ec, es = Ec[:, j, :], Es[:, j, :]
                        P.tt("pool", hr[:, c_], xr[:, c_], ec, ALU.mult)
                        P.tt("pool", tm1[:, c_], xi[:, c_], es, ALU.mult)
                        P.tt("pool", hr[:, c_], hr[:, c_], tm1[:, c_], ALU.add)
                        P.tt("pool", hi[:, c_], xi[:, c_], ec, ALU.mult)
                        P.tt("pool", tm2[:, c_], xr[:, c_], es, ALU.mult)
                        P.tt("pool", hi[:, c_], hi[:, c_], tm2[:, c_], ALU.subtract)
                        P.scan(zr[:, c_], rhoT[:, j, :], hr[:, c_], carry[:, 0, j:j + 1], ALU.mult, ALU.add)
                        P.scan(zi[:, c_], rhoT[:, j, :], hi[:, c_], carry[:, 1, j:j + 1], ALU.mult, ALU.add)
                        P.tt("dve", sr[:, c_], zr[:, c_], ec, ALU.mult)
                        P.tt("pool", tm1[:, c_], zi[:, c_], es, ALU.mult)
                        P.tt("dve", sr[:, c_], sr[:, c_], tm1[:, c_], ALU.subtract)
                        P.tt("dve", si[:, c_], zr[:, c_], es, ALU.mult)
                        P.tt("pool", tm2[:, c_], zi[:, c_], ec, ALU.mult)
                        P.tt("dve", si[:, c_], si[:, c_], tm2[:, c_], ALU.add)
                        P.copy("act", carry[:, 0, j:j + 1], sr[:, (sg_ + 1) * TS - 1:(sg_ + 1) * TS])
                        P.copy("act", carry[:, 1, j:j + 1], si[:, (sg_ + 1) * TS - 1:(sg_ + 1) * TS])
                P.copy("act", srb[:, 0:n], sr[:, 0:n])
                P.act(sib[:, 0:n], si[:, 0:n], AF.Copy, scale=-1.0)
                psy = PS[6 + j // 4]
                P.mm(psy[:, 0:n], WC[:, 0, j, :], srb[:, 0:n], j % 4 == 0, False)
                P.mm(psy[:, 0:n], WC[:, 1, j, :], sib[:, 0:n], False, j % 4 == 3)
            for yc in range(2):
                P.stt(yv[:, yc, 0:n], uTf[:, yc, 0:n], cols[:, 0, yc:yc + 1], PS[6 + yc][:, 0:n], ALU.mult, ALU.add)
                P.act(zf[:, yc, 0:n], yv[:, yc, 0:n], AF.Gelu_apprx_tanh)
                P.copy("dve", zb[:, yc, 0:n], zf[:, yc, 0:n])
            for yc in range(2):
                ps = PS[2 + yc]
                for k in range(2):
                    P.mm(ps[:, 0:n], wglu[:, k, yc * 128:(yc + 1) * 128], zb[:, k, 0:n], k == 0, k == 1)
                P.act(sg[:, 0:n], ps[:, 0:n], AF.Sigmoid, bias=cols[:, 1, yc:yc + 1])
                P.tt("dve", ysb[:, yc, 0:n], zf[:, yc, 0:n], sg[:, 0:n], ALU.mult)
                P.dma("sp", self.yT[l].part(("g", t0, yc), (yc, slice(None), slice(t0, t0 + n))), ysb[:, yc, 0:n])

        for (t0, n) in self.groups:
            self.load_x(st, xsrc, t0, n, xt)
            self.make_xT(xt, n, xT)
            s5_group(t0, n)
            if t0 + n == L:
                P.dma("sp", self.sre_p[l, :, :], carry[:, 0, :])
                P.dma("sp", self.sim_p[l, :, :], carry[:, 1, :])
    P.barrier()


Model.pass1a = _pass1a


def _rwkv_setup(self, l, st):
    P = self.P
    rw = Ctx()
    sb = lambda name, shape, dt=F32: P.sb("r_" + name, shape, dt, st)
    rw.sb = sb
    rw.cols = sb("cols", [128, 21])
    P.dma("sp", rw.cols[:, :], self.rwcols[l, :, :])
    col = lambda p, hc: rw.cols[:, 7 + 2 * p + hc:8 + 2 * p + hc]
    rw.col = col
    rw.negw0 = sb("negw0", [128, 2])
    P.ts("dve", rw.negw0[:, :], rw.cols[:, 7:9], -1.0, None, op0=ALU.mult)
    rw.omka = sb("omka", [128, 2])
    P.ts("dve", rw.omka[:, :], rw.cols[:, 13:15], -1.0, 1.0, op0=ALU.mult, op1=ALU.add)
    rw.nhalf = sb("nhalf", [128, 1]); P.memset("dve", rw.nhalf[:, :], -0.5)
    rw.one = sb("one", [128, 1]); P.memset("dve", rw.one[:, :], 1.0)
    rw.lora = sb("lora", [128, 256], BF16)
    P.dma("pool", rw.lora[:, :], self.lora[l, :, :])
    rw.gnB = sb("gnB", [128, 256]); rw.bnB = sb("bnB", [128, 256])
    self.bcast_row_load(rw.gnB, self.rwrows, (l * 2) * 256, 256)
    self.bcast_row_load(rw.bnB, self.rwrows, (l * 2 + 1) * 256, 256)
    rw.masks = sb("masks", [128, 3, 128])
    P.dma("sp", rw.masks[:, :, :], self.c_masks[:, :, :])
    rw.blk = sb("blk", [128, 128]); P.dma("sp", rw.blk[:, :], self.c_blk[:, :])
    rw.hsel = sb("hsel", [128, 2]); P.dma("sp", rw.hsel[:, :], self.c_hsel[:, :])
    rw.ones = sb("ones", [128, 128]); P.memset("dve", rw.ones[:, :], 1.0)
    rw.cfull = sb("cfull", [128, 7, 513]); P.memset("dve", rw.cfull[:, :, :], 0.0)
    rw.H32 = sb("H32", [128, 2, 64]); P.memset("dve", rw.H32[:, :, :], 0.0)
    rw.Hb = sb("Hb", [128, 2, 64], BF16); P.memset("dve", rw.Hb[:, :, :], 0.0)
    rw.sh0 = sb("sh0", [128, 7, NS]); P.dma("sp", rw.sh0[:, :, :], self.rw_sh0[l, :, :, :])
    rw.cf = sb("cf", [128, 7, 512])
    rw.dtmp = sb("dtmp", [128, 512])
    rw.lo = sb("lo", [128, 512], BF16)
    for nm in ("e1", "e2", "cl", "clx", "kk", "sq", "rs", "tt1"):
        setattr(rw, nm, sb(nm, [128, 512]))
    for nm in ("eg", "egx", "egi", "a", "kkn", "km", "A32", "B32", "pr"):
        setattr(rw, nm, sb(nm, [128, 2, 512]))
    rw.ARb = sb("ARb", [128, 2, 4, 2, 128], BF16)
    rw.Bb = sb("Bb", [128, 2, 512], BF16); rw.Kb = sb("Kb", [128, 2, 512], BF16); rw.vb = sb("vb", [128, 2, 512], BF16)
    rw.Btm = sb("Btm", [128, 2, 4, 128], BF16); rw.Ktm = sb("Ktm", [128, 2, 4, 128], BF16); rw.Vtm = sb("Vtm", [128, 2, 4, 128], BF16)
    rw.X = [sb("X%d" % h, [128, 128]) for h in range(4)]
    rw.XT = [sb("XT%d" % h, [128, 128]) for h in range(4)]
    rw.N = [sb("N%d" % h, [128, 128]) for h in range(4)]
    rw.ArbT = [sb("ArbT%d" % h, [128, 128], BF16) for h in range(4)]
    rw.AkT = [sb("AkT%d" % h, [128, 256], BF16) for h in range(4)]
    rw.W32 = [sb("W32%d" % h, [128, 64]) for h in range(4)]
    rw.Ub = [sb("Ub%d" % h, [128, 64], BF16) for h in range(4)]
    rw.bst = sb("bst", [128, 24]); rw.bmv = sb("bmv", [128, 4, 2]); rw.sd = sb("sd", [128, 4]); rw.rstd = sb("rstd", [128, 4])
    rw.yn = sb("yn", [128, 256]); rw.yr = sb("yr", [128, 256]); rw.bon = sb("bon", [128, 16])
    rw.yrT = sb("yrT", [128, 2, 512], BF16)
    rw.rows = sb("rows", [128, 5, 64]); rw.S = sb("S", [128, 64]); rw.S1 = sb("S1", [128, 64]); rw.tS = sb("tS", [128, 64])
    rw.sa = sb("sa", [128, 1]); rw.ys = sb("ys", [128, 2, NS]); rw.vec = sb("vec", [128, 5, 2, NS])
    return rw


def _rwkv_group(self, l, rw, t0, n, xT, wuc):
    P = self.P
    PS = self.PS
    L = self.L
    samp = n < 128
    cf, cfull, lo = rw.cf, rw.cfull, rw.lo
    col = rw.col
    for ch in range(7):
        ps = PS[2 + ch % 2]
        for k in range(8):
            P.mm(ps[:, 0:n], wuc[:, k, 256 + ch * 128:256 + (ch + 1) * 128], xT[:, k, 0:n], k == 0, k == 7)
        if samp:
            P.copy("act", cfull[:, ch, 1:1 + n], ps[:, 0:n])
            P.tt("dve", rw.dtmp[:, 0:n], rw.sh0[:, ch, :], cfull[:, ch, 1:1 + n], ALU.subtract)
        else:
            P.copy("act", cfull[:, ch, 1:1 + n], ps[:, 0:n])
            P.tt("pool", rw.dtmp[:, 0:n], cfull[:, ch, 0:n], cfull[:, ch, 1:1 + n], ALU.subtract)
        P.stt(cf[:, ch, 0:n], rw.dtmp[:, 0:n], rw.cols[:, ch:ch + 1], cfull[:, ch, 1:1 + n], ALU.mult, ALU.add)
    if samp:
        P.dma("sp", self.sh_s[l, :, :, :], cfull[:, :, 1:1 + n])
    else:
        P.copy("act", cfull[:, :, 0:1], cfull[:, :, n:n + 1])
        if t0 + n == L:
            P.copy("act", rw.bon[:, 0:7], cfull[:, :, 0])
            P.dma("sp", self.sh_p[l, :, :], rw.bon[:, 0:7])
    P.act(lo[0:32, 0:n], cf[0:32, 6, 0:n], AF.Tanh)
    P.copy("act", lo[32:64, 0:n], cf[32:64, 6, 0:n])
    P.act(lo[64:128, 0:n], cf[64:128, 6, 0:n], AF.Sigmoid)
    for hc in range(2):
        hs = slice(hc * 128, (hc + 1) * 128)
        r_, k_, v_ = cf[:, hc, 0:n], cf[:, 2 + hc, 0:n], cf[:, 4 + hc, 0:n]
        psw = PS[2]
        P.mm(psw[:, 0:n], rw.lora[0:32, hs], lo[0:32, 0:n])
        P.act(rw.e1[:, 0:n], psw[:, 0:n], AF.Exp, bias=rw.negw0[:, hc:hc + 1], scale=-1.0)
        P.act(rw.e1[:, 0:n], rw.e1[:, 0:n], AF.Ln, bias=rw.one[:, 0:1])
        P.act(rw.e2[:, 0:n], rw.e1[:, 0:n], AF.Exp, bias=rw.nhalf[:, 0:1], scale=-1.0)
        psa = PS[3]
        P.mm(psa[:, 0:n], rw.lora[32:64, hs], lo[32:64, 0:n])
        P.act(rw.a[:, hc, 0:n], psa[:, 0:n], AF.Sigmoid, bias=col(1, hc))
        P.ts("pool", rw.kk[:, 0:n], k_, col(2, hc), None, op0=ALU.mult)
        P.tt("pool", rw.sq[:, 0:n], rw.kk[:, 0:n], rw.kk[:, 0:n], ALU.mult)
        pss = PS[4]
        P.mm(pss[:, 0:n], rw.blk[:, :], rw.sq[:, 0:n])
        P.ts("dve", rw.rs[:, 0:n], pss[:, 0:n], 1e-24, None, op0=ALU.max)
        P.act(rw.rs[:, 0:n], rw.rs[:, 0:n], AF.Sqrt)
        P.recip(rw.rs[:, 0:n], rw.rs[:, 0:n])
        P.tt("dve", rw.kkn[:, hc, 0:n], rw.kk[:, 0:n], rw.rs[:, 0:n], ALU.mult)
        P.ts("dve", rw.tt1[:, 0:n], rw.a[:, hc, 0:n], col(3, hc), rw.omka[:, hc:hc + 1], op0=ALU.mult, op1=ALU.add)
        P.tt("dve", rw.km[:, hc, 0:n], k_, rw.tt1[:, 0:n], ALU.mult)
        P.stt(rw.pr[:, hc, 0:n], r_, col(4, hc), rw.km[:, hc, 0:n], ALU.mult, ALU.mult)
        if samp:
            P.act(rw.vec[:, 0, hc, :], rw.e2[:, 0:n], AF.Exp, scale=-1.0)
            P.ts("dve", rw.vec[:, 1, hc, :], rw.kkn[:, hc, 0:n], -1.0, None, op0=ALU.mult)
            P.tt("dve", rw.vec[:, 2, hc, :], rw.kkn[:, hc, 0:n], rw.a[:, hc, 0:n], ALU.mult)
            P.copy("dve", rw.vec[:, 3, hc, :], rw.km[:, hc, 0:n])
            P.copy("dve", rw.vec[:, 4, hc, :], r_)
            continue
        for c in range(4):
            cs = slice(c * 128, (c + 1) * 128)
            P.scan(rw.cl[:, cs], rw.ones[:, :], rw.e2[:, cs], 0.0, ALU.mult, ALU.add)
        P.tt("pool", rw.clx[:, 0:n], rw.cl[:, 0:n], rw.e2[:, 0:n], ALU.subtract)
        P.act(rw.eg[:, hc, 0:n], rw.cl[:, 0:n], AF.Exp, scale=-1.0)
        P.act(rw.egx[:, hc, 0:n], rw.clx[:, 0:n], AF.Exp, scale=-1.0)
        P.act(rw.egi[:, hc, 0:n], rw.cl[:, 0:n], AF.Exp)
        P.stt(rw.A32[:, hc, 0:n], rw.kkn[:, hc, 0:n], -1.0, rw.egx[:, hc, 0:n], ALU.mult, ALU.mult)
        P.tt("pool", rw.tt1[:, 0:n], rw.kkn[:, hc, 0:n], rw.a[:, hc, 0:n], ALU.mult)
        P.tt("pool", rw.B32[:, hc, 0:n], rw.tt1[:, 0:n], rw.egi[:, hc, 0:n], ALU.mult)
        a3 = lambda buf, idx: buf.v(buf.t[:, hc, idx, :, :]) if False else None
        P.copy("act", rw.ARb.v(rw.ARb.t[:, hc, :, 0, :]), rw.A32.v(rw.A32.t[:, hc, :].rearrange("p (c t) -> p c t", c=4)))
        P.tt("dve", rw.ARb.v(rw.ARb.t[:, hc, :, 1, :]), cf.v(cf.t[:, hc, :].rearrange("p (c t) -> p c t", c=4)),
             rw.eg.v(rw.eg.t[:, hc, :].rearrange("p (c t) -> p c t", c=4)), ALU.mult)
        P.copy("act", rw.Bb[:, hc, 0:n], rw.B32[:, hc, 0:n])
        P.tt("dve", rw.Kb[:, hc, 0:n], rw.km[:, hc, 0:n], rw.egi[:, hc, 0:n], ALU.mult)
        P.copy("act", rw.vb[:, hc, 0:n], v_)
        psb = PS[7].v(PS[7].t[:, :].bitcast(BF16))
        for src, dst in ((rw.Bb, rw.Btm), (rw.Kb, rw.Ktm), (rw.vb, rw.Vtm)):
            for c in range(4):
                P.tr(psb[:, c * 128:(c + 1) * 128], src[:, hc, c * 128:(c + 1) * 128], self.identb[:, :])
            P.copy("dve", dst.v(dst.t[:, hc, :, :].rearrange("p c t -> p (c t)")), psb[:, 0:512])
        for c in range(4):
            o = (hc * 4 + c) * 2
            P.mm(PS[5][:, o:o + 2], rw.pr[:, hc, c * 128:(c + 1) * 128], rw.hsel[:, :])
    if samp:
        return self.rwkv_sample(l, rw, n)
    P.copy("act", rw.bon[:, :], PS[5][:, 0:16])
    mask2 = rw.masks.v(rw.masks.t[:, 0:2, :].rearrange("p a t -> p (a t)"))
    for c in range(4):
        cs = slice(c * 128, (c + 1) * 128)
        for h in range(4):
            hc, hl = divmod(h, 2)
            sl = slice(hl * 64, hl * 64 + 64)
            ps1, ps2, ps3, ps4 = PS[0], PS[1], PS[2], PS[3]
            P.mm(ps1[:, 0:128], rw.B32[sl, hc, cs], rw.A32[sl, hc, cs])
            P.tt("dve", rw.X[h][:, :], ps1[:, 0:128], rw.masks[:, 0, :], ALU.mult)
            P.mm(ps2[:, 0:128], rw.A32[sl, hc, cs], rw.B32[sl, hc, cs])
            P.tt("dve", rw.XT[h][:, :], ps2[:, 0:128], rw.masks[:, 2, :], ALU.mult)
            P.tt("pool", rw.N[h][:, :], rw.X[h][:, :], self.ident[:, :], ALU.add)
            P.mm(ps3[:, 0:128], rw.Bb[sl, hc, cs], rw.ARb[sl, hc, c, 1, :])
            P.tt("dve", rw.ArbT[h][:, :], ps3[:, 0:128], rw.masks[:, 1, :], ALU.mult)
            P.mm(ps4[:, 0:256], rw.Kb[sl, hc, cs], rw.ARb.v(rw.ARb.t[sl, hc, c, :, :].rearrange("p a t -> p (a t)")))
            P.tt("dve", rw.AkT[h][:, :], ps4[:, 0:256], mask2, ALU.mult)
        for k in range(1, 7):
            for h in range(4):
                pa, pb, pc = PS[(3 * h) % 4], PS[(3 * h + 1) % 4], PS[(3 * h + 2) % 4]
                if k < 6:
                    P.mm(pa[:, 0:128], rw.XT[h][:, :], rw.X[h][:, :])
                P.mm(pb[:, 0:128], rw.X[h][:, :], rw.XT[h][:, :])
                if k < 6:
                    P.copy("act", rw.X[h][:, :], pa[:, 0:128])
                P.copy("dve", rw.XT[h][:, :], pb[:, 0:128])
                P.mm(pc[:, 0:128], rw.XT[h][:, :], rw.N[h][:, :])
                P.tt("dve", rw.N[h][:, :], rw.N[h][:, :], pc[:, 0:128], ALU.add)
        for h in range(4):
            hc, hl = divmod(h, 2)
            sl = slice(hl * 64, hl * 64 + 64)
            vtm = rw.Vtm[:, hc, c, hl * 64:(hl + 1) * 64]
            pw, pu = PS[4], PS[0]
            P.mm(pw[:, 0:64], rw.ARb[sl, hc, c, 0, :], rw.Hb[sl, hc, :], True, False)
            P.mm(pw[:, 0:64], rw.AkT[h][:, 0:128], vtm, False, True)
            P.copy("act", rw.W32[h][:, :], pw[:, 0:64])
            P.mm(pu[:, 0:64], rw.N[h][:, :], rw.W32[h][:, :])
            P.copy("act", rw.Ub[h][:, :], pu[:, 0:64])
            py = PS[6]
            P.mm(py[:, h * 64:(h + 1) * 64], rw.ARb[sl, hc, c, 1, :], rw.Hb[sl, hc, :], True, False)
            P.mm(py[:, h * 64:(h + 1) * 64], rw.ArbT[h][:, :], rw.Ub[h][:, :], False, False)
            P.mm(py[:, h * 64:(h + 1) * 64], rw.AkT[h][:, 128:256], vtm, False, True)
        for hc in range(2):
            ph = PS[1]
            for hl in range(2):
                h = 2 * hc + hl
                vtm = rw.Vtm[:, hc, c, hl * 64:(hl + 1) * 64]
                P.mm(ph[hl * 64:(hl + 1) * 64, 0:64], rw.Btm[:, hc, c, hl * 64:(hl + 1) * 64], rw.Ub[h][:, :], True, False)
                P.mm(ph[hl * 64:(hl + 1) * 64, 0:64], rw.Ktm[:, hc, c, hl * 64:(hl + 1) * 64], vtm, False, True)
            gT = rw.eg[:, hc, c * 128 + 127:c * 128 + 128]
            P.ts("dve", rw.H32[:, hc, :], rw.H32[:, hc, :], gT, None, op0=ALU.mult)
            P.stt(rw.H32[:, hc, :], ph[:, 0:64], gT, rw.H32[:, hc, :], ALU.mult, ALU.add)
            P.copy("act", rw.Hb[:, hc, :], rw.H32[:, hc, :])
        py = PS[6]
        for h in range(4):
            P.emit("dve", lambda e, h=h: e.bn_stats(rw.bst.t[:, h * 6:(h + 1) * 6], py.t[:, h * 64:(h + 1) * 64]), [py[:, :]], [rw.bst[:, :]])
            P.emit("dve", lambda e, h=h: e.bn_aggr(rw.bmv.t[:, h, :], rw.bst.t[:, h * 6:(h + 1) * 6]), [rw.bst[:, :]], [rw.bmv[:, :, :]])
        P.act(rw.sd[:, :], rw.bmv[:, :, 1], AF.Sqrt, bias=self.epsc[GN_EPS][:, 0:1])
        P.recip(rw.rstd[:, :], rw.sd[:, :])
        for h in range(4):
            P.ts("dve", rw.yn[:, h * 64:(h + 1) * 64], py[:, h * 64:(h + 1) * 64], rw.bmv[:, h, 0:1], rw.rstd[:, h:h + 1], op0=ALU.subtract, op1=ALU.mult)
        P.tt("pool", rw.yn[:, :], rw.yn[:, :], rw.gnB[:, :], ALU.mult)
        P.tt("pool", rw.yn[:, :], rw.yn[:, :], rw.bnB[:, :], ALU.add)
        for h in range(4):
            hc, hl = divmod(h, 2)
            o = (hc * 4 + c) * 2 + hl
            P.stt(rw.yn[:, h * 64:(h + 1) * 64], rw.Vtm[:, hc, c, hl * 64:(hl + 1) * 64], rw.bon[:, o:o + 1], rw.yn[:, h * 64:(h + 1) * 64], ALU.mult, ALU.add)
        pg = PS[4]
        P.mm(pg[:, 0:256], lo[64:128, cs], rw.lora[64:128, :])
        P.tt("dve", rw.yr[:, :], rw.yn[:, :], pg[:, 0:256], ALU.mult)
        for hc in range(2):
            pt = PS[5 + 2 * hc]
            P.tr(pt[:, cs], rw.yr[:, hc * 128:(hc + 1) * 128], self.ident[:, :])
    for hc in range(2):
        P.copy("act", rw.yrT[:, hc, 0:n], PS[5 + 2 * hc][:, 0:n])
        P.dma("sp", self.yT[l].part(("g", t0, 2 + hc), (2 + hc, slice(None), slice(t0, t0 + n))), rw.yrT[:, hc, 0:n])
    if t0 + n == L:
        P.dma("sp", self.rw_p[l, :, :, :], rw.H32.v(rw.H32.t[:, :, :].rearrange("p a b -> p a b")) if False else rw.H32[:, :, :])


def _rwkv_sample(self, l, rw, n):
    P = self.P
    PS = self.PS
    L = self.L
    cf, lo, col = rw.cf, rw.lo, rw.col
    scr = self.rwscr[l]
    P.dma("sp", scr.v(scr.t[:, :, :, :].rearrange("v c s q -> q (v c s)")), rw.vec.v(rw.vec.t[:, :, :, :].rearrange("q v c s -> q (v c s)")), allow_slow_non_contiguous=True)
    for s_ in range(n):
        for hc in range(2):
            for hl in range(2):
                off = ((hc * NS + s_) * 128 + hl * 64)
                ap = bass.AP(scr.t, off, [[0, 64], [NS * 2 * 128, 5], [1, 64]])
                P.dma("sp", rw.rows[hl * 64:(hl + 1) * 64, :, :], scr.v(ap))
            P.dma("sp", rw.S[:, :], self.st_rwkv[l, s_, hc, :, :])
            P.tt("dve", rw.tS[:, :], rw.S[:, :], rw.rows[:, 1, :], ALU.mult)
            P.reduce(rw.sa[:, :], rw.tS[:, :], ALU.add)
            P.tt("dve", rw.S1[:, :], rw.S[:, :], rw.rows[:, 0, :], ALU.mult)
            P.stt(rw.S1[:, :], rw.rows[:, 2, :], rw.sa[:, 0:1], rw.S1[:, :], ALU.mult, ALU.add)
            P.stt(rw.S1[:, :], rw.rows[:, 3, :], cf[:, 4 + hc, s_:s_ + 1], rw.S1[:, :], ALU.mult, ALU.add)
            P.tt("dve", rw.tS[:, :], rw.S1[:, :], rw.rows[:, 4, :], ALU.mult)
            P.reduce(rw.ys[:, hc, s_:s_ + 1], rw.tS[:, :], ALU.add)
            P.dma("sp", self.rw_s[l, s_, hc, :, :], rw.S1[:, :])
    t = [rw.sb("st%d" % i, [128, NS]) for i in range(4)]
    for hc in range(2):
        hs = slice(hc * 128, (hc + 1) * 128)
        y = rw.ys[:, hc, :]
        pm = PS[2]
        P.mm(pm[:, 0:n], rw.blk[:, :], y)
        P.stt(t[0][:, :], pm[:, 0:n], -1.0 / 64, y, ALU.mult, ALU.add)
        P.tt("dve", t[1][:, :], t[0][:, :], t[0][:, :], ALU.mult)
        pv = PS[3]
        P.mm(pv[:, 0:n], rw.blk[:, :], t[1][:, :])
        P.act(t[1][:, :], pv[:, 0:n], AF.Sqrt, bias=self.epsc[GN_EPS][:, 0:1], scale=1.0 / 64)
        P.recip(t[1][:, :], t[1][:, :])
        P.tt("dve", t[0][:, :], t[0][:, :], t[1][:, :], ALU.mult)
        P.ts("dve", t[0][:, :], t[0][:, :], col(5, hc), col(6, hc), op0=ALU.mult, op1=ALU.add)
        pb = PS[4]
        P.mm(pb[:, 0:n], rw.blk[:, :], rw.pr[:, hc, 0:n])
        P.tt("dve", t[2][:, :], pb[:, 0:n], cf[:, 4 + hc, 0:n], ALU.mult)
        P.tt("dve", t[0][:, :], t[0][:, :], t[2][:, :], ALU.add)
        pg = PS[5]
        P.mm(pg[:, 0:n], rw.lora[64:128, hs], lo[64:128, 0:n])
        P.tt("dve", rw.yrT[:, hc, 0:n], t[0][:, :], pg[:, 0:n], ALU.mult)
        P.dma("sp", self.yT[l].part(("g", L, 2 + hc), (2 + hc, slice(None), slice(L, L + n))), rw.yrT[:, hc, 0:n])


Model.rwkv_setup = _rwkv_setup
Model.rwkv_group = _rwkv_group
Model.rwkv_sample = _rwkv_sample


def _pass1b(self, l, xsrc):
    P = self.P
    L, NPG = self.L, self.NPG
    PS = self.PS
    NBLK = L // 256
    with ExitStack() as st:
        sb = lambda name, shape, dt=F32: P.sb("m_" + name, shape, dt, st)
        wq = sb("wq", [128, 8, 512], BF16); wkd = sb("wkd", [128, 8, 4, 128], BF16); wv = sb("wv", [128, 8, 256], BF16)
        for k in range(8):
            rs_ = slice(k * 128, (k + 1) * 128)
            P.dma("pool", wq[:, k, :], self.w_in[l, rs_, O_Q:O_K])
            P.dma("pool", wv[:, k, :], self.w_in[l, rs_, O_V:N_IN])
            for g in range(4):
                for hl in range(2):
                    P.dma("pool", wkd[:, k, g, hl * 64:(hl + 1) * 64], self.w_in[l, rs_, O_K + g * 64:O_K + (g + 1) * 64])
        rot = sb("rot", [128, 128]); P.dma("sp", rot[:, :], self.c_rot[:, :])
        caus = sb("caus", [128, 2, 512], BF16); P.dma("pool", caus[:, :, :], self.c_causal[:, :, :])
        oneh = sb("oneh", [128, 32 * 128], BF16)
        P.memset("dve", oneh[:, :], 0.0)
        for c0 in range(0, 4096, 1024):
            P.dma("pool", oneh[0:32, c0:c0 + 1024], self.c_onehot[:, c0:c0 + 1024])
        ones32 = sb("ones32", [128, 64]); P.memset("dve", ones32[:, :], 1.0)
        KM = sb("KM", [128, 4, 32]); P.memset("dve", KM[:, :, :], 0.0)
        xt = sb("xt", [128, 4, D]); xT = sb("xT", [128, 8, 512], BF16)
        rC = sb("rC", [128, 512]); rS = sb("rS", [128, 512])
        xs = [sb("xs%d" % i, [128, 512]) for i in range(1)] * 2
        t1 = [sb("t1%d" % i, [128, 512]) for i in range(1)] * 2
        qr32 = sb("qr32", [128, 4, 512])
        Qblk = sb("Qblk", [128, 4, 2, 512], BF16); P.memset("dve", Qblk[:, :, :, :], 0.0)
        mskb = [sb("mskb%d" % i, [128, 512], BF16) for i in range(2)]
        kr32 = [sb("kr32%d" % i, [128, 512]) for i in range(2)]
        vtm = [sb("vtm%d" % i, [128, 256]) for i in range(2)]
        gsb = sb("gsb", [128, 4, 32]); P.memset("dve", gsb[:, :, :], -1e30)
        m8 = sb("m8", [128, 4, 8]); neg = sb("neg", [128, 4, 32]); negT = sb("negT", [128, 512], BF16); P.memset("dve", negT[:, :], 0.0)
        pt = [sb("pt%d" % i, [128, 512], BF16) for i in range(3)]
        srow = sb("srow", [128, 512]); bcs = sb("bcs", [64, 512]); ya = sb("ya", [64, 512], BF16)
        knew = sb("knew", [128, 4, NS])
        stp = ExitStack()
        KT = P.sb("m_KT", [128, 4, L], BF16, stp)
        Vaug = P.sb("m_Vaug", [128, L // 128, 4, 65], BF16, stp); P.memset("dve", Vaug[:, :, :, :], 1.0)

        def rope(ps, n, out32, tabC, tabS, i):
            x_, t_ = xs[i % 2], t1[i % 2]
            P.copy("act", x_[:, 0:n], ps[:, 0:n])
            pr = PS[4 + i % 2]
            P.mm(pr[:, 0:n], rot[:, :], x_[:, 0:n])
            P.tt("pool", t_[:, 0:n], x_[:, 0:n], tabC, ALU.mult)
            P.tt("dve", out32, pr[:, 0:n], tabS, ALU.mult)
            P.tt("pool", out32, out32, t_[:, 0:n], ALU.add)

        for (t0, n) in self.groups:
            samp = n < 128
            if samp:
                stp.close()
                P.barrier()
            self.load_x(st, xsrc, t0, n, xt)
            self.make_xT(xt, n, xT)
            if samp:
                P.dma("sp", rC[:, 0:n], self.ropeCs[:, 0:n]); P.dma("sp", rS[:, 0:n], self.ropeSs[:, 0:n])
            else:
                P.dma("sp", rC[:, 0:n], self.ropeC[:, t0:t0 + n]); P.dma("sp", rS[:, 0:n], self.ropeS[:, t0:t0 + n])
            it = 0
            for qc in range(4):
                ps = PS[2 + it % 2]
                for k in range(8):
                    P.mm(ps[:, 0:n], wq[:, k, qc * 128:(qc + 1) * 128], xT[:, k, 0:n], k == 0, k == 7)
                rope(ps, n, qr32[:, qc, 0:n], rC[:, 0:n], rS[:, 0:n], it); it += 1
                if not samp:
                    for qi in range(2):
                        P.copy("act", Qblk[0:64, qc, qi, 0:256], qr32[0:64, qc, qi * 256:(qi + 1) * 256])
                        P.copy("dve", Qblk[64:128, qc, qi, 256:512], qr32[64:128, qc, qi * 256:(qi + 1) * 256])
            for g in range(4):
                ps = PS[2 + it % 2]
                for k in range(8):
                    P.mm(ps[:, 0:n], wkd[:, k, g, :], xT[:, k, 0:n], k == 0, k == 7)
                kr = kr32[g % 2]
                rope(ps, n, kr[:, 0:n], rC[:, 0:n], rS[:, 0:n], it); it += 1
                if samp:
                    P.dma("sp", self.k_s[l, g, :, :], kr[0:64, 0:n])
                    P.copy("dve", knew[:, g, :], kr[:, 0:n])
                else:
                    P.dma("sp", self.k_p[l, g, :, t0:t0 + n], kr[0:64, 0:n])
                    P.copy("act", KT[:, g, t0:t0 + n], kr[:, 0:n])
                    P.reduce(KM[:, g, t0 // 256:t0 // 256 + 2], kr.v(kr.t[:, 0:n].rearrange("p (b t) -> p b t", b=2)), ALU.add)
                    P.ts("dve", KM[:, g, t0 // 256:t0 // 256 + 2], KM[:, g, t0 // 256:t0 // 256 + 2], 1.0 / 256, None, op0=ALU.mult)
            for i, m_ in self.subs(n):
                ps = PS[2 + i % 2]
                for k in range(8):
                    P.mm(ps[0:m_, 0:256], xT[:, k, i * 128:i * 128 + m_], wv[:, k, :], k == 0, k == 7)
                v_ = vtm[i % 2]
                P.copy("act", v_[0:m_, :], ps[0:m_, 0:256])
                if samp:
                    P.dma("sp", self.v_s[l, :, :], v_[0:m_, :])
                else:
                    P.dma("sp", self.v_p[l, t0 + i * 128:t0 + (i + 1) * 128, :], v_[:, :])
                    P.copy("dve", Vaug[:, (t0 // 128) + i, :, 0:64], v_.v(v_.t[:, :].rearrange("p (g d) -> p g d", g=4)))
            if samp:
                self.moba_sample(l, qr32, knew, st)
                continue
            for qo in (0, 256):
                qb = (t0 + qo) // 256
                for g in range(4):
                    if qb > 0:
                        for hl in range(2):
                            pgt = PS[4 + hl]
                            for half in range(2):
                                P.mm(pgt[:, half * 32:(half + 1) * 32], qr32[hl * 64:(hl + 1) * 64, g, qo + half * 128:qo + (half + 1) * 128],
                                     KM[hl * 64:(hl + 1) * 64, g, :])
                            P.copy("act", gsb[:, hl * 2:hl * 2 + 2, 0:qb], pgt.v(pgt.t[:, 0:64].rearrange("p (a b) -> p a b", a=2)[:, :, 0:qb]))
                        for idx in range(4):
                            P.emit("dve", lambda e, idx=idx: e.max(m8.t[:, idx, :], gsb.t[:, idx, :]), [gsb[:, :, :]], [m8[:, :, :]])
                            P.ts("dve", neg[:, idx, :], gsb[:, idx, :], m8[:, idx, 2:3], None, op0=ALU.is_ge)
                        pnt = PS[5]
                        for idx in range(4):
                            P.tr(pnt[0:32, idx * 128:(idx + 1) * 128], neg[:, idx, :], self.ident[:, :])
                        P.copy("act", negT[0:32, :], pnt[0:32, :])
                    po = PS[6 + g % 2]
                    tiles = [(nb, kt) for nb in range(qb) for kt in range(2)] + [(qb, 0), (qb, 1)]
                    qi = qo // 256
                    for ti, (nb, kt) in enumerate(tiles):
                        psS = PS[ti % 4]
                        ks = slice(nb * 256 + kt * 128, nb * 256 + (kt + 1) * 128)
                        p_ = pt[ti % 3]
                        if nb < qb and kt == 0:
                            pm = PS[4 + nb % 2]
                            mk = mskb[nb % 2]
                            P.mm(pm[:, :], oneh[:, nb * 128:(nb + 1) * 128], negT[:, :])
                            P.copy("dve", mk[:, :], pm[:, :])
                        P.mm(psS[:, :], KT[:, g, ks], Qblk[:, g, qi, :])
                        P.act(p_[:, :], psS[:, :], AF.Exp, scale=0.125)
                        if nb < qb:
                            P.tt("dve", p_[:, :], p_[:, :], mskb[nb % 2][:, :], ALU.mult)
                        else:
                            P.tt("dve", p_[:, :], p_[:, :], caus[:, kt, :], ALU.mult)
                        P.mm(po[0:65, :], Vaug[:, nb * 2 + kt, g, :], p_[:, :], ti == 0, ti == len(tiles) - 1)
                    P.copy("act", srow[64:65, :], po[64:65, :])
                    P.recip(srow[64:65, :], srow[64:65, :])
                    pbc = PS[4]
                    P.mm(pbc[0:64, :], ones32[64:65, 0:64], srow[64:65, :])
                    P.copy("act", bcs[:, :], pbc[0:64, :])
                    P.tt("dve", ya[:, :], po[0:64, :], bcs[:, :], ALU.mult)
                    for hl in range(2):
                        P.dma("sp", self.yT[l].part(("g", t0 + qo, 4 + g, hl), (4 + g, slice(hl * 64, (hl + 1) * 64), slice(t0 + qo, t0 + qo + 256))),
                              ya[:, hl * 256:(hl + 1) * 256])
    P.barrier()


Model.pass1b = _pass1b


def _moba_sample(self, l, qr32, knew, st):
    P = self.P
    PS = self.PS
    L, NPG = self.L, self.NPG
    NBP = NPG // 2
    npl = NPG
    sb = lambda name, shape, dt=F32: P.sb("ms_" + name, shape, dt, st)
    qs = sb("qs", [128, 4, NS]); P.copy("dve", qs[:, :, :], qr32[:, :, 0:NS])
    qscr, kscr = self.qscr[l], self.kscr[l]
    P.dma("sp", qscr.v(qscr.t[:, :, :].rearrange("c s q -> q (c s)")), qs.v(qs.t[:, :, :].rearrange("q c s -> q (c s)")), allow_slow_non_contiguous=True)
    P.dma("sp", kscr.v(kscr.t[:, :, :].rearrange("g s d -> d (g s)")), knew.v(knew.t[0:64, :, :].rearrange("d g s -> d (g s)")), allow_slow_non_contiguous=True)
    ptT = sb("ptT", [128, NS], I32); P.dma("sp", ptT[0:npl, :], self.ptT[:, :])
    pair = sb("pair", [128, 64]); P.dma("sp", pair[:, :], self.c_pair[:, :])
    pairT = sb("pairT", [64, 128]); P.dma("sp", pairT[:, :], self.c_pairT[:, :])
    ones = sb("ones", [128, 1]); P.memset("dve", ones[:, :], 1.0)
    qrow = sb("qrow", [128, 512]); krow = sb("krow", [128, 256]); vrow = sb("vrow", [1, 256])
    idx2 = [sb("idx%d" % i, [128, 1], I32) for i in range(2)]
    Kc = [sb("Kc%d" % i, [128, 8, 256]) for i in range(1)] * 2
    prod = sb("prod", [128, 8, 2, 64])
    S_all = sb("S_all", [128, 128, 8]); P_all = sb("P_all", [128, 128, 8])
    R = sb("R", [128, 8]); Psum_ = sb("Psum", [128, 8]); sself = sb("sself", [128, 8]); pself = sb("pself", [128, 8])
    gsb = sb("gsb", [8, 64]); P.memset("dve", gsb[:, :], -1e30)
    m8 = sb("m8", [8, 8]); sel = sb("sel", [8, 64]); selT = sb("selT", [64, 8]); selP = sb("selP", [128, 8])
    rden = sb("rden", [8, 1]); osb = sb("osb", [8, 256], BF16)

    def gather(dst, flat, s_, tc, i):
        ix = idx2[i % 2]
        P.ts("dve", ix[0:npl, :], ptT[0:npl, s_:s_ + 1], 16.0, float(l * self.npool * 16 + tc), op0=ALU.mult, op1=ALU.add)
        d_ap = dst.t[0:npl, :, :].rearrange("p a b -> p (a b)")
        src = flat.t[:, :]
        i_ap = ix.t[0:npl, 0:1]
        P.dma_custom("pool", lambda e: e.indirect_dma_start(out=d_ap, out_offset=None, in_=src,
                                                             in_offset=bass.IndirectOffsetOnAxis(ap=i_ap, axis=0)),
                     [flat[:, :], ix[0:npl, :]], [dst[0:npl, :, :]])

    for s_ in range(NS):
        P.dma("sp", qrow[:, :], qscr.v(bass.AP(qscr.t, s_ * 128, [[0, 128], [NS * 128, 4], [1, 128]])))
        P.dma("sp", krow[:, :], kscr.v(bass.AP(kscr.t, s_ * 64, [[0, 128], [NS * 64, 4], [1, 64]])))
        P.dma("sp", vrow[0:1, :], self.v_s[l, s_:s_ + 1, :])
        for tc in range(16):
            kc = Kc[tc % 2]
            gather(kc, self.ck_flat, s_, tc, tc)
            for g in range(4):
                kv = kc.v(bass.AP(kc.t, g * 64, [[8 * 256, npl], [256, 8], [0, 2], [1, 64]]))
                qv = qrow.v(bass.AP(qrow.t, g * 128, [[512, npl], [0, 8], [64, 2], [1, 64]]))
                P.tt("dve", prod[0:npl, :, :, :], kv, qv, ALU.mult)
                P.reduce(S_all.v(bass.AP(S_all.t, tc * 64 + g * 2, [[1024, npl], [8, 8], [1, 2]])),
                         prod.v(prod.t[0:npl, :, :, :].rearrange("p a b d -> p (a b) d")), ALU.add)
        P.reduce(R[0:npl, :], S_all.v(S_all.t[0:npl, :, :].rearrange("p t h -> p h t")), ALU.add)
        pg = PS[2]
        P.mm(pg[0:8, 0:NBP], R[0:npl, :], pair[0:npl, 0:NBP])
        P.copy("act", gsb[:, 0:NBP], pg[0:8, 0:NBP])
        P.emit("dve", lambda e: e.max(m8.t[:, :], gsb.t[:, :]), [gsb[:, :]], [m8[:, :]])
        P.ts("dve", sel[:, :], gsb[:, :], m8[:, 2:3], None, op0=ALU.is_ge)
        pt_ = PS[3]
        P.tr(pt_[0:64, 0:8], sel[:, :], self.ident[0:8, 0:8])
        P.copy("act", selT[:, :], pt_[0:64, 0:8])
        pp = PS[4]
        P.mm(pp[0:npl, 0:8], pairT[0:NBP, 0:npl], selT[0:NBP, :])
        P.copy("act", selP[0:npl, :], pp[0:npl, 0:8])
        P.act(P_all.v(P_all.t[0:npl, :, :].rearrange("p t h -> p (t h)")), S_all.v(S_all.t[0:npl, :, :].rearrange("p t h -> p (t h)")), AF.Exp, scale=0.125)
        P.tt("dve", P_all[0:npl, :, :], P_all[0:npl, :, :], selP.v(bc_mid(selP.t[0:npl, :], 128)), ALU.mult)
        P.reduce(Psum_[0:npl, :], P_all.v(P_all.t[0:npl, :, :].rearrange("p t h -> p h t")), ALU.add)
        for g in range(4):
            kv = krow.v(bass.AP(krow.t, g * 64, [[256, 128], [0, 2], [1, 64]]))
            P.tt("dve", prod[:, 0, :, :], kv, qrow.v(bass.AP(qrow.t, g * 128, [[512, 128], [64, 2], [1, 64]])), ALU.mult)
            P.reduce(sself[:, g * 2:(g + 1) * 2], prod[:, 0, :, :], ALU.add)
        P.act(pself[:, :], sself[:, :], AF.Exp, scale=0.125)
        pd = PS[5]
        P.mm(pd[0:8, 0:1], Psum_[0:npl, :], ones[0:npl, 0:1], True, False)
        P.mm(pd[0:8, 0:1], pself[0:1, :], ones[0:1, 0:1], False, True)
        po = PS[6]
        for tc in range(16):
            vc = Kc[tc % 2]
            gather(vc, self.cv_flat, s_, tc, tc)
            for tk in range(8):
                P.mm(po[0:8, 0:256], P_all[0:npl, tc * 8 + tk, :], vc[0:npl, tk, :], tc == 0 and tk == 0, False)
        P.mm(po[0:8, 0:256], pself[0:1, :], vrow[0:1, :], False, True)
        P.copy("act", rden[:, :], pd[0:8, 0:1])
        P.recip(rden[:, :], rden[:, :])
        P.ts("dve", osb[:, :], po[0:8, 0:256], rden[:, 0:1], None, op0=ALU.mult)
        for h in range(8):
            g, hl = divmod(h, 2)
            yt_ = self.yT[l]
            P.dma("sp", yt_.v(yt_.t[4 + g, hl * 64:(hl + 1) * 64, L + s_:L + s_ + 1].rearrange("d a -> a d"), key=("g", L, 4 + g, hl, s_)),
                  osb[h:h + 1, g * 64:(g + 1) * 64], allow_slow_non_contiguous=True)


Model.moba_sample = _moba_sample


_CACHE = {}


def kernel(**inputs):
    inp = {k: np.asarray(v) for k, v in inputs.items()}
    B, L = inp["x_prompt"].shape[0], inp["x_prompt"].shape[1]
    nsamp = inp["x_sample"].shape[0]
    ncores = 8
    assert nsamp == ncores * NS
    cfg = dict(L=L, NPG=inp["page_table"].shape[1], npool=inp["cache_k"].shape[1])
    key = (cfg["L"], cfg["NPG"], cfg["npool"])
    if key not in _CACHE:
        m = Model(cfg)
        m.build()
        _CACHE[key] = m
    m = _CACHE[key]
    prompt_b = [c % B for c in range(ncores)]
    maps = host_inputs(inp, cfg, m, ncores, prompt_b)
    res = run_bass_kernel_spmd(m.nc, maps, core_ids=list(range(ncores)))
    outs = host_outputs(res.results, cfg, ncores, prompt_b)
    return tuple(np.ascontiguousarray(o, dtype=np.float32) for o in outs)
```

```python
import math
from contextlib import ExitStack
import numpy as np
import concourse.bass as bass
import concourse.mybir as mybir
from concourse.bass_utils import run_bass_kernel_spmd

F32 = mybir.dt.float32
BF16 = mybir.dt.bfloat16
I32 = mybir.dt.int32
F32R = mybir.dt.float32r
import os as _os0
USE_F32R = False
AF = mybir.ActivationFunctionType
ALU = mybir.AluOpType
AX = mybir.AxisListType

N_DSEM = 12


class Trk:
    __slots__ = ("w", "r", "excl")

    def __init__(self, excl=False):
        self.w = None
        self.r = {}
        self.excl = excl


class V:
    __slots__ = ("trk", "ap")

    def __init__(self, trk, ap):
        self.trk = trk
        self.ap = ap

    def __getitem__(self, idx):
        return V(self.trk, self.ap[idx])


class Buf:
    def __init__(self, t, space):
        self.t = t
        self.space = space
        self.trk = Trk(space == "ps")
        self.parts = {}

    def __getitem__(self, idx):
        return V(self.trk, self.t[idx])

    def part(self, key, idx):
        trk = self.parts.get(key)
        if trk is None:
            trk = self.parts[key] = Trk(self.space == "ps")
        return V(trk, self.t[idx])

    def v(self, ap, key=None):
        if key is None:
            return V(self.trk, ap)
        trk = self.parts.get(key)
        if trk is None:
            trk = self.parts[key] = Trk(self.space == "ps")
        return V(trk, ap)


class Prog:
    ENG = ("pe", "act", "dve", "pool", "sp")

    def __init__(self, nc):
        self.nc = nc
        self.ops = {k: [] for k in self.ENG}
        self.cnt = {k: 0 for k in self.ENG}
        self.sem = {k: nc.alloc_semaphore(name="s_" + k) for k in self.ENG}
        self.semobj = {("e", k): self.sem[k] for k in self.ENG}
        self.dq = {}
        for q in ("sp", "pool", "act"):
            sems = [nc.alloc_semaphore(name="d_%s_%d" % (q, i)) for i in range(N_DSEM)]
            for i, s in enumerate(sems):
                self.semobj[("d", q, i)] = s
            self.dq[q] = {"tgt": [0] * N_DSEM, "next": 0}
        self.seen = {k: {} for k in self.ENG}
        self.stack = ExitStack()
        self.n_inst = 0
        self.psum_pool = []

    def sb(self, name, shape, dtype, stack=None):
        self.uid = getattr(self, "uid", 0) + 1
        name = "%s_u%d" % (name, self.uid)
        t = (stack or self.stack).enter_context(self.nc.sbuf_tensor(name, list(shape), dtype))
        return Buf(t, "sb")

    def ps(self, name, shape=(128, 512), dtype=F32, stack=None):
        t = (stack or self.stack).enter_context(self.nc.psum_tensor(name, list(shape), dtype))
        return Buf(t, "ps")

    def dram(self, name, shape, dtype, kind="Internal"):
        t = self.nc.dram_tensor(name, list(shape), dtype, kind=kind)
        return Buf(t, "dram")

    def _deps(self, reads, writes):
        deps = {}

        def add(tok):
            if tok is None:
                return
            k, v = tok
            if deps.get(k, 0) < v:
                deps[k] = v

        for x in reads:
            add(x.trk.w)
            if x.trk.excl:
                for k, v in x.trk.r.items():
                    add((k, v))
        for x in writes:
            add(x.trk.w)
            for k, v in x.trk.r.items():
                add((k, v))
        return deps

    def _mark(self, tok, reads, writes):
        for x in reads:
            k, v = tok
            if x.trk.r.get(k, 0) < v:
                x.trk.r[k] = v
        for x in writes:
            x.trk.w = tok
            x.trk.r = {}

    def _waits(self, eng, deps):
        ws = []
        seen = self.seen[eng]
        for k, v in deps.items():
            if eng == "pe" and k == ("e", "pe"):
                continue
            if seen.get(k, 0) < v:
                seen[k] = v
                ws.append((k, v))
        return ws

    def emit(self, eng, fn, reads, writes):
        reads = [r for r in reads if isinstance(r, V)]
        deps = self._deps(reads, writes)
        ws = self._waits(eng, deps)
        self.cnt[eng] += 1
        tok = (("e", eng), self.cnt[eng])
        self.ops[eng].append((ws, fn, (self.sem[eng], 1)))
        self._mark(tok, reads, writes)
        self.n_inst += 1 + len(ws)
        return tok

    def dma(self, q, out, in_, **kw):
        oap, iap = out.ap, in_.ap
        return self.dma_custom(q, lambda e: e.dma_start(out=oap, in_=iap, **kw), [in_], [out])

    def dma_custom(self, q, fn, reads, writes):
        deps = self._deps(reads, writes)
        st = self.dq[q]
        slot = st["next"] % N_DSEM
        st["next"] += 1
        key = ("d", q, slot)
        if st["tgt"][slot] > 0:
            if deps.get(key, 0) < st["tgt"][slot]:
                deps[key] = st["tgt"][slot]
        ws = self._waits(q, deps)
        st["tgt"][slot] += 16
        tok = (key, st["tgt"][slot])
        self.ops[q].append((ws, fn, (self.semobj[key], 16)))
        self._mark(tok, reads, writes)
        self.n_inst += 1 + len(ws)
        return tok

    def barrier(self):
        deps = {}
        for k in self.ENG:
            if self.cnt[k] > 0:
                deps[("e", k)] = self.cnt[k]
        for q, st in self.dq.items():
            for i in range(N_DSEM):
                if st["tgt"][i] > 0:
                    deps[("d", q, i)] = st["tgt"][i]
        for eng in self.ENG:
            d = {k: v for k, v in deps.items() if k != ("e", eng)}
            ws = self._waits(eng, d)
            if ws:
                self.ops[eng].append((ws, None, None))
                self.n_inst += len(ws)

    def finish(self):
        self.barrier()
        nc = self.nc
        eobj = {"pe": "tensor", "act": "scalar", "dve": "vector", "pool": "gpsimd", "sp": "sync"}
        with nc.Block() as block:
            for eng in self.ENG:
                ops = self.ops[eng]
                semobj = self.semobj

                def run(e, ops=ops):
                    for ws, fn, inc in ops:
                        for k, v in ws:
                            e.wait_ge(semobj[k], v)
                        if fn is not None:
                            ins = fn(e)
                            ins.then_inc(inc[0], inc[1])

                getattr(block, eobj[eng])(run)
        self.stack.close()

    def mm(self, out, lhsT, rhs, start=True, stop=True, **kw):
        o, l, r = out.ap, lhsT.ap, rhs.ap
        return self.emit("pe", lambda e: e.matmul(o, l, r, start=start, stop=stop, **kw), [lhsT, rhs], [out])

    def tr(self, out, in_, ident):
        o, i, d = out.ap, in_.ap, ident.ap
        return self.emit("pe", lambda e: e.transpose(o, i, d), [in_, ident], [out])

    def act(self, out, in_, func, bias=None, scale=1.0, accum=None, eng="act"):
        o, i = out.ap, in_.ap
        b = bias.ap if isinstance(bias, V) else bias
        s = scale.ap if isinstance(scale, V) else scale
        kw = {}
        if b is not None:
            kw["bias"] = b
        if accum is not None:
            kw["accum_out"] = accum.ap
        wr = [out] + ([accum] if accum is not None else [])
        return self.emit("act", lambda e: e.activation(out=o, in_=i, func=func, scale=s, **kw),
                         [in_, bias, scale], wr)

    def copy(self, eng, out, in_):
        o, i = out.ap, in_.ap
        if eng == "act":
            return self.emit("act", lambda e: e.copy(o, i), [in_], [out])
        return self.emit(eng, lambda e: e.tensor_copy(o, i), [in_], [out])

    def tt(self, eng, out, in0, in1, op):
        o, a, b = out.ap, in0.ap, in1.ap
        return self.emit(eng, lambda e: e.tensor_tensor(o, a, b, op), [in0, in1], [out])

    def ts(self, eng, out, in0, s1, s2=None, op0=ALU.mult, op1=None, accum=None):
        o, a = out.ap, in0.ap
        x1 = s1.ap if isinstance(s1, V) else s1
        x2 = s2.ap if isinstance(s2, V) else s2
        kw = {}
        if op1 is not None:
            kw["op1"] = op1
        if accum is not None:
            kw["accum_out"] = accum.ap
        wr = [out] + ([accum] if accum is not None else [])
        return self.emit(eng, lambda e: e.tensor_scalar(o, a, x1, x2, op0, **kw), [in0, s1, s2], wr)

    def stt(self, out, in0, scalar, in1, op0, op1, eng="dve"):
        o, a, b = out.ap, in0.ap, in1.ap
        s = scalar.ap if isinstance(scalar, V) else scalar
        return self.emit(eng, lambda e: e.scalar_tensor_tensor(o, a, s, b, op0, op1), [in0, scalar, in1], [out])

    def scan(self, out, d0, d1, init, op0, op1):
        o, a, b = out.ap, d0.ap, d1.ap
        s = init.ap if isinstance(init, V) else init
        return self.emit("dve", lambda e: e.tensor_tensor_scan(o, a, b, s, op0, op1), [d0, d1, init], [out])

    def memset(self, eng, out, val):
        o = out.ap
        return self.emit(eng, lambda e: e.memset(o, val), [], [out])

    def reduce(self, out, in_, op, axis=AX.X, eng="dve"):
        o, i = out.ap, in_.ap
        return self.emit(eng, lambda e: e.tensor_reduce(o, i, axis, op), [in_], [out])

    def recip(self, out, in_):
        o, i = out.ap, in_.ap
        return self.emit("dve", lambda e: e.reciprocal(o, i), [in_], [out])


D = 1024
DEPTH = 2
NS = 4
PAGE = 128
R_IN = 896
N_IN = 5248
O_SSM, O_RWKV, O_Q, O_K, O_V = 3072, 3328, 4224, 4736, 4992
D_FF = 2816
NFC = 44
ALPHA = (2 * DEPTH) ** 0.25
LN_EPS = 1e-5
GN_EPS = 64e-5
ROPE_THETA = 500000.0
NEG = -30000.0


def bc_mid(ap, n):
    return bass.AP(ap.tensor, ap.offset, [list(ap.ap[0]), [0, n]] + [list(x) for x in ap.ap[1:]])


def bc_last(ap, n):
    return bass.AP(ap.tensor, ap.offset, [list(x) for x in ap.ap] + [[0, n]])


class Ctx:
    pass


class Model:
    def __init__(self, cfg):
        self.cfg = cfg
        self.L = cfg["L"]
        self.NPG = cfg["NPG"]
        self.npool = cfg["npool"]
        self.dbg = cfg.get("dbg", {})
        self.NT = self.L + NS
        self.nc = bass.Bass("TRN2", target_bir_lowering=False)
        self.P = Prog(self.nc)
        self.inputs = {}
        self.outputs = {}
        L = self.L
        self.groups = [(t0, 512) for t0 in range(0, L, 512)] + [(L, NS)]

    def din(self, name, shape, dtype=F32):
        b = self.P.dram(name, shape, dtype, kind="ExternalInput")
        self.inputs[name] = (tuple(shape), dtype)
        return b

    def dout(self, name, shape, dtype=F32):
        b = self.P.dram(name, shape, dtype, kind="ExternalOutput")
        self.outputs[name] = (tuple(shape), dtype)
        return b

    def scratch(self, name, shape, dtype=F32):
        if name in self.dbg.get("out", ()):
            return self.dout(name, shape, dtype)
        if name in self.dbg.get("in", ()):
            return self.din(name, shape, dtype)
        return self.P.dram(name, shape, dtype)

    def subs(self, n):
        return [(i, 128) for i in range(n // 128)] if n >= 128 else [(0, n)]

    def load_x(self, st, xsrc, t0, n, xt):
        P = self.P
        for i, m in self.subs(n):
            P.dma("sp", xt.part(i, (slice(0, m), i, slice(None))), xsrc.part(("r", t0 + i * 128), (slice(t0 + i * 128, t0 + i * 128 + m), slice(None))))

    def make_xT(self, xt, n, xT):
        P = self.P
        for k in range(8):
            ps = self.PS[k % 2]
            for i, m in self.subs(n):
                P.tr(ps[:, i * 128:i * 128 + m], xt.part(i, (slice(0, m), i, slice(k * 128, (k + 1) * 128))), self.ident[0:m, 0:m])
            if k % 2 == 0:
                P.copy("act", xT[:, k, 0:n], ps[:, 0:n])
            else:
                P.copy("dve", xT[:, k, 0:n], ps[:, 0:n])

    def ln_rows(self, src, m, gB, bB, dst, eps):
        P = self.P
        st, mv = self.ln_st, self.ln_mv
        for h in range(2):
            sap = src.ap[0:m, h * 512:(h + 1) * 512]
            P.emit("dve", lambda e, h=h, sap=sap: e.bn_stats(st.t[0:m, h * 6:(h + 1) * 6], sap), [src], [st[0:m, :]])
        P.emit("dve", lambda e: e.bn_aggr(mv.t[0:m, 0:2], st.t[0:m, 0:12]), [st[0:m, :]], [mv[0:m, :]])
        P.act(mv[0:m, 2:3], mv[0:m, 1:2], AF.Sqrt, bias=self.epsc[eps][0:m, 0:1])
        P.recip(mv[0:m, 3:4], mv[0:m, 2:3])
        P.ts("dve", V(dst.trk, dst.ap[0:m, :]), V(src.trk, src.ap[0:m, :]), mv[0:m, 0:1], mv[0:m, 3:4], op0=ALU.subtract, op1=ALU.mult)
        P.tt("pool", V(dst.trk, dst.ap[0:m, :]), V(dst.trk, dst.ap[0:m, :]), gB[0:m, :], ALU.mult)
        P.tt("pool", V(dst.trk, dst.ap[0:m, :]), V(dst.trk, dst.ap[0:m, :]), bB[0:m, :], ALU.add)

    def bcast_row_load(self, dst, src_buf, off, ncols, np_=128):
        ap = bass.AP(src_buf.t, off, [[0, np_], [1, ncols]])
        self.P.dma("sp", dst[0:np_, 0:ncols], src_buf.v(ap))

    def setup(self):
        P = self.P
        L, NT = self.L, self.NT
        self.PS = [P.ps("ps%d" % i) for i in range(8)]
        self.x_p = self.din("x_p", [L, D])
        self.x_s = self.din("x_s", [NS, D])
        self.c_ident = self.din("c_ident", [128, 128])
        self.w_in = self.din("w_in", [DEPTH, D, N_IN])
        self.proj = self.din("proj", [DEPTH, D, D])
        self.w_o = self.din("w_o", [DEPTH, D, D])
        self.w_up = self.din("ffn_w_up", [DEPTH, D, 2 * D_FF])
        self.w_dn = self.din("ffn_w_down", [DEPTH, D_FF, D])
        self.lnrows = self.din("lnrows", [2 + 4 * DEPTH, D])
        self.convp = self.din("convp", [DEPTH, 128, NFC, 4])
        self.conv0 = self.din("conv0", [DEPTH, 128, NFC, NS, 2])
        self.ropeC = self.din("ropeC", [128, L]); self.ropeS = self.din("ropeS", [128, L])
        self.ropeCs = self.din("ropeCs", [128, NS]); self.ropeSs = self.din("ropeSs", [128, NS])
        self.c_rot = self.din("c_rot", [128, 128])
        self.c_causal = self.din("c_causal", [128, 2, 512])
        self.c_onehot = self.din("c_onehot", [32, 32 * 128])
        self.c_pair = self.din("c_pair", [128, 64]); self.c_pairT = self.din("c_pairT", [64, 128])
        self.ptT = self.din("ptT", [self.NPG, NS], I32)
        self.ck_flat = self.din("ck_flat", [DEPTH * self.npool * 16, 2048])
        self.cv_flat = self.din("cv_flat", [DEPTH * self.npool * 16, 2048])
        self.qscr = [self.P.dram("qscr%d" % l, [4, NS, 128], F32) for l in range(DEPTH)]
        self.kscr = [self.P.dram("kscr%d" % l, [4, NS, 64], F32) for l in range(DEPTH)]
        self.k_p = self.dout("k_p", [DEPTH, 4, 64, L]); self.v_p = self.dout("v_p", [DEPTH, L, 256])
        self.k_s = self.dout("k_s", [DEPTH, 4, 64, NS]); self.v_s = self.dout("v_s", [DEPTH, NS, 256])
        self.s5B = self.din("s5B", [DEPTH, 2, 8, 128, 128])
        self.s5C = self.din("s5C", [DEPTH, 2, 8, 128, 128])
        self.w_glu = self.din("ssm_w_glu", [DEPTH, 256, 256])
        self.s5p = self.din("s5p", [DEPTH, 128, 3, 8])
        self.s5cols = self.din("s5cols", [DEPTH, 128, 2, 2])
        self.s5s0 = self.din("s5s0", [DEPTH, 128, 2, 8, NS])
        self.rwcols = self.din("rwcols", [DEPTH, 128, 21])
        self.rwrows = self.din("rwrows", [DEPTH * 2, 256])
        self.lora = self.din("lora", [DEPTH, 128, 256])
        self.c_masks = self.din("c_masks", [128, 3, 128])
        self.c_blk = self.din("c_blk", [128, 128])
        self.c_hsel = self.din("c_hsel", [128, 2])
        self.rw_sh0 = self.din("rw_sh0", [DEPTH, 128, 7, NS])
        self.st_rwkv = self.din("st_rwkv", [DEPTH, NS, 2, 128, 64])
        self.rwscr = [self.P.dram("rwscr%d" % l, [5, 2, NS, 128], F32) for l in range(DEPTH)]
        self.rw_p = self.dout("rw_p", [DEPTH, 128, 2, 64])
        self.rw_s = self.dout("rw_s", [DEPTH, NS, 2, 128, 64])
        self.sh_p = self.dout("sh_p", [DEPTH, 128, 7])
        self.sh_s = self.dout("sh_s", [DEPTH, 128, 7, NS])
        self.sre_p = self.dout("sre_p", [DEPTH, 128, 8])
        self.sim_p = self.dout("sim_p", [DEPTH, 128, 8])
        self.sre_s = self.dout("sre_s", [DEPTH, 128, 8, NS])
        self.sim_s = self.dout("sim_s", [DEPTH, 128, 8, NS])
        self.y_p = self.dout("y_p", [L, D])
        self.y_s = self.dout("y_s", [NS, D])
        self.cv_p = self.dout("cv_p", [DEPTH, 128, NFC, 2])
        self.cv_s = self.dout("cv_s", [DEPTH, 128, NFC, NS, 2])
        self.xA = self.scratch("xA", [NT, D])
        self.xB = self.scratch("xB", [NT, D])
        self.xC = self.scratch("xC", [NT, D])
        self.yT = [self.scratch("yT%d" % l, [8, 128, NT], BF16) for l in range(DEPTH)]
        self.hT = self.scratch("hT", [22, 128, NT], BF16)
        self.ident = P.sb("ident", [128, 128], F32)
        P.dma("sp", self.ident[:, :], self.c_ident[:, :])
        self.identb = P.sb("identb", [128, 128], BF16)
        P.copy("dve", self.identb[:, :], self.ident[:, :])
        self.ln_st = P.sb("ln_st", [128, 12], F32)
        self.ln_mv = P.sb("ln_mv", [128, 4], F32)
        self.epsc = {}
        for eps in (LN_EPS, GN_EPS):
            t = P.sb("eps%d" % len(self.epsc), [128, 1], F32)
            P.memset("pool", t[:, :], eps)
            self.epsc[eps] = t
        self.gB = P.sb("gB", [128, D], F32)
        self.bB = P.sb("bB", [128, D], F32)

    def load_ln(self, row):
        self.bcast_row_load(self.gB, self.lnrows, row * D, D)
        self.bcast_row_load(self.bB, self.lnrows, (row + 1) * D, D)

    def xrow(self, buf, t0, m):
        return buf.part(("r", t0), (slice(t0, t0 + m), slice(None)))

    def pass2(self, l, xsrc, xdst):
        P = self.P
        with ExitStack() as st:
            wg = P.sb("p2_wg", [128, 8, 3072], BF16, st)
            pj = P.sb("p2_pj", [128, 8, D], BF16, st)
            wo = P.sb("p2_wo", [128, 8, D], BF16, st)
            xt = P.sb("p2_xt", [128, 4, D], F32, st)
            xT = P.sb("p2_xT", [128, 8, 512], BF16, st)
            yt = P.sb("p2_yt", [128, 8, 512], BF16, st)
            mT = P.sb("p2_mT", [128, 8, 512], BF16, st)
            gt = [P.sb("p2_g%d" % b, [128, 512], F32, st) for b in range(3)]
            acc = [P.sb("p2_a%d" % b, [128, 512], F32, st) for b in range(3)]
            tres = [P.sb("p2_t%d" % b, [128, D], F32, st) for b in range(2)]
            for k in range(8):
                for c0 in range(0, 3072, 1024):
                    P.dma("pool", wg[:, k, c0:c0 + 1024], self.w_in[l, k * 128:(k + 1) * 128, c0:c0 + 1024])
                P.dma("pool", pj[:, k, :], self.proj[l, k * 128:(k + 1) * 128, :])
                P.dma("pool", wo[:, k, :], self.w_o[l, k * 128:(k + 1) * 128, :])
            self.load_ln(2 + 4 * l)
            PS = self.PS
            for (t0, n) in self.groups:
                self.load_x(st, xsrc, t0, n, xt)
                self.make_xT(xt, n, xT)
                for kc in range(8):
                    P.dma("sp", yt[:, kc, 0:n], self.yT[l].part(("g", t0), (kc, slice(None), slice(t0, t0 + n))))
                for m in range(8):
                    cs = slice(m * 128, (m + 1) * 128)
                    for b in range(3):
                        for k in range(8):
                            P.mm(PS[2 + b][:, 0:n], wg[:, k, b * 1024 + m * 128:b * 1024 + (m + 1) * 128], xT[:, k, 0:n], k == 0, k == 7)
                    kcs = [(0, 2), (2, 4), (4, 8)]
                    for b in range(3):
                        a, e_ = kcs[b]
                        for kc in range(a, e_):
                            P.mm(PS[5 + b][:, 0:n], pj[:, kc, cs], yt[:, kc, 0:n], kc == a, kc == e_ - 1)
                    for b in range(3):
                        P.act(gt[b][:, 0:n], PS[2 + b][:, 0:n], AF.Sigmoid)
                    for b in range(3):
                        P.tt("dve", acc[b][:, 0:n], gt[b][:, 0:n], PS[5 + b][:, 0:n], ALU.mult)
                    P.tt("pool", acc[0][:, 0:n], acc[0][:, 0:n], acc[1][:, 0:n], ALU.add)
                    P.tt("pool", mT[:, m, 0:n], acc[0][:, 0:n], acc[2][:, 0:n], ALU.add)
                for i, m_ in self.subs(n):
                    tr_ = tres[i % 2]
                    for half in range(2):
                        ps = PS[half]
                        for k in range(8):
                            P.mm(ps[0:m_, :], mT[:, k, i * 128:i * 128 + m_], wo[:, k, half * 512:(half + 1) * 512], k == 0, k == 7)
                        P.stt(tr_[0:m_, half * 512:(half + 1) * 512], xt.part(i, (slice(0, m_), i, slice(half * 512, (half + 1) * 512))),
                              ALPHA, ps[0:m_, :], ALU.mult, ALU.add)
                    self.ln_rows(tr_[:, :], m_, self.gB, self.bB, tr_[:, :], LN_EPS)
                    P.dma("sp", self.xrow(xdst, t0 + i * 128, m_), tr_[0:m_, :])
        P.barrier()

    def pass3a(self, l, xsrc):
        P = self.P
        L = self.L
        with ExitStack() as st:
            wup = P.sb("p3_wup", [128, 8, 2 * D_FF], BF16, st)
            xt = P.sb("p3_xt", [128, 4, D], F32, st)
            xT = P.sb("p3_xT", [128, 8, 512], BF16, st)
            cp = P.sb("p3_cp", [128, NFC, 4], F32, st)
            c0 = P.sb("p3_c0", [128, NFC, NS, 2], F32, st)
            halo = P.sb("p3_halo", [128, NFC, 2], F32, st)
            cvs = P.sb("p3_cvs", [128, NFC, NS, 2], F32, st)
            ext = [P.sb("p3_ext%d" % i, [128, 514], F32, st) for i in range(4)]
            cv = [P.sb("p3_cv%d" % i, [128, 512], F32, st) for i in range(4)]
            gl = [P.sb("p3_gl%d" % i, [128, 512], F32, st) for i in range(2)]
            hm = [P.sb("p3_hm%d" % i, [128, 512], BF16, st) for i in range(2)]
            for k in range(8):
                for c in range(0, 2 * D_FF, 1024):
                    w = min(1024, 2 * D_FF - c)
                    P.dma("pool", wup[:, k, c:c + w], self.w_up[l, k * 128:(k + 1) * 128, c:c + w])
            P.dma("sp", cp[:, :, :], self.convp[l, :, :, :])
            P.dma("sp", c0[:, :, :, :], self.conv0[l, :, :, :, :])
            P.memset("pool", halo[:, :, :], 0.0)
            PS = self.PS
            it = 0
            import os as _os
            _sk = _os.environ.get("SKIP", "")
            for (t0, n) in self.groups:
                samp = n < 128
                if samp and "s" in _sk:
                    continue
                self.load_x(st, xsrc, t0, n, xt)
                self.make_xT(xt, n, xT)
                for c in range(22):
                    cvp = []
                    for cc in (c + 22, c):
                        ps = PS[2 + it % 6]
                        e_, cv_ = ext[it % 4], cv[it % 4]
                        it += 1
                        for k in range(8):
                            P.mm(ps[:, 0:n], wup[:, k, cc * 128:(cc + 1) * 128], xT[:, k, 0:n], k == 0, k == 7)
                        if not samp:
                            ce = "act"
                            P.copy(ce, e_[:, 0:2], halo[:, cc, :])
                            P.copy("act", e_[:, 2:2 + n], ps[:, 0:n])
                            P.copy(ce, halo[:, cc, :], e_[:, n:n + 2])
                            P.act(cv_[:, 0:n], ps[:, 0:n], AF.Identity, bias=cp[:, cc, 3:4], scale=cp[:, cc, 2:3])
                            P.stt(cv_[:, 0:n], e_[:, 1:1 + n], cp[:, cc, 1:2], cv_[:, 0:n], ALU.mult, ALU.add)
                            P.stt(cv_[:, 0:n], e_[:, 0:n], cp[:, cc, 0:1], cv_[:, 0:n], ALU.mult, ALU.add)
                        else:
                            P.copy("act", cvs[:, cc, :, 1], ps[:, 0:n])
                            P.copy("dve", cvs[:, cc, :, 0], c0[:, cc, :, 1])
                            P.ts("dve", cv_[:, 0:n], ps[:, 0:n], cp[:, cc, 2:3], cp[:, cc, 3:4], op0=ALU.mult, op1=ALU.add)
                            P.stt(cv_[:, 0:n], c0[:, cc, :, 1], cp[:, cc, 1:2], cv_[:, 0:n], ALU.mult, ALU.add)
                            P.stt(cv_[:, 0:n], c0[:, cc, :, 0], cp[:, cc, 0:1], cv_[:, 0:n], ALU.mult, ALU.add)
                        cvp.append(cv_)
                    g_, h_ = gl[c % 2], hm[c % 2]
                    P.act(g_[:, 0:n], cvp[1][:, 0:n], AF.Identity if "g" in _sk else AF.Gelu_apprx_tanh)
                    P.tt("dve" if "p" in _sk else "pool", h_[:, 0:n], g_[:, 0:n], cvp[0][:, 0:n], ALU.mult)
                    if "h" not in _sk:
                        P.dma("sp", self.hT.part(("g", t0), (c, slice(None), slice(t0, t0 + n))), h_[:, 0:n])
                if t0 + n == L and "o" not in _sk and "1" not in _sk:
                    P.dma("sp", self.cv_p[l, :, :, :], halo[:, :, :])
            if "o" not in _sk and "2" not in _sk:
                P.dma("sp", self.cv_s[l, :, :, :, :], cvs[:, :, :, :])
        P.barrier()

    def pass3b(self, l, xsrc, xdst_p, xdst_s):
        P = self.P
        L = self.L
        with ExitStack() as st:
            wdn = P.sb("p4_wdn", [128, 22, D], BF16, st)
            xt = P.sb("p4_xt", [128, 4, D], F32, st)
            ht = P.sb("p4_ht", [128, 22, 512], BF16, st)
            tres = [P.sb("p4_t%d" % b, [128, D], F32, st) for b in range(2)]
            for c in range(22):
                P.dma("pool", wdn[:, c, :], self.w_dn[l, c * 128:(c + 1) * 128, :])
            self.load_ln(4 + 4 * l)
            PS = self.PS
            for (t0, n) in self.groups:
                self.load_x(st, xsrc, t0, n, xt)
                for c in range(22):
                    P.dma("sp", ht[:, c, 0:n], self.hT.part(("g", t0), (c, slice(None), slice(t0, t0 + n))))
                for i, m_ in self.subs(n):
                    tr_ = tres[i % 2]
                    for half in range(2):
                        ps = PS[(2 * i + half) % 8]
                        for c in range(22):
                            P.mm(ps[0:m_, :], ht[:, c, i * 128:i * 128 + m_], wdn[:, c, half * 512:(half + 1) * 512], c == 0, c == 21)
                        P.stt(tr_[0:m_, half * 512:(half + 1) * 512], xt.part(i, (slice(0, m_), i, slice(half * 512, (half + 1) * 512))),
                              ALPHA, ps[0:m_, :], ALU.mult, ALU.add)
                    self.ln_rows(tr_[:, :], m_, self.gB, self.bB, tr_[:, :], LN_EPS)
                    if t0 < L:
                        P.dma("sp", self.xrow(xdst_p, t0 + i * 128, m_), tr_[0:m_, :])
                    else:
                        P.dma("sp", self.xrow(xdst_s, (t0 - L) if xdst_s is not xdst_p else t0, m_), tr_[0:m_, :])
        P.barrier()

    def pass0(self):
        P = self.P
        L = self.L
        with ExitStack() as st:
            xt = P.sb("p0_xt", [128, 4, D], F32, st)
            self.load_ln(0)
            for (t0, n) in self.groups:
                src = self.x_p if t0 < L else self.x_s
                s0 = t0 if t0 < L else 0
                self.load_x(st, src, s0, n, xt)
                for i, m_ in self.subs(n):
                    v = xt.part(i, (slice(None), i, slice(None)))
                    self.ln_rows(v, m_, self.gB, self.bB, v, LN_EPS)
                    P.dma("sp", self.xrow(self.xA, t0 + i * 128, m_), xt.part(i, (slice(0, m_), i, slice(None))))
        P.barrier()

    def build(self):
        self.setup()
        stages = self.cfg.get("stages", "0ab23")
        if "0" in stages:
            self.pass0()
        for l in self.cfg.get("layers", range(DEPTH)):
            if "a" in stages and "s" not in self.cfg.get("skip", ""):
                self.pass1a(l, self.xA, "s")
            if "a" in stages and "r" not in self.cfg.get("skip", ""):
                self.pass1a(l, self.xA, "r")
            if "b" in stages:
                self.pass1b(l, self.xA)
            if "2" in stages:
                self.pass2(l, self.xA, self.xB)
            if "3" in stages or "x" in stages:
                self.pass3a(l, self.xB)
            if "3" in stages or "y" in stages:
                last = l == DEPTH - 1
                self.pass3b(l, self.xB, self.y_p if last else self.xA, self.y_s if last else self.xA)
        self.P.finish()
        return self.nc


def _fm(v, nch):
    v = np.asarray(v)
    lead = v.shape[:-1]
    t = v.reshape(lead + (nch, 128))
    t = np.moveaxis(t, -1, 0)
    t = np.moveaxis(t, -1, 1)
    return np.ascontiguousarray(t)


def host_inputs(inp, cfg, m, ncores, prompt_b, nsb=None):
    L = cfg["L"]
    f32 = np.float32
    shared = {}
    shared["c_ident"] = np.eye(128, dtype=f32)
    shared["w_in"] = np.ascontiguousarray(inp["w_in"], dtype=f32)
    shared["proj"] = np.ascontiguousarray(np.concatenate([inp["proj_ssm"], inp["proj_rwkv"], inp["proj_attn"]], axis=1), dtype=f32)
    shared["w_o"] = np.ascontiguousarray(inp["w_o"], dtype=f32)
    shared["ffn_w_up"] = np.ascontiguousarray(inp["ffn_w_up"], dtype=f32)
    shared["ffn_w_down"] = np.ascontiguousarray(inp["ffn_w_down"], dtype=f32)
    rows = [inp["ln_in_g"], inp["ln_in_b"]]
    for l in range(DEPTH):
        rows += [inp["ln1_g"][l], inp["ln1_b"][l], inp["ln2_g"][l], inp["ln2_b"][l]]
    shared["lnrows"] = np.ascontiguousarray(np.stack(rows), dtype=f32)
    cw = np.concatenate([inp["ffn_conv_w"], inp["ffn_conv_b"][:, None, :]], axis=1)
    shared["convp"] = np.ascontiguousarray(np.stack([np.moveaxis(_fm(cw[l], NFC), 2, 3)[:, :, :] for l in range(DEPTH)]), dtype=f32) \
        if False else np.ascontiguousarray(np.stack([_fm(cw[l], NFC) for l in range(DEPTH)]), dtype=f32)
    s5B = np.zeros((DEPTH, 2, 8, 128, 128), f32)
    s5C = np.zeros((DEPTH, 2, 8, 128, 128), f32)
    for l in range(DEPTH):
        for ri, (bm, cm) in enumerate(((inp["ssm_b_re"], inp["ssm_c_re"]), (inp["ssm_b_im"], inp["ssm_c_im"]))):
            for j in range(8):
                for gl in range(2):
                    g = 2 * j + gl
                    r0 = 32 * (j % 4) + 16 * gl
                    s5B[l, ri, j, r0:r0 + 16, gl * 64:(gl + 1) * 64] = bm[l, g].T
                    s5C[l, ri, j, gl * 64:(gl + 1) * 64, r0:r0 + 16] = cm[l, g].T
    shared["s5B"] = s5B
    shared["s5C"] = s5C
    shared["ssm_w_glu"] = np.ascontiguousarray(inp["ssm_w_glu"], dtype=f32)
    def qj(a):
        return np.ascontiguousarray(np.asarray(a).reshape(8, 2, 64).transpose(1, 2, 0).reshape(128, 8))
    shared["s5p"] = np.ascontiguousarray(np.stack([np.stack([qj(inp["ssm_a_re"][l]), qj(inp["ssm_a_im"][l]),
                                                             qj(np.repeat(inp["ssm_log_dt"][l][:, None], 64, 1))], 1) for l in range(DEPTH)]), dtype=f32)
    shared["s5cols"] = np.ascontiguousarray(np.stack([np.stack([_fm(inp["ssm_d"][l].reshape(-1), 2), _fm(inp["ssm_b_glu"][l], 2)], 1)
                                                      for l in range(DEPTH)]), dtype=f32)
    rc = np.zeros((DEPTH, 128, 21), f32)
    for l in range(DEPTH):
        rc[l, :, 0:7] = _fm(inp["rwkv_mu"][l], 7)
        for p, nm in enumerate(("rwkv_w0", "rwkv_a0", "rwkv_k_k", "rwkv_k_a", "rwkv_r_k", "rwkv_lnx_g", "rwkv_lnx_b")):
            rc[l, :, 7 + 2 * p:9 + 2 * p] = _fm(np.asarray(inp[nm][l]).reshape(-1), 2)
    shared["rwcols"] = rc
    shared["rwrows"] = np.ascontiguousarray(np.stack([r for l in range(DEPTH) for r in (inp["rwkv_lnx_g"][l], inp["rwkv_lnx_b"][l])]), dtype=f32)
    shared["lora"] = np.ascontiguousarray(np.stack([np.concatenate([inp["rwkv_w2"][l], inp["rwkv_a2"][l], inp["rwkv_g2"][l]], 0) for l in range(DEPTH)]), dtype=f32)
    ii = np.arange(128)
    su = (ii[:, None] < ii[None, :]).astype(f32)
    iu = (ii[:, None] <= ii[None, :]).astype(f32)
    slo = (ii[None, :] < ii[:, None]).astype(f32)
    shared["c_masks"] = np.ascontiguousarray(np.stack([su, iu, slo], 1))
    shared["c_blk"] = ((ii[:, None] // 64) == (ii[None, :] // 64)).astype(f32)
    shared["c_hsel"] = ((ii[:, None] // 64) == np.arange(2)[None, :]).astype(f32)
    NPG = cfg["NPG"]
    half = 8
    inv = (np.float32(ROPE_THETA) ** (-np.arange(half, dtype=f32) / np.float32(half))).astype(f32)
    def rope_tab(pos):
        ang = pos.astype(f32)[:, None] * inv[None, :]
        cs_, sn_ = np.cos(ang).astype(f32), np.sin(ang).astype(f32)
        C = np.ones((64, len(pos)), f32); S = np.zeros((64, len(pos)), f32)
        C[0:8] = cs_.T; C[8:16] = cs_.T; S[0:8] = sn_.T; S[8:16] = sn_.T
        return np.ascontiguousarray(np.concatenate([C, C], 0)), np.ascontiguousarray(np.concatenate([S, S], 0))
    shared["ropeC"], shared["ropeS"] = rope_tab(np.arange(L))
    shared["ropeCs"], shared["ropeSs"] = rope_tab(np.full((NS,), NPG * PAGE))
    rot = np.zeros((128, 128), f32)
    for b0 in (0, 64):
        for d_ in range(8):
            rot[b0 + d_ + 8, b0 + d_] = -1.0
            rot[b0 + d_, b0 + d_ + 8] = 1.0
    shared["c_rot"] = rot
    pk = np.arange(128)[:, None, None]; kt_ = np.arange(2)[None, :, None]; qq = (np.arange(512) % 256)[None, None, :]
    shared["c_causal"] = np.where(kt_ * 128 + pk <= qq, 1.0, 0.0).astype(f32)
    oh = np.zeros((32, 32, 128), f32)
    for n_ in range(32):
        oh[n_, n_, :] = 1.0
    shared["c_onehot"] = oh.reshape(32, 32 * 128)
    shared["c_pair"] = ((np.arange(128)[:, None] // 2) == np.arange(64)[None, :]).astype(f32) * np.float32(1.0 / 256)
    shared["c_pairT"] = ((np.arange(128)[None, :] // 2) == np.arange(64)[:, None]).astype(f32)
    npool = inp["cache_k"].shape[1]
    shared["ck_flat"] = np.ascontiguousarray(inp["cache_k"], dtype=f32).reshape(DEPTH * npool * 16, 2048)
    shared["cv_flat"] = np.ascontiguousarray(inp["cache_v"], dtype=f32).reshape(DEPTH * npool * 16, 2048)
    maps = []
    for c in range(ncores):
        d = dict(shared)
        sl = slice(c * NS, (c + 1) * NS)
        d["x_p"] = np.ascontiguousarray(inp["x_prompt"][prompt_b[c]], dtype=f32)
        d["x_s"] = np.ascontiguousarray(inp["x_sample"][sl, 0], dtype=f32)
        sre, sim = inp["state_ssm_re"][:, sl], inp["state_ssm_im"][:, sl]
        d["s5s0"] = np.ascontiguousarray(np.stack([np.stack([np.stack([qj(t[l, s_]) for s_ in range(NS)], -1) for t in (sre, sim)], 1)
                                                   for l in range(DEPTH)]), dtype=f32)
        d["rw_sh0"] = np.ascontiguousarray(np.stack([_fm(inp["state_rwkv_shift"][l, sl], 7) for l in range(DEPTH)]), dtype=f32)
        d["st_rwkv"] = np.ascontiguousarray(inp["state_rwkv"][:, sl].reshape(DEPTH, NS, 2, 128, 64), dtype=f32)
        d["ptT"] = np.ascontiguousarray(inp["page_table"][sl].T, dtype=np.int32)
        sc = inp["state_conv"][:, sl]
        d["conv0"] = np.ascontiguousarray(np.stack([_fm(sc[l], NFC) for l in range(DEPTH)]), dtype=f32)
        maps.append({k: v for k, v in d.items() if k in m.inputs})
        for k in m.inputs:
            if k in maps[-1]:
                assert tuple(maps[-1][k].shape) == m.inputs[k][0], (k, maps[-1][k].shape, m.inputs[k][0])
    return maps


def host_outputs(results, cfg, ncores, prompt_b):
    L = cfg["L"]
    nbp = len(set(prompt_b))
    first = {b: prompt_b.index(b) for b in set(prompt_b)}
    def P_(name, fn=lambda a: a):
        if name not in results[0]:
            return None
        return np.stack([fn(results[first[b]][name]) for b in range(nbp)])
    def S_(name, fn=lambda a: a):
        if name not in results[0]:
            return None
        return np.concatenate([fn(results[c][name]) for c in range(ncores)], axis=0)
    def unfm(a):
        a = np.moveaxis(a, 0, -1)
        a = np.moveaxis(a, 0, -2)
        return a.reshape(a.shape[:-2] + (-1,))
    out = [None] * 16
    def unqj(a):
        return a.reshape(2, 64, 8).transpose(2, 0, 1).reshape(16, 64)
    for idx, nm in ((6, "sre_p"), (7, "sim_p")):
        t = P_(nm, lambda a: np.stack([unqj(a[l]) for l in range(DEPTH)]))
        out[idx] = None if t is None else np.ascontiguousarray(np.moveaxis(t, 0, 1))
    for idx, nm in ((8, "sre_s"), (9, "sim_s")):
        if nm in results[0]:
            t = np.concatenate([np.stack([np.stack([unqj(results[c][nm][l][:, :, s_]) for s_ in range(NS)]) for l in range(DEPTH)])
                                for c in range(ncores)], axis=1)
            out[idx] = np.ascontiguousarray(t)
    t = P_("rw_p")
    if t is not None:
        B_ = t.shape[0]
        t = t.reshape(B_, DEPTH, 2, 64, 2, 64)
        out[10] = np.ascontiguousarray(t.transpose(1, 0, 4, 2, 5, 3).reshape(DEPTH, B_, 4, 64, 64))
    if "rw_s" in results[0]:
        out[11] = np.ascontiguousarray(np.concatenate([results[c]["rw_s"].reshape(DEPTH, NS, 4, 64, 64) for c in range(ncores)], axis=1))
    t = P_("sh_p", lambda a: np.stack([unfm(a[l]) for l in range(DEPTH)]))
    out[12] = None if t is None else np.ascontiguousarray(np.moveaxis(t, 0, 1))
    if "sh_s" in results[0]:
        out[13] = np.ascontiguousarray(np.concatenate([np.stack([unfm(results[c]["sh_s"][l]) for l in range(DEPTH)]) for c in range(ncores)], axis=1))
    t = P_("k_p")
    out[2] = None if t is None else np.ascontiguousarray(t.transpose(1, 0, 4, 2, 3))
    t = P_("v_p")
    out[3] = None if t is None else np.ascontiguousarray(np.moveaxis(t, 0, 1).reshape(DEPTH, t.shape[0], L, 4, 64))
    if "k_s" in results[0]:
        out[4] = np.ascontiguousarray(np.concatenate([results[c]["k_s"].transpose(0, 3, 1, 2) for c in range(ncores)], axis=1))[:, :, None]
        out[5] = np.ascontiguousarray(np.concatenate([results[c]["v_s"].reshape(DEPTH, NS, 4, 64) for c in range(ncores)], axis=1))[:, :, None]
    out[0] = P_("y_p")
    t = S_("y_s")
    out[1] = None if t is None else t[:, None, :]
    t = P_("cv_p", lambda a: np.stack([unfm(a[l]) for l in range(DEPTH)]))
    out[14] = None if t is None else np.ascontiguousarray(np.moveaxis(t, 0, 1))
    if "cv_s" in results[0]:
        t = np.concatenate([np.stack([unfm(results[c]["cv_s"][l]) for l in range(DEPTH)], 0) for c in range(ncores)], axis=1)
        out[15] = np.ascontiguousarray(t)
    return out


def _pass1a(self, l, xsrc, which):
    P = self.P
    L = self.L
    PS = self.PS
    TS = 256
    with ExitStack() as st:
        sb = lambda name, shape, dt=F32: P.sb("a_" + name, shape, dt, st)
        wuc = sb("wuc", [128, 8, 1152], BF16)
        for k in range(8):
            for c0 in (0, 576):
                P.dma("pool", wuc[:, k, c0:c0 + 576], self.w_in[l, k * 128:(k + 1) * 128, O_SSM + c0:O_SSM + c0 + 576])
        xt = sb("xt", [128, 4, D]); xT = sb("xT", [128, 8, 512], BF16)
        if which == "r":
            rw = self.rwkv_setup(l, st)
            for (t0, n) in self.groups:
                self.load_x(st, xsrc, t0, n, xt)
                self.make_xT(xt, n, xT)
                self.rwkv_group(l, rw, t0, n, xT, wuc)
            P.barrier()
            return
        WB = sb("WB", [128, 2, 8, 128], BF16)
        WC = sb("WC", [128, 2, 8, 128], BF16)
        for ri in range(2):
            P.dma("pool", WB[:, ri, :, :], self.s5B.v(self.s5B.t[l, ri, :, :, :].rearrange("j k m -> k j m")))
            P.dma("pool", WC[:, ri, :, :], self.s5C.v(self.s5C.t[l, ri, :, :, :].rearrange("j k m -> k j m")))
        wglu = sb("wglu", [128, 2, 256], BF16)
        for k in range(2):
            P.dma("pool", wglu[:, k, :], self.w_glu[l, k * 128:(k + 1) * 128, :])
        sp_ = sb("s5p", [128, 3, 8])
        P.dma("sp", sp_[:, :, :], self.s5p[l, :, :, :])
        cols = sb("s5cols", [128, 2, 2])
        P.dma("sp", cols[:, :, :], self.s5cols[l, :, :, :])
        s0 = sb("s5s0", [128, 2, 8, NS])
        P.dma("sp", s0[:, :, :, :], self.s5s0[l, :, :, :, :])
        dt_ = sb("dt", [128, 8]); ard = sb("ard", [128, 8]); th = sb("th", [128, 8])
        rho = sb("rho", [128, 8]); nq = sb("nq", [128, 8]); tq = sb("tq", [128, 8])
        cs = sb("cs", [128, 8]); sn = sb("sn", [128, 8]); abr = sb("abr", [128, 8]); abi = sb("abi", [128, 8])
        cfr = sb("cfr", [128, 8]); cfi = sb("cfi", [128, 8]); cfin = sb("cfin", [128, 8]); den = sb("den", [128, 8])
        t8 = [sb("t8_%d" % i, [128, 8]) for i in range(3)]
        are, aim = sp_[:, 0, :], sp_[:, 1, :]
        P.act(dt_[:, :], sp_[:, 2, :], AF.Exp)
        P.tt("dve", ard[:, :], are, dt_[:, :], ALU.mult)
        P.tt("dve", th[:, :], aim, dt_[:, :], ALU.mult)
        P.act(rho[:, :], ard[:, :], AF.Exp)
        P.memset("dve", nq[:, :], 0.0)
        for mlt in (1, 3, 5, 7, 9, 11):
            P.ts("dve", tq[:, :], th[:, :], float(mlt * math.pi), None, op0=ALU.is_gt)
            P.tt("dve", nq[:, :], nq[:, :], tq[:, :], ALU.add)
        C1 = 6.28125
        C2 = 2.0 * math.pi - C1
        P.stt(tq[:, :], nq[:, :], -C1, th[:, :], ALU.mult, ALU.add)
        P.stt(tq[:, :], nq[:, :], -C2, tq[:, :], ALU.mult, ALU.add)
        P.act(sn[:, :], tq[:, :], AF.Sin)
        P.ts("dve", t8[1][:, :], tq[:, :], -1.0, None, op0=ALU.mult)
        P.tt("dve", t8[0][:, :], tq[:, :], t8[1][:, :], ALU.max)
        P.ts("dve", t8[0][:, :], t8[0][:, :], -1.0, float(math.pi / 2), op0=ALU.mult, op1=ALU.add)
        P.act(cs[:, :], t8[0][:, :], AF.Sin)
        P.tt("dve", abr[:, :], rho[:, :], cs[:, :], ALU.mult)
        P.tt("dve", abi[:, :], rho[:, :], sn[:, :], ALU.mult)
        P.tt("dve", den[:, :], are, are, ALU.mult)
        P.tt("dve", t8[0][:, :], aim, aim, ALU.mult)
        P.tt("dve", den[:, :], den[:, :], t8[0][:, :], ALU.add)
        P.recip(den[:, :], den[:, :])
        P.ts("dve", t8[0][:, :], abr[:, :], -1.0, None, op0=ALU.add)
        P.tt("dve", t8[1][:, :], t8[0][:, :], are, ALU.mult)
        P.tt("dve", t8[2][:, :], abi[:, :], aim, ALU.mult)
        P.tt("dve", t8[1][:, :], t8[1][:, :], t8[2][:, :], ALU.add)
        P.tt("dve", cfr[:, :], t8[1][:, :], den[:, :], ALU.mult)
        P.tt("dve", t8[1][:, :], abi[:, :], are, ALU.mult)
        P.tt("dve", t8[2][:, :], t8[0][:, :], aim, ALU.mult)
        P.tt("dve", t8[1][:, :], t8[1][:, :], t8[2][:, :], ALU.subtract)
        P.tt("dve", cfi[:, :], t8[1][:, :], den[:, :], ALU.mult)
        P.ts("dve", cfin[:, :], cfi[:, :], -1.0, None, op0=ALU.mult)
        Ec = sb("Ec", [128, 8, TS]); Es = sb("Es", [128, 8, TS])
        tA = sb("tA", [128, 8, TS // 2]); tB = sb("tB", [128, 8, TS // 2])
        P.copy("dve", Ec[:, :, 0], cs[:, :])
        P.copy("dve", Es[:, :, 0], sn[:, :])
        s_ = 1
        while s_ < TS:
            cb = Ec.v(bc_last(Ec.t[:, :, s_ - 1], s_))
            sbb = Es.v(bc_last(Es.t[:, :, s_ - 1], s_))
            P.tt("dve", tA[:, :, 0:s_], Ec[:, :, 0:s_], cb, ALU.mult)
            P.tt("dve", tB[:, :, 0:s_], Es[:, :, 0:s_], sbb, ALU.mult)
            P.tt("dve", tA[:, :, 0:s_], tA[:, :, 0:s_], tB[:, :, 0:s_], ALU.subtract)
            P.tt("dve", tB[:, :, 0:s_], Ec[:, :, 0:s_], sbb, ALU.mult)
            P.tt("dve", Es[:, :, s_:2 * s_], Es[:, :, 0:s_], cb, ALU.mult)
            P.tt("dve", Es[:, :, s_:2 * s_], Es[:, :, s_:2 * s_], tB[:, :, 0:s_], ALU.add)
            P.copy("dve", Ec[:, :, s_:2 * s_], tA[:, :, 0:s_])
            s_ *= 2
        rhoT = sb("rhoT", [128, 8, TS])
        P.memset("dve", rhoT[:, :, :], 1.0)
        P.tt("dve", rhoT[:, :, :], rhoT[:, :, :], rho.v(bc_last(rho.t[:, :], TS)), ALU.mult)
        cj = [sb("s5carry%d" % j, [128, 2]) for j in range(8)]
        for j in range(8):
            P.memset("dve", cj[j][:, :], 0.0)

        uTb = sb("uTb", [128, 2, 512], BF16); uTf = sb("uTf", [128, 2, 512])
        dbl = {}
        for nm in ("xr", "xi", "tm1", "tm2", "hr", "hi", "zr", "zi", "sr", "si"):
            dbl[nm] = [sb(nm + str(i), [128, 512]) for i in range(2)]
        for nm in ("srb", "sib"):
            dbl[nm] = [sb(nm + str(i), [128, 512], BF16) for i in range(2)]
        yv = sb("yv", [128, 2, 512]); zf = sb("zf", [128, 2, 512]); zb = sb("zb", [128, 2, 512], BF16)
        sg = sb("sg", [128, 512]); ysb = sb("ysb", [128, 2, 512], BF16)

        def s5_group(t0, n):
            samp = n < 128
            for yc in range(2):
                ps = PS[2 + yc]
                for k in range(8):
                    P.mm(ps[:, 0:n], wuc[:, k, yc * 128:(yc + 1) * 128], xT[:, k, 0:n], k == 0, k == 7)
                P.copy("act", uTb[:, yc, 0:n], ps[:, 0:n])
                P.copy("dve", uTf[:, yc, 0:n], ps[:, 0:n])
            def chain(j):
                psA, psB = (PS[4], PS[5]) if j % 2 == 0 else (PS[0], PS[1])
                xr, xi, tm1, tm2, hr, hi, zr, zi, sr, si, srb, sib = [dbl[nm][j % 2] for nm in
                                                                       ("xr", "xi", "tm1", "tm2", "hr", "hi", "zr", "zi", "sr", "si", "srb", "sib")]
                P.mm(psA[:, 0:n], WB[:, 0, j, :], uTb[:, j // 4, 0:n])
                yield
                P.mm(psB[:, 0:n], WB[:, 1, j, :], uTb[:, j // 4, 0:n])
                yield
                P.act(tm1[:, 0:n], psB[:, 0:n], AF.Identity, scale=cfin[:, j:j + 1])
                yield
                P.act(tm2[:, 0:n], psA[:, 0:n], AF.Identity, scale=cfi[:, j:j + 1])
                yield
                P.stt(xr[:, 0:n], psA[:, 0:n], cfr[:, j:j + 1], tm1[:, 0:n], ALU.mult, ALU.add)
                yield
                P.stt(xi[:, 0:n], psB[:, 0:n], cfr[:, j:j + 1], tm2[:, 0:n], ALU.mult, ALU.add)
                yield
                if samp:
                    P.stt(xr[:, 0:n], s0[:, 0, j, :], abr[:, j:j + 1], xr[:, 0:n], ALU.mult, ALU.add)
                    yield
                    P.ts("dve", tm1[:, 0:n], s0[:, 1, j, :], abi[:, j:j + 1], -1.0, op0=ALU.mult, op1=ALU.mult)
                    yield
                    P.tt("dve", sr[:, 0:n], xr[:, 0:n], tm1[:, 0:n], ALU.add)
                    yield
                    P.stt(xi[:, 0:n], s0[:, 1, j, :], abr[:, j:j + 1], xi[:, 0:n], ALU.mult, ALU.add)
                    yield
                    P.stt(si[:, 0:n], s0[:, 0, j, :], abi[:, j:j + 1], xi[:, 0:n], ALU.mult, ALU.add)
                    yield
                    P.dma("sp", self.sre_s[l, :, j, :], sr[:, 0:n])
                    yield
                    P.dma("sp", self.sim_s[l, :, j, :], si[:, 0:n])
                    yield
                else:
                    for sg_ in range(n // TS):
                        c_ = slice(sg_ * TS, (sg_ + 1) * TS)
                        ec, es = Ec[:, j, :], Es[:, j, :]
                        P.tt("pool", hr[:, c_], xr[:, c_], ec, ALU.mult)
                        yield
                        P.tt("pool", tm1[:, c_], xi[:, c_], es, ALU.mult)
                        yield
                        P.tt("pool", hr[:, c_], hr[:, c_], tm1[:, c_], ALU.add)
                        yield
                        P.tt("pool", hi[:, c_], xi[:, c_], ec, ALU.mult)
                        yield
                        P.tt("pool", tm2[:, c_], xr[:, c_], es, ALU.mult)
                        yield
                        P.tt("pool", hi[:, c_], hi[:, c_], tm2[:, c_], ALU.subtract)
                        yield
                        P.scan(zr[:, c_], rhoT[:, j, :], hr[:, c_], cj[j][:, 0:1], ALU.mult, ALU.add)
                        yield
                        P.scan(zi[:, c_], rhoT[:, j, :], hi[:, c_], cj[j][:, 1:2], ALU.mult, ALU.add)
                        yield
                        P.tt("dve", sr[:, c_], zr[:, c_], ec, ALU.mult)
                        yield
                        P.tt("dve", zr[:, c_], zr[:, c_], es, ALU.mult)
                        yield
                        P.tt("dve", si[:, c_], zi[:, c_], ec, ALU.mult)
                        yield
                        P.tt("dve", zi[:, c_], zi[:, c_], es, ALU.mult)
                        yield
                        P.tt("dve", sr[:, c_], sr[:, c_], zi[:, c_], ALU.subtract)
                        yield
                        P.tt("dve", si[:, c_], si[:, c_], zr[:, c_], ALU.add)
                        yield
                        P.copy("act", cj[j][:, 0:1], sr[:, (sg_ + 1) * TS - 1:(sg_ + 1) * TS])
                        yield
                        P.copy("act", cj[j][:, 1:2], si[:, (sg_ + 1) * TS - 1:(sg_ + 1) * TS])
                        yield
                P.copy("act", srb[:, 0:n], sr[:, 0:n])
                yield
                P.act(sib[:, 0:n], si[:, 0:n], AF.Copy, scale=-1.0)
                yield
                psy = PS[6 + j // 4]
                P.mm(psy[:, 0:n], WC[:, 0, j, :], srb[:, 0:n], j % 4 == 0, False)
                yield
                P.mm(psy[:, 0:n], WC[:, 1, j, :], sib[:, 0:n], False, j % 4 == 3)
                yield
            for j0 in range(0, 8, 2):
                g0, g1 = chain(j0), chain(j0 + 1)
                alive = [g0, g1]
                while alive:
                    for g_ in list(alive):
                        try:
                            next(g_)
                        except StopIteration:
                            alive.remove(g_)
            for yc in range(2):
                P.stt(yv[:, yc, 0:n], uTf[:, yc, 0:n], cols[:, 0, yc:yc + 1], PS[6 + yc][:, 0:n], ALU.mult, ALU.add)
                P.act(zf[:, yc, 0:n], yv[:, yc, 0:n], AF.Gelu_apprx_tanh)
                P.copy("dve", zb[:, yc, 0:n], zf[:, yc, 0:n])
            for yc in range(2):
                ps = PS[2 + yc]
                for k in range(2):
                    P.mm(ps[:, 0:n], wglu[:, k, yc * 128:(yc + 1) * 128], zb[:, k, 0:n], k == 0, k == 1)
                P.act(sg[:, 0:n], ps[:, 0:n], AF.Sigmoid, bias=cols[:, 1, yc:yc + 1])
                P.tt("dve", ysb[:, yc, 0:n], zf[:, yc, 0:n], sg[:, 0:n], ALU.mult)
                P.dma("sp", self.yT[l].part(("g", t0, yc), (yc, slice(None), slice(t0, t0 + n))), ysb[:, yc, 0:n])

        for (t0, n) in self.groups:
            self.load_x(st, xsrc, t0, n, xt)
            self.make_xT(xt, n, xT)
            s5_group(t0, n)
            if t0 + n == L:
                for j in range(8):
                    P.dma("sp", self.sre_p[l, :, j:j + 1], cj[j][:, 0:1], allow_slow_non_contiguous=True)
                    P.dma("sp", self.sim_p[l, :, j:j + 1], cj[j][:, 1:2], allow_slow_non_contiguous=True)
    P.barrier()


Model.pass1a = _pass1a


def _rwkv_setup(self, l, st):
    P = self.P
    rw = Ctx()
    sb = lambda name, shape, dt=F32: P.sb("r_" + name, shape, dt, st)
    rw.sb = sb
    rw.cols = sb("cols", [128, 21])
    P.dma("sp", rw.cols[:, :], self.rwcols[l, :, :])
    col = lambda p, hc: rw.cols[:, 7 + 2 * p + hc:8 + 2 * p + hc]
    rw.col = col
    rw.negw0 = sb("negw0", [128, 2])
    P.ts("dve", rw.negw0[:, :], rw.cols[:, 7:9], -1.0, None, op0=ALU.mult)
    rw.omka = sb("omka", [128, 2])
    P.ts("dve", rw.omka[:, :], rw.cols[:, 13:15], -1.0, 1.0, op0=ALU.mult, op1=ALU.add)
    rw.nhalf = sb("nhalf", [128, 1]); P.memset("dve", rw.nhalf[:, :], -0.5)
    rw.one = sb("one", [128, 1]); P.memset("dve", rw.one[:, :], 1.0)
    rw.lora = sb("lora", [128, 256], BF16)
    P.dma("pool", rw.lora[:, :], self.lora[l, :, :])
    rw.gnB = sb("gnB", [128, 256]); rw.bnB = sb("bnB", [128, 256])
    self.bcast_row_load(rw.gnB, self.rwrows, (l * 2) * 256, 256)
    self.bcast_row_load(rw.bnB, self.rwrows, (l * 2 + 1) * 256, 256)
    rw.masks = sb("masks", [128, 3, 128])
    P.dma("sp", rw.masks[:, :, :], self.c_masks[:, :, :])
    rw.blk = sb("blk", [128, 128]); P.dma("sp", rw.blk[:, :], self.c_blk[:, :])
    rw.hsel = sb("hsel", [128, 2]); P.dma("sp", rw.hsel[:, :], self.c_hsel[:, :])
    rw.ones = sb("ones", [128, 128]); P.memset("dve", rw.ones[:, :], 1.0)
    rw.cfull = sb("cfull", [128, 7, 513]); P.memset("dve", rw.cfull[:, :, :], 0.0)
    rw.H32 = sb("H32", [128, 2, 64]); P.memset("dve", rw.H32[:, :, :], 0.0)
    rw.Hb = sb("Hb", [128, 2, 64], BF16); P.memset("dve", rw.Hb[:, :, :], 0.0)
    rw.sh0 = sb("sh0", [128, 7, NS]); P.dma("sp", rw.sh0[:, :, :], self.rw_sh0[l, :, :, :])
    rw.cf = sb("cf", [128, 7, 512])
    rw.dtmp = sb("dtmp", [128, 512])
    rw.lo = sb("lo", [128, 512], BF16)
    for nm in ("e1", "e2", "cl", "clx", "kk", "sq", "rs", "tt1"):
        setattr(rw, nm, sb(nm, [128, 512]))
    for nm in ("eg", "egx", "egi", "a", "kkn", "km", "A32", "B32", "pr"):
        setattr(rw, nm, sb(nm, [128, 2, 512]))
    rw.ARb = sb("ARb", [128, 2, 4, 2, 128], BF16)
    rw.Bb = sb("Bb", [128, 2, 512], BF16); rw.Kb = sb("Kb", [128, 2, 512], BF16); rw.vb = sb("vb", [128, 2, 512], BF16)
    rw.Btm = sb("Btm", [128, 2, 4, 128], BF16); rw.Ktm = sb("Ktm", [128, 2, 4, 128], BF16); rw.Vtm = sb("Vtm", [128, 2, 4, 128], BF16)
    rw.X = [sb("X%d" % h, [128, 128], F32R) for h in range(4)]
    rw.XT = [sb("XT%d" % h, [128, 128], F32R) for h in range(4)]
    rw.N = [sb("N%d" % h, [128, 128], F32R) for h in range(4)]
    rw.IXT = [sb("IXT%d" % h, [128, 128], F32R) for h in range(4)]
    rw.tmpX = [sb("tmpX%d" % h, [128, 128]) for h in range(4)]
    rw.ArbT = [sb("ArbT%d" % h, [128, 128], BF16) for h in range(4)]
    rw.AkT = [sb("AkT%d" % h, [128, 256], BF16) for h in range(4)]
    rw.W32 = [sb("W32%d" % h, [128, 64], F32R) for h in range(4)]
    rw.Ub = [sb("Ub%d" % h, [128, 64], BF16) for h in range(4)]
    rw.bst = sb("bst", [128, 24]); rw.bmv = sb("bmv", [128, 4, 2]); rw.sd = sb("sd", [128, 4]); rw.rstd = sb("rstd", [128, 4])
    rw.yn = sb("yn", [128, 256]); rw.yr = sb("yr", [128, 256]); rw.bon = sb("bon", [128, 16])
    rw.yrT = sb("yrT", [128, 2, 512], BF16)
    rw.rows = sb("rows", [128, 5, 64]); rw.S = sb("S", [128, 64]); rw.S1 = sb("S1", [128, 64]); rw.tS = sb("tS", [128, 64])
    rw.sa = sb("sa", [128, 1]); rw.ys = sb("ys", [128, 2, NS]); rw.vec = sb("vec", [128, 5, 2, NS])
    return rw


def _rwkv_group(self, l, rw, t0, n, xT, wuc):
    P = self.P
    PS = self.PS
    L = self.L
    samp = n < 128
    cf, cfull, lo = rw.cf, rw.cfull, rw.lo
    col = rw.col
    for ch in range(7):
        ps = PS[2 + ch % 2]
        for k in range(8):
            P.mm(ps[:, 0:n], wuc[:, k, 256 + ch * 128:256 + (ch + 1) * 128], xT[:, k, 0:n], k == 0, k == 7)
        if samp:
            P.copy("act", cfull[:, ch, 1:1 + n], ps[:, 0:n])
            P.tt("dve", rw.dtmp[:, 0:n], rw.sh0[:, ch, :], cfull[:, ch, 1:1 + n], ALU.subtract)
        else:
            P.copy("act", cfull[:, ch, 1:1 + n], ps[:, 0:n])
            P.tt("pool", rw.dtmp[:, 0:n], cfull[:, ch, 0:n], cfull[:, ch, 1:1 + n], ALU.subtract)
        P.stt(cf[:, ch, 0:n], rw.dtmp[:, 0:n], rw.cols[:, ch:ch + 1], cfull[:, ch, 1:1 + n], ALU.mult, ALU.add)
    if samp:
        P.dma("sp", self.sh_s[l, :, :, :], cfull[:, :, 1:1 + n])
    else:
        P.copy("act", cfull[:, :, 0:1], cfull[:, :, n:n + 1])
        if t0 + n == L:
            P.copy("act", rw.bon[:, 0:7], cfull[:, :, 0])
            P.dma("sp", self.sh_p[l, :, :], rw.bon[:, 0:7])
    P.act(lo[0:32, 0:n], cf[0:32, 6, 0:n], AF.Tanh)
    P.copy("act", lo[32:64, 0:n], cf[32:64, 6, 0:n])
    P.act(lo[64:128, 0:n], cf[64:128, 6, 0:n], AF.Sigmoid)
    for hc in range(2):
        hs = slice(hc * 128, (hc + 1) * 128)
        r_, k_, v_ = cf[:, hc, 0:n], cf[:, 2 + hc, 0:n], cf[:, 4 + hc, 0:n]
        psw = PS[2]
        P.mm(psw[:, 0:n], rw.lora[0:32, hs], lo[0:32, 0:n])
        P.act(rw.e1[:, 0:n], psw[:, 0:n], AF.Exp, bias=rw.negw0[:, hc:hc + 1], scale=-1.0)
        P.act(rw.e1[:, 0:n], rw.e1[:, 0:n], AF.Ln, bias=rw.one[:, 0:1])
        P.act(rw.e2[:, 0:n], rw.e1[:, 0:n], AF.Exp, bias=rw.nhalf[:, 0:1], scale=-1.0)
        psa = PS[3]
        P.mm(psa[:, 0:n], rw.lora[32:64, hs], lo[32:64, 0:n])
        P.act(rw.a[:, hc, 0:n], psa[:, 0:n], AF.Sigmoid, bias=col(1, hc))
        P.ts("pool", rw.kk[:, 0:n], k_, col(2, hc), None, op0=ALU.mult)
        P.tt("pool", rw.sq[:, 0:n], rw.kk[:, 0:n], rw.kk[:, 0:n], ALU.mult)
        pss = PS[4]
        P.mm(pss[:, 0:n], rw.blk[:, :], rw.sq[:, 0:n])
        P.ts("dve", rw.rs[:, 0:n], pss[:, 0:n], 1e-24, None, op0=ALU.max)
        P.act(rw.rs[:, 0:n], rw.rs[:, 0:n], AF.Sqrt)
        P.recip(rw.rs[:, 0:n], rw.rs[:, 0:n])
        P.tt("dve", rw.kkn[:, hc, 0:n], rw.kk[:, 0:n], rw.rs[:, 0:n], ALU.mult)
        P.ts("dve", rw.tt1[:, 0:n], rw.a[:, hc, 0:n], col(3, hc), rw.omka[:, hc:hc + 1], op0=ALU.mult, op1=ALU.add)
        P.tt("dve", rw.km[:, hc, 0:n], k_, rw.tt1[:, 0:n], ALU.mult)
        P.stt(rw.pr[:, hc, 0:n], r_, col(4, hc), rw.km[:, hc, 0:n], ALU.mult, ALU.mult)
        if samp:
            P.act(rw.vec[:, 0, hc, :], rw.e2[:, 0:n], AF.Exp, scale=-1.0)
            P.ts("dve", rw.vec[:, 1, hc, :], rw.kkn[:, hc, 0:n], -1.0, None, op0=ALU.mult)
            P.tt("dve", rw.vec[:, 2, hc, :], rw.kkn[:, hc, 0:n], rw.a[:, hc, 0:n], ALU.mult)
            P.copy("dve", rw.vec[:, 3, hc, :], rw.km[:, hc, 0:n])
            P.copy("dve", rw.vec[:, 4, hc, :], r_)
            continue
        for c in range(4):
            cs = slice(c * 128, (c + 1) * 128)
            P.scan(rw.cl[:, cs], rw.ones[:, :], rw.e2[:, cs], 0.0, ALU.mult, ALU.add)
        P.tt("pool", rw.clx[:, 0:n], rw.cl[:, 0:n], rw.e2[:, 0:n], ALU.subtract)
        P.act(rw.eg[:, hc, 0:n], rw.cl[:, 0:n], AF.Exp, scale=-1.0)
        P.act(rw.egx[:, hc, 0:n], rw.clx[:, 0:n], AF.Exp, scale=-1.0)
        P.act(rw.egi[:, hc, 0:n], rw.cl[:, 0:n], AF.Exp)
        P.stt(rw.A32[:, hc, 0:n], rw.kkn[:, hc, 0:n], -1.0, rw.egx[:, hc, 0:n], ALU.mult, ALU.mult)
        P.tt("pool", rw.tt1[:, 0:n], rw.kkn[:, hc, 0:n], rw.a[:, hc, 0:n], ALU.mult)
        P.tt("pool", rw.B32[:, hc, 0:n], rw.tt1[:, 0:n], rw.egi[:, hc, 0:n], ALU.mult)
        a3 = lambda buf, idx: buf.v(buf.t[:, hc, idx, :, :]) if False else None
        P.copy("act", rw.ARb.v(rw.ARb.t[:, hc, :, 0, :]), rw.A32.v(rw.A32.t[:, hc, :].rearrange("p (c t) -> p c t", c=4)))
        P.tt("dve", rw.ARb.v(rw.ARb.t[:, hc, :, 1, :]), cf.v(cf.t[:, hc, :].rearrange("p (c t) -> p c t", c=4)),
             rw.eg.v(rw.eg.t[:, hc, :].rearrange("p (c t) -> p c t", c=4)), ALU.mult)
        P.copy("act", rw.Bb[:, hc, 0:n], rw.B32[:, hc, 0:n])
        P.tt("dve", rw.Kb[:, hc, 0:n], rw.km[:, hc, 0:n], rw.egi[:, hc, 0:n], ALU.mult)
        P.copy("act", rw.vb[:, hc, 0:n], v_)
        psb = PS[7].v(PS[7].t[:, :].bitcast(BF16))
        for src, dst in ((rw.Bb, rw.Btm), (rw.Kb, rw.Ktm), (rw.vb, rw.Vtm)):
            for c in range(4):
                P.tr(psb[:, c * 128:(c + 1) * 128], src[:, hc, c * 128:(c + 1) * 128], self.identb[:, :])
            P.copy("dve", dst.v(dst.t[:, hc, :, :].rearrange("p c t -> p (c t)")), psb[:, 0:512])
        for c in range(4):
            o = (hc * 4 + c) * 2
            P.mm(PS[5][:, o:o + 2], rw.pr[:, hc, c * 128:(c + 1) * 128], rw.hsel[:, :])
    if samp:
        return self.rwkv_sample(l, rw, n)
    P.copy("act", rw.bon[:, :], PS[5][:, 0:16])
    mask2 = rw.masks.v(rw.masks.t[:, 0:2, :].rearrange("p a t -> p (a t)"))
    for c in range(4):
        cs = slice(c * 128, (c + 1) * 128)
        for h in range(4):
            hc, hl = divmod(h, 2)
            sl = slice(hl * 64, hl * 64 + 64)
            ps1, ps2, ps3, ps4 = PS[0], PS[1], PS[2], PS[3]
            P.mm(ps1[:, 0:128], rw.B32[sl, hc, cs], rw.A32[sl, hc, cs])
            P.tt("dve", rw.tmpX[h][:, :], ps1[:, 0:128], rw.masks[:, 0, :], ALU.mult)
            P.copy("act", rw.X[h][:, :], rw.tmpX[h][:, :])
            P.tt("pool", rw.N[h][:, :], rw.tmpX[h][:, :], self.ident[:, :], ALU.add)
            P.mm(ps2[:, 0:128], rw.A32[sl, hc, cs], rw.B32[sl, hc, cs])
            P.tt("dve", rw.XT[h][:, :], ps2[:, 0:128], rw.masks[:, 2, :], ALU.mult)
            P.mm(ps3[:, 0:128], rw.Bb[sl, hc, cs], rw.ARb[sl, hc, c, 1, :])
            P.tt("dve", rw.ArbT[h][:, :], ps3[:, 0:128], rw.masks[:, 1, :], ALU.mult)
            P.mm(ps4[:, 0:256], rw.Kb[sl, hc, cs], rw.ARb.v(rw.ARb.t[sl, hc, c, :, :].rearrange("p a t -> p (a t)")))
            P.tt("dve", rw.AkT[h][:, :], ps4[:, 0:256], mask2, ALU.mult)
        for k in range(1, 7):
            for h in range(4):
                pa, pb, pc = PS[(3 * h) % 4], PS[(3 * h + 1) % 4], PS[(3 * h + 2) % 4]
                if k < 6:
                    P.mm(pa[:, 0:128], rw.XT[h][:, :], rw.X[h][:, :])
                P.mm(pb[:, 0:128], rw.X[h][:, :], rw.XT[h][:, :])
                if k < 6:
                    P.copy("act", rw.X[h][:, :], pa[:, 0:128])
                    P.copy("dve", rw.XT[h][:, :], pb[:, 0:128])
                P.tt("dve", rw.IXT[h][:, :], pb[:, 0:128], self.ident[:, :], ALU.add)
                P.mm(pc[:, 0:128], rw.IXT[h][:, :], rw.N[h][:, :])
                P.copy("act", rw.N[h][:, :], pc[:, 0:128])
        for h in range(4):
            hc, hl = divmod(h, 2)
            sl = slice(hl * 64, hl * 64 + 64)
            vtm = rw.Vtm[:, hc, c, hl * 64:(hl + 1) * 64]
            pw, pu = PS[4], PS[0]
            P.mm(pw[:, 0:64], rw.ARb[sl, hc, c, 0, :], rw.Hb[sl, hc, :], True, False)
            P.mm(pw[:, 0:64], rw.AkT[h][:, 0:128], vtm, False, True)
            P.copy("act", rw.W32[h][:, :], pw[:, 0:64])
            P.mm(pu[:, 0:64], rw.N[h][:, :], rw.W32[h][:, :])
            P.copy("act", rw.Ub[h][:, :], pu[:, 0:64])
            py = PS[6]
            P.mm(py[:, h * 64:(h + 1) * 64], rw.ARb[sl, hc, c, 1, :], rw.Hb[sl, hc, :], True, False)
            P.mm(py[:, h * 64:(h + 1) * 64], rw.ArbT[h][:, :], rw.Ub[h][:, :], False, False)
            P.mm(py[:, h * 64:(h + 1) * 64], rw.AkT[h][:, 128:256], vtm, False, True)
        for hc in range(2):
            ph = PS[1]
            for hl in range(2):
                h = 2 * hc + hl
                vtm = rw.Vtm[:, hc, c, hl * 64:(hl + 1) * 64]
                P.mm(ph[hl * 64:(hl + 1) * 64, 0:64], rw.Btm[:, hc, c, hl * 64:(hl + 1) * 64], rw.Ub[h][:, :], True, False)
                P.mm(ph[hl * 64:(hl + 1) * 64, 0:64], rw.Ktm[:, hc, c, hl * 64:(hl + 1) * 64], vtm, False, True)
            gT = rw.eg[:, hc, c * 128 + 127:c * 128 + 128]
            P.ts("dve", rw.H32[:, hc, :], rw.H32[:, hc, :], gT, None, op0=ALU.mult)
            P.stt(rw.H32[:, hc, :], ph[:, 0:64], gT, rw.H32[:, hc, :], ALU.mult, ALU.add)
            P.copy("act", rw.Hb[:, hc, :], rw.H32[:, hc, :])
        py = PS[6]
        for h in range(4):
            P.emit("dve", lambda e, h=h: e.bn_stats(rw.bst.t[:, h * 6:(h + 1) * 6], py.t[:, h * 64:(h + 1) * 64]), [py[:, :]], [rw.bst[:, :]])
            P.emit("dve", lambda e, h=h: e.bn_aggr(rw.bmv.t[:, h, :], rw.bst.t[:, h * 6:(h + 1) * 6]), [rw.bst[:, :]], [rw.bmv[:, :, :]])
        P.act(rw.sd[:, :], rw.bmv[:, :, 1], AF.Sqrt, bias=self.epsc[GN_EPS][:, 0:1])
        P.recip(rw.rstd[:, :], rw.sd[:, :])
        for h in range(4):
            P.ts("dve", rw.yn[:, h * 64:(h + 1) * 64], py[:, h * 64:(h + 1) * 64], rw.bmv[:, h, 0:1], rw.rstd[:, h:h + 1], op0=ALU.subtract, op1=ALU.mult)
        P.tt("pool", rw.yn[:, :], rw.yn[:, :], rw.gnB[:, :], ALU.mult)
        P.tt("pool", rw.yn[:, :], rw.yn[:, :], rw.bnB[:, :], ALU.add)
        for h in range(4):
            hc, hl = divmod(h, 2)
            o = (hc * 4 + c) * 2 + hl
            P.stt(rw.yn[:, h * 64:(h + 1) * 64], rw.Vtm[:, hc, c, hl * 64:(hl + 1) * 64], rw.bon[:, o:o + 1], rw.yn[:, h * 64:(h + 1) * 64], ALU.mult, ALU.add)
        pg = PS[4]
        P.mm(pg[:, 0:256], lo[64:128, cs], rw.lora[64:128, :])
        P.tt("dve", rw.yr[:, :], rw.yn[:, :], pg[:, 0:256], ALU.mult)
        for hc in range(2):
            pt = PS[5 + 2 * hc]
            P.tr(pt[:, cs], rw.yr[:, hc * 128:(hc + 1) * 128], self.ident[:, :])
    for hc in range(2):
        P.copy("act", rw.yrT[:, hc, 0:n], PS[5 + 2 * hc][:, 0:n])
        P.dma("sp", self.yT[l].part(("g", t0, 2 + hc), (2 + hc, slice(None), slice(t0, t0 + n))), rw.yrT[:, hc, 0:n])
    if t0 + n == L:
        P.dma("sp", self.rw_p[l, :, :, :], rw.H32.v(rw.H32.t[:, :, :].rearrange("p a b -> p a b")) if False else rw.H32[:, :, :])


def _rwkv_sample(self, l, rw, n):
    P = self.P
    PS = self.PS
    L = self.L
    cf, lo, col = rw.cf, rw.lo, rw.col
    scr = self.rwscr[l]
    P.dma("sp", scr.v(scr.t[:, :, :, :].rearrange("v c s q -> q (v c s)")), rw.vec.v(rw.vec.t[:, :, :, :].rearrange("q v c s -> q (v c s)")), allow_slow_non_contiguous=True)
    for s_ in range(n):
        for hc in range(2):
            for hl in range(2):
                off = ((hc * NS + s_) * 128 + hl * 64)
                ap = bass.AP(scr.t, off, [[0, 64], [NS * 2 * 128, 5], [1, 64]])
                P.dma("sp", rw.rows[hl * 64:(hl + 1) * 64, :, :], scr.v(ap))
            P.dma("sp", rw.S[:, :], self.st_rwkv[l, s_, hc, :, :])
            P.tt("dve", rw.tS[:, :], rw.S[:, :], rw.rows[:, 1, :], ALU.mult)
            P.reduce(rw.sa[:, :], rw.tS[:, :], ALU.add)
            P.tt("dve", rw.S1[:, :], rw.S[:, :], rw.rows[:, 0, :], ALU.mult)
            P.stt(rw.S1[:, :], rw.rows[:, 2, :], rw.sa[:, 0:1], rw.S1[:, :], ALU.mult, ALU.add)
            P.stt(rw.S1[:, :], rw.rows[:, 3, :], cf[:, 4 + hc, s_:s_ + 1], rw.S1[:, :], ALU.mult, ALU.add)
            P.tt("dve", rw.tS[:, :], rw.S1[:, :], rw.rows[:, 4, :], ALU.mult)
            P.reduce(rw.ys[:, hc, s_:s_ + 1], rw.tS[:, :], ALU.add)
            P.dma("sp", self.rw_s[l, s_, hc, :, :], rw.S1[:, :])
    t = [rw.sb("st%d" % i, [128, NS]) for i in range(4)]
    for hc in range(2):
        hs = slice(hc * 128, (hc + 1) * 128)
        y = rw.ys[:, hc, :]
        pm = PS[2]
        P.mm(pm[:, 0:n], rw.blk[:, :], y)
        P.stt(t[0][:, :], pm[:, 0:n], -1.0 / 64, y, ALU.mult, ALU.add)
        P.tt("dve", t[1][:, :], t[0][:, :], t[0][:, :], ALU.mult)
        pv = PS[3]
        P.mm(pv[:, 0:n], rw.blk[:, :], t[1][:, :])
        P.act(t[1][:, :], pv[:, 0:n], AF.Sqrt, bias=self.epsc[GN_EPS][:, 0:1], scale=1.0 / 64)
        P.recip(t[1][:, :], t[1][:, :])
        P.tt("dve", t[0][:, :], t[0][:, :], t[1][:, :], ALU.mult)
        P.ts("dve", t[0][:, :], t[0][:, :], col(5, hc), col(6, hc), op0=ALU.mult, op1=ALU.add)
        pb = PS[4]
        P.mm(pb[:, 0:n], rw.blk[:, :], rw.pr[:, hc, 0:n])
        P.tt("dve", t[2][:, :], pb[:, 0:n], cf[:, 4 + hc, 0:n], ALU.mult)
        P.tt("dve", t[0][:, :], t[0][:, :], t[2][:, :], ALU.add)
        pg = PS[5]
        P.mm(pg[:, 0:n], rw.lora[64:128, hs], lo[64:128, 0:n])
        P.tt("dve", rw.yrT[:, hc, 0:n], t[0][:, :], pg[:, 0:n], ALU.mult)
        P.dma("sp", self.yT[l].part(("g", L, 2 + hc), (2 + hc, slice(None), slice(L, L + n))), rw.yrT[:, hc, 0:n])


Model.rwkv_setup = _rwkv_setup
Model.rwkv_group = _rwkv_group
Model.rwkv_sample = _rwkv_sample


def _pass1b(self, l, xsrc):
    P = self.P
    L, NPG = self.L, self.NPG
    PS = self.PS
    NBLK = L // 256
    with ExitStack() as st:
        sb = lambda name, shape, dt=F32: P.sb("m_" + name, shape, dt, st)
        wq = sb("wq", [128, 8, 512], BF16); wkd = sb("wkd", [128, 8, 4, 128], BF16); wv = sb("wv", [128, 8, 256], BF16)
        for k in range(8):
            rs_ = slice(k * 128, (k + 1) * 128)
            P.dma("pool", wq[:, k, :], self.w_in[l, rs_, O_Q:O_K])
            P.dma("pool", wv[:, k, :], self.w_in[l, rs_, O_V:N_IN])
            for g in range(4):
                for hl in range(2):
                    P.dma("pool", wkd[:, k, g, hl * 64:(hl + 1) * 64], self.w_in[l, rs_, O_K + g * 64:O_K + (g + 1) * 64])
        rot = sb("rot", [128, 128]); P.dma("sp", rot[:, :], self.c_rot[:, :])
        caus = sb("caus", [128, 2, 512], BF16); P.dma("pool", caus[:, :, :], self.c_causal[:, :, :])
        oneh = sb("oneh", [128, 32 * 128], BF16)
        P.memset("dve", oneh[:, :], 0.0)
        for c0 in range(0, 4096, 1024):
            P.dma("pool", oneh[0:32, c0:c0 + 1024], self.c_onehot[:, c0:c0 + 1024])
        ones32 = sb("ones32", [128, 64]); P.memset("dve", ones32[:, :], 1.0)
        KM = sb("KM", [128, 4, 32]); P.memset("dve", KM[:, :, :], 0.0)
        xt = sb("xt", [128, 4, D]); xT = sb("xT", [128, 8, 512], BF16)
        rC = sb("rC", [128, 512]); rS = sb("rS", [128, 512])
        xs = [sb("xs%d" % i, [128, 512]) for i in range(1)] * 2
        t1 = [sb("t1%d" % i, [128, 512]) for i in range(1)] * 2
        qr32 = sb("qr32", [128, 4, 512])
        Qblk = sb("Qblk", [128, 4, 2, 512], BF16); P.memset("dve", Qblk[:, :, :, :], 0.0)
        mskb = [sb("mskb%d" % i, [128, 512], BF16) for i in range(2)]
        kr32 = [sb("kr32%d" % i, [128, 512]) for i in range(2)]
        vtm = [sb("vtm%d" % i, [128, 256]) for i in range(2)]
        gsb = sb("gsb", [128, 4, 32]); P.memset("dve", gsb[:, :, :], -1e30)
        m8 = sb("m8", [128, 4, 8]); neg = sb("neg", [128, 4, 32]); negT = sb("negT", [128, 512], BF16); P.memset("dve", negT[:, :], 0.0)
        pt = [sb("pt%d" % i, [128, 512], BF16) for i in range(4)]
        srow = sb("srow", [128, 512]); bcs = sb("bcs", [64, 512]); ya = sb("ya", [64, 512], BF16)
        knew = sb("knew", [128, 4, NS])
        stp = ExitStack()
        KT = P.sb("m_KT", [128, 4, L], BF16, stp)
        Vaug = P.sb("m_Vaug", [128, L // 128, 4, 65], BF16, stp); P.memset("dve", Vaug[:, :, :, :], 1.0)

        def rope(ps, n, out32, tabC, tabS, i):
            x_, t_ = xs[i % 2], t1[i % 2]
            P.copy("act", x_[:, 0:n], ps[:, 0:n])
            pr = PS[4 + i % 2]
            P.mm(pr[:, 0:n], rot[:, :], x_[:, 0:n])
            P.tt("pool", t_[:, 0:n], x_[:, 0:n], tabC, ALU.mult)
            P.tt("dve", out32, pr[:, 0:n], tabS, ALU.mult)
            P.tt("pool", out32, out32, t_[:, 0:n], ALU.add)

        for (t0, n) in self.groups:
            samp = n < 128
            if samp:
                stp.close()
                P.barrier()
            self.load_x(st, xsrc, t0, n, xt)
            self.make_xT(xt, n, xT)
            if samp:
                P.dma("sp", rC[:, 0:n], self.ropeCs[:, 0:n]); P.dma("sp", rS[:, 0:n], self.ropeSs[:, 0:n])
            else:
                P.dma("sp", rC[:, 0:n], self.ropeC[:, t0:t0 + n]); P.dma("sp", rS[:, 0:n], self.ropeS[:, t0:t0 + n])
            it = 0
            for qc in range(4):
                ps = PS[2 + it % 2]
                for k in range(8):
                    P.mm(ps[:, 0:n], wq[:, k, qc * 128:(qc + 1) * 128], xT[:, k, 0:n], k == 0, k == 7)
                rope(ps, n, qr32[:, qc, 0:n], rC[:, 0:n], rS[:, 0:n], it); it += 1
                if not samp:
                    for qi in range(2):
                        P.copy("act", Qblk[0:64, qc, qi, 0:256], qr32[0:64, qc, qi * 256:(qi + 1) * 256])
                        P.copy("dve", Qblk[64:128, qc, qi, 256:512], qr32[64:128, qc, qi * 256:(qi + 1) * 256])
            for g in range(4):
                ps = PS[2 + it % 2]
                for k in range(8):
                    P.mm(ps[:, 0:n], wkd[:, k, g, :], xT[:, k, 0:n], k == 0, k == 7)
                kr = kr32[g % 2]
                rope(ps, n, kr[:, 0:n], rC[:, 0:n], rS[:, 0:n], it); it += 1
                if samp:
                    P.dma("sp", self.k_s[l, g, :, :], kr[0:64, 0:n])
                    P.copy("dve", knew[:, g, :], kr[:, 0:n])
                else:
                    P.dma("sp", self.k_p[l, g, :, t0:t0 + n], kr[0:64, 0:n])
                    P.copy("act", KT[:, g, t0:t0 + n], kr[:, 0:n])
                    P.reduce(KM[:, g, t0 // 256:t0 // 256 + 2], kr.v(kr.t[:, 0:n].rearrange("p (b t) -> p b t", b=2)), ALU.add)
                    P.ts("dve", KM[:, g, t0 // 256:t0 // 256 + 2], KM[:, g, t0 // 256:t0 // 256 + 2], 1.0 / 256, None, op0=ALU.mult)
            for i, m_ in self.subs(n):
                ps = PS[2 + i % 2]
                for k in range(8):
                    P.mm(ps[0:m_, 0:256], xT[:, k, i * 128:i * 128 + m_], wv[:, k, :], k == 0, k == 7)
                v_ = vtm[i % 2]
                P.copy("act", v_[0:m_, :], ps[0:m_, 0:256])
                if samp:
                    P.dma("sp", self.v_s[l, :, :], v_[0:m_, :])
                else:
                    P.dma("sp", self.v_p[l, t0 + i * 128:t0 + (i + 1) * 128, :], v_[:, :])
                    P.copy("dve", Vaug[:, (t0 // 128) + i, :, 0:64], v_.v(v_.t[:, :].rearrange("p (g d) -> p g d", g=4)))
            if samp:
                self.moba_sample(l, qr32, knew, st)
                continue
            for qo in (0, 256):
                qb = (t0 + qo) // 256
                for g in range(4):
                    if qb > 0:
                        for hl in range(2):
                            pgt = PS[4 + hl]
                            for half in range(2):
                                P.mm(pgt[:, half * 32:(half + 1) * 32], qr32[hl * 64:(hl + 1) * 64, g, qo + half * 128:qo + (half + 1) * 128],
                                     KM[hl * 64:(hl + 1) * 64, g, :])
                            P.copy("act", gsb[:, hl * 2:hl * 2 + 2, 0:qb], pgt.v(pgt.t[:, 0:64].rearrange("p (a b) -> p a b", a=2)[:, :, 0:qb]))
                        for idx in range(4):
                            P.emit("dve", lambda e, idx=idx: e.max(m8.t[:, idx, :], gsb.t[:, idx, :]), [gsb[:, :, :]], [m8[:, :, :]])
                            P.ts("dve", neg[:, idx, :], gsb[:, idx, :], m8[:, idx, 2:3], None, op0=ALU.is_ge)
                        pnt = PS[5]
                        for idx in range(4):
                            P.tr(pnt[0:32, idx * 128:(idx + 1) * 128], neg[:, idx, :], self.ident[:, :])
                        P.copy("act", negT[0:32, :], pnt[0:32, :])
                    po = PS[6 + g % 2]
                    tiles = [(nb, kt) for nb in range(qb) for kt in range(2)] + [(qb, 0), (qb, 1)]
                    qi = qo // 256
                    ntl = len(tiles)

                    def front(ti):
                        nb, kt = tiles[ti]
                        psS = PS[ti % 4]
                        ks = slice(nb * 256 + kt * 128, nb * 256 + (kt + 1) * 128)
                        p_ = pt[ti % 4]
                        if nb < qb and kt == 0:
                            pm = PS[4 + nb % 2]
                            P.mm(pm[:, :], oneh[:, nb * 128:(nb + 1) * 128], negT[:, :])
                            P.copy("dve", mskb[nb % 2][:, :], pm[:, :])
                        P.mm(psS[:, :], KT[:, g, ks], Qblk[:, g, qi, :])
                        P.act(p_[:, :], psS[:, :], AF.Exp, scale=0.125)
                        if nb < qb:
                            P.tt("dve", p_[:, :], p_[:, :], mskb[nb % 2][:, :], ALU.mult)
                        else:
                            P.tt("dve", p_[:, :], p_[:, :], caus[:, kt, :], ALU.mult)

                    def back(ti):
                        nb, kt = tiles[ti]
                        P.mm(po[0:65, :], Vaug[:, nb * 2 + kt, g, :], pt[ti % 4][:, :], ti == 0, ti == ntl - 1)

                    LOOK = 2
                    for ti in range(ntl + LOOK):
                        if ti < ntl:
                            front(ti)
                        if ti >= LOOK:
                            back(ti - LOOK)
                    P.copy("act", srow[64:65, :], po[64:65, :])
                    P.recip(srow[64:65, :], srow[64:65, :])
                    pbc = PS[4]
                    P.mm(pbc[0:64, :], ones32[64:65, 0:64], srow[64:65, :])
                    P.copy("act", bcs[:, :], pbc[0:64, :])
                    P.tt("dve", ya[:, :], po[0:64, :], bcs[:, :], ALU.mult)
                    for hl in range(2):
                        P.dma("sp", self.yT[l].part(("g", t0 + qo, 4 + g, hl), (4 + g, slice(hl * 64, (hl + 1) * 64), slice(t0 + qo, t0 + qo + 256))),
                              ya[:, hl * 256:(hl + 1) * 256])
    P.barrier()


Model.pass1b = _pass1b


def _moba_sample(self, l, qr32, knew, st):
    P = self.P
    PS = self.PS
    L, NPG = self.L, self.NPG
    NBP = NPG // 2
    npl = NPG
    sb = lambda name, shape, dt=F32: P.sb("ms_" + name, shape, dt, st)
    qs = sb("qs", [128, 4, NS]); P.copy("dve", qs[:, :, :], qr32[:, :, 0:NS])
    qscr, kscr = self.qscr[l], self.kscr[l]
    P.dma("sp", qscr.v(qscr.t[:, :, :].rearrange("c s q -> q (c s)")), qs.v(qs.t[:, :, :].rearrange("q c s -> q (c s)")), allow_slow_non_contiguous=True)
    P.dma("sp", kscr.v(kscr.t[:, :, :].rearrange("g s d -> d (g s)")), knew.v(knew.t[0:64, :, :].rearrange("d g s -> d (g s)")), allow_slow_non_contiguous=True)
    ptT = sb("ptT", [128, NS], I32); P.dma("sp", ptT[0:npl, :], self.ptT[:, :])
    pair = sb("pair", [128, 64]); P.dma("sp", pair[:, :], self.c_pair[:, :])
    pairT = sb("pairT", [64, 128]); P.dma("sp", pairT[:, :], self.c_pairT[:, :])
    ones = sb("ones", [128, 1]); P.memset("dve", ones[:, :], 1.0)
    qrow = sb("qrow", [128, 512]); krow = sb("krow", [128, 256]); vrow = sb("vrow", [1, 256])
    idx2 = [sb("idx%d" % i, [128, 1], I32) for i in range(2)]
    Kc = [sb("Kc%d" % i, [128, 8, 256]) for i in range(3)]
    prod = sb("prod", [128, 8, 2, 64])
    S_all = sb("S_all", [128, 128, 8]); P_all = sb("P_all", [128, 128, 8])
    R = sb("R", [128, 8]); Psum_ = sb("Psum", [128, 8]); sself = sb("sself", [128, 8]); pself = sb("pself", [128, 8])
    gsb = sb("gsb", [8, 64]); P.memset("dve", gsb[:, :], -1e30)
    m8 = sb("m8", [8, 8]); sel = sb("sel", [8, 64]); selT = sb("selT", [64, 8]); selP = sb("selP", [128, 8])
    rden = sb("rden", [8, 1]); osb = sb("osb", [8, 256], BF16)

    def gather(dst, flat, s_, tc, i):
        ix = idx2[i % 2]
        P.ts("dve", ix[0:npl, :], ptT[0:npl, s_:s_ + 1], 16.0, float(l * self.npool * 16 + tc), op0=ALU.mult, op1=ALU.add)
        d_ap = dst.t[0:npl, :, :].rearrange("p a b -> p (a b)")
        src = flat.t[:, :]
        i_ap = ix.t[0:npl, 0:1]
        P.dma_custom("pool", lambda e: e.indirect_dma_start(out=d_ap, out_offset=None, in_=src,
                                                             in_offset=bass.IndirectOffsetOnAxis(ap=i_ap, axis=0)),
                     [flat[:, :], ix[0:npl, :]], [dst[0:npl, :, :]])

    for s_ in range(NS):
        P.dma("sp", qrow[:, :], qscr.v(bass.AP(qscr.t, s_ * 128, [[0, 128], [NS * 128, 4], [1, 128]])))
        P.dma("sp", krow[:, :], kscr.v(bass.AP(kscr.t, s_ * 64, [[0, 128], [NS * 64, 4], [1, 64]])))
        P.dma("sp", vrow[0:1, :], self.v_s[l, s_:s_ + 1, :])
        for tc in range(16):
            kc = Kc[tc % 3]
            gather(kc, self.ck_flat, s_, tc, tc)
            for g in range(4):
                kv = kc.v(bass.AP(kc.t, g * 64, [[8 * 256, npl], [256, 8], [0, 2], [1, 64]]))
                qv = qrow.v(bass.AP(qrow.t, g * 128, [[512, npl], [0, 8], [64, 2], [1, 64]]))
                P.tt("dve", prod[0:npl, :, :, :], kv, qv, ALU.mult)
                P.reduce(S_all.v(bass.AP(S_all.t, tc * 64 + g * 2, [[1024, npl], [8, 8], [1, 2]])),
                         prod.v(prod.t[0:npl, :, :, :].rearrange("p a b d -> p (a b) d")), ALU.add)
        P.reduce(R[0:npl, :], S_all.v(S_all.t[0:npl, :, :].rearrange("p t h -> p h t")), ALU.add)
        pg = PS[2]
        P.mm(pg[0:8, 0:NBP], R[0:npl, :], pair[0:npl, 0:NBP])
        P.copy("act", gsb[:, 0:NBP], pg[0:8, 0:NBP])
        P.emit("dve", lambda e: e.max(m8.t[:, :], gsb.t[:, :]), [gsb[:, :]], [m8[:, :]])
        P.ts("dve", sel[:, :], gsb[:, :], m8[:, 2:3], None, op0=ALU.is_ge)
        pt_ = PS[3]
        P.tr(pt_[0:64, 0:8], sel[:, :], self.ident[0:8, 0:8])
        P.copy("act", selT[:, :], pt_[0:64, 0:8])
        pp = PS[4]
        P.mm(pp[0:npl, 0:8], pairT[0:NBP, 0:npl], selT[0:NBP, :])
        P.copy("act", selP[0:npl, :], pp[0:npl, 0:8])
        P.act(P_all.v(P_all.t[0:npl, :, :].rearrange("p t h -> p (t h)")), S_all.v(S_all.t[0:npl, :, :].rearrange("p t h -> p (t h)")), AF.Exp, scale=0.125)
        P.tt("dve", P_all[0:npl, :, :], P_all[0:npl, :, :], selP.v(bc_mid(selP.t[0:npl, :], 128)), ALU.mult)
        P.reduce(Psum_[0:npl, :], P_all.v(P_all.t[0:npl, :, :].rearrange("p t h -> p h t")), ALU.add)
        for g in range(4):
            kv = krow.v(bass.AP(krow.t, g * 64, [[256, 128], [0, 2], [1, 64]]))
            P.tt("dve", prod[:, 0, :, :], kv, qrow.v(bass.AP(qrow.t, g * 128, [[512, 128], [64, 2], [1, 64]])), ALU.mult)
            P.reduce(sself[:, g * 2:(g + 1) * 2], prod[:, 0, :, :], ALU.add)
        P.act(pself[:, :], sself[:, :], AF.Exp, scale=0.125)
        pd = PS[5]
        P.mm(pd[0:8, 0:1], Psum_[0:npl, :], ones[0:npl, 0:1], True, False)
        P.mm(pd[0:8, 0:1], pself[0:1, :], ones[0:1, 0:1], False, True)
        po = PS[6]
        for tc in range(16):
            vc = Kc[(tc + 1) % 3]
            gather(vc, self.cv_flat, s_, tc, tc)
            for tk in range(8):
                P.mm(po[0:8, 0:256], P_all[0:npl, tc * 8 + tk, :], vc[0:npl, tk, :], tc == 0 and tk == 0, False)
        P.mm(po[0:8, 0:256], pself[0:1, :], vrow[0:1, :], False, True)
        P.copy("act", rden[:, :], pd[0:8, 0:1])
        P.recip(rden[:, :], rden[:, :])
        P.ts("dve", osb[:, :], po[0:8, 0:256], rden[:, 0:1], None, op0=ALU.mult)
        for h in range(8):
            g, hl = divmod(h, 2)
            yt_ = self.yT[l]
            P.dma("sp", yt_.v(yt_.t[4 + g, hl * 64:(hl + 1) * 64, L + s_:L + s_ + 1].rearrange("d a -> a d"), key=("g", L, 4 + g, hl, s_)),
                  osb[h:h + 1, g * 64:(g + 1) * 64], allow_slow_non_contiguous=True)


Model.moba_sample = _moba_sample


_CACHE = {}


def kernel(**inputs):
    inp = {k: np.asarray(v) for k, v in inputs.items()}
    B, L = inp["x_prompt"].shape[0], inp["x_prompt"].shape[1]
    nsamp = inp["x_sample"].shape[0]
    ncores = 8
    assert nsamp == ncores * NS
    cfg = dict(L=L, NPG=inp["page_table"].shape[1], npool=inp["cache_k"].shape[1])
    key = (cfg["L"], cfg["NPG"], cfg["npool"])
    if key not in _CACHE:
        m = Model(cfg)
        m.build()
        _CACHE[key] = m
    m = _CACHE[key]
    prompt_b = [c % B for c in range(ncores)]
    maps = host_inputs(inp, cfg, m, ncores, prompt_b)
    res = run_bass_kernel_spmd(m.nc, maps, core_ids=list(range(ncores)))
    outs = host_outputs(res.results, cfg, ncores, prompt_b)
    return tuple(np.ascontiguousarray(o, dtype=np.float32) for o in outs)
```
